# Optimizing a Trainium2 kernel written in Bass

```python
import math
import numpy as np
import jax, jax.numpy as jnp
from jax import lax

D_MODEL = 1024
BATCH = 8
SEQ = 4096
DEPTH = 2

HEAD_DIM = 64
ATTN_Q_HEADS = 8
ATTN_KV_HEADS = 2
ATTN_GROUP = ATTN_Q_HEADS // ATTN_KV_HEADS
ATTN_WIDTH = ATTN_Q_HEADS * HEAD_DIM
KV_WIDTH = ATTN_KV_HEADS * HEAD_DIM
WINDOW = 128
BLOCK = 128
N_BUCKETS = 32
MAX_DISTANCE = 128
RW_HEADS = 8
RW_HEAD = 64
RW_WIDTH = RW_HEADS * RW_HEAD
DECAY_RANK = 64
A_RANK = 64
V_RANK = 32
G_RANK = 128
GN_EPS = 64e-5
D_FF = 2816
ALPHA = (2 * DEPTH) ** 0.25
BETA = (8 * DEPTH) ** -0.25
LN_EPS = 1e-5

OFF_GATE_A = 0
OFF_GATE_B = D_MODEL
OFF_Q = 2 * D_MODEL
OFF_K = OFF_Q + ATTN_WIDTH
OFF_V = OFF_K + KV_WIDTH
OFF_RW = OFF_V + KV_WIDTH
RW_COLS = 3 * RW_WIDTH + DECAY_RANK + A_RANK + G_RANK
PROJ_WIDTH = OFF_RW + RW_COLS

kernel_name = 'hybrid_swa_rwkv7_macaron_deepnorm'


def layer_norm(x, g, b, eps=LN_EPS):
    xf = x.astype(jnp.float32)
    mu = jnp.mean(xf, axis=-1, keepdims=True)
    var = jnp.mean(jnp.square(xf - mu), axis=-1, keepdims=True)
    return ((xf - mu) * lax.rsqrt(var + eps) * g + b).astype(x.dtype)


def swiglu(x, w_gu, w_down):
    gate, up = jnp.split(x @ w_gu, 2, axis=-1)
    return (jax.nn.silu(gate) * up) @ w_down


def t5_bucket(n):
    max_exact = N_BUCKETS // 2
    nf = jnp.maximum(n, 1).astype(jnp.float32)
    large = max_exact + (jnp.log(nf / max_exact) / math.log(MAX_DISTANCE / max_exact)
                         * (N_BUCKETS - max_exact)).astype(jnp.int32)
    large = jnp.minimum(large, N_BUCKETS - 1)
    return jnp.where(n < max_exact, n, large)


def sliding_window_attention(q, k, v, dist_bias, sinks):
    b_, s_, _ = q.shape
    nb = s_ // BLOCK
    qb = q.reshape(b_, nb, BLOCK, ATTN_KV_HEADS, ATTN_GROUP, HEAD_DIM)

    def band(t):
        tb = t.reshape(b_, nb, BLOCK, ATTN_KV_HEADS, HEAD_DIM)
        prev = jnp.pad(tb, ((0, 0), (1, 0), (0, 0), (0, 0), (0, 0)))[:, :-1]
        return jnp.concatenate([prev, tb], axis=2)

    kb, vb = band(k), band(v)
    s = jnp.einsum('bnqhgd,bnkhd->bnhgqk', qb, kb).astype(jnp.float32) * (HEAD_DIM ** -0.5)
    qi = np.arange(BLOCK)[:, None]
    kj = np.arange(2 * BLOCK)[None, :]
    dist = qi + BLOCK - kj
    local = (dist >= 0) & (dist < WINDOW)
    first = (np.arange(nb)[:, None, None] > 0) | (kj[None] >= BLOCK)
    mask = local[None] & first
    bias = dist_bias[:, np.clip(dist, 0, WINDOW - 1)].astype(jnp.float32)
    bias = bias.reshape(ATTN_KV_HEADS, ATTN_GROUP, BLOCK, 2 * BLOCK)
    s = jnp.where(mask[None, :, None, None], s + bias, -jnp.inf)
    sink = sinks.astype(jnp.float32).reshape(ATTN_KV_HEADS, ATTN_GROUP, 1, 1)
    m = jnp.maximum(jnp.max(s, axis=-1, keepdims=True), sink)
    p = jnp.exp(s - m)
    p = p / (jnp.sum(p, axis=-1, keepdims=True) + jnp.exp(sink - m))
    o = jnp.einsum('bnhgqk,bnkhd->bnqhgd', p.astype(v.dtype), vb)
    return o.reshape(b_, s_, ATTN_WIDTH)


def wkv7_scan(r, decay, k, v, a, b):
    def step(state, inp):
        r_t, w_t, k_t, v_t, a_t, b_t = inp
        sa = jnp.einsum('bhij,bhj->bhi', state, a_t)
        state = (state * w_t[:, :, None, :] + sa[..., None] * b_t[:, :, None, :]
                 + v_t[..., None] * k_t[:, :, None, :])
        return state, jnp.einsum('bhij,bhj->bhi', state, r_t)

    xs = tuple(jnp.moveaxis(t.astype(jnp.float32), 1, 0) for t in (r, decay, k, v, a, b))
    b_, _, h_, n_ = r.shape
    s0 = jnp.zeros((b_, h_, n_, n_), jnp.float32)
    _, y = lax.scan(step, s0, xs)
    return jnp.moveaxis(y, 0, 1)


def rwkv7_mix(u, w0, w2, a0, a2, g2, k_k, k_a, r_k, gn_g, gn_b, v_first, vmix):
    b_, s_, _ = u.shape
    c = RW_WIDTH
    r = u[..., :c]
    k = u[..., c:2 * c]
    v = u[..., 2 * c:3 * c]
    o = 3 * c
    xw = u[..., o:o + DECAY_RANK]
    xa = u[..., o + DECAY_RANK:o + DECAY_RANK + A_RANK]
    xg = u[..., o + DECAY_RANK + A_RANK:]
    w = -jax.nn.softplus(-(w0 + jnp.tanh(xw) @ w2)) - 0.5
    decay = jnp.exp(-jnp.exp(w.astype(jnp.float32)))
    a = jax.nn.sigmoid(a0 + xa @ a2)
    g = jax.nn.sigmoid(xg) @ g2
    if vmix is None:
        v_first = v
    else:
        v0, v1, v2 = vmix
        v = v + (v_first - v) * jax.nn.sigmoid(v0 + (v @ v1) @ v2)
    heads = lambda t: t.reshape(b_, s_, RW_HEADS, RW_HEAD)
    kk = heads(k * k_k).astype(jnp.float32)
    kk = kk / jnp.maximum(jnp.sqrt(jnp.sum(kk * kk, axis=-1, keepdims=True)), 1e-12)
    k = k * (1.0 + (a - 1.0) * k_a)
    rh, kh, vh, ah = heads(r), heads(k), heads(v), heads(a)
    y = wkv7_scan(rh, heads(decay), kh, vh, -kk, kk * ah)
    mu = jnp.mean(y, axis=-1, keepdims=True)
    var = jnp.mean(jnp.square(y - mu), axis=-1, keepdims=True)
    y = ((y - mu) * lax.rsqrt(var + GN_EPS)).reshape(b_, s_, c) * gn_g + gn_b
    bonus = jnp.sum(rh * kh * r_k, axis=-1, keepdims=True) * vh
    y = y + bonus.reshape(b_, s_, c)
    return y * g, v_first


def token_mix(x, w_in, b_in, mu, sinks, dist_bias, w0, w2, a0, a2, g2, k_k, k_a, r_k,
              gn_g, gn_b, w_ba, w_bb, w_o, v_first, vmix):
    proj = x @ w_in + b_in
    gate_a = jax.nn.sigmoid(proj[..., OFF_GATE_A:OFF_GATE_B])
    gate_b = jax.nn.sigmoid(proj[..., OFF_GATE_B:OFF_Q])
    attn = sliding_window_attention(proj[..., OFF_Q:OFF_K], proj[..., OFF_K:OFF_V],
                                    proj[..., OFF_V:OFF_RW], dist_bias, sinks)
    u = proj[..., OFF_RW:]
    u_prev = jnp.pad(u, ((0, 0), (1, 0), (0, 0)))[:, :-1]
    u = u + (u_prev - u) * mu
    rw, v_first = rwkv7_mix(u, w0, w2, a0, a2, g2, k_k, k_a, r_k, gn_g, gn_b, v_first, vmix)
    merged = gate_a * (attn @ w_ba) + gate_b * (rw @ w_bb)
    return merged @ w_o, v_first


def setup_inputs(seed: int = 0) -> dict:
    key = jax.random.key(seed)
    ks = jax.random.split(key, 26)
    nrm = lambda k, shape, scale: jax.random.normal(k, shape, jnp.float32) * scale
    L = DEPTH
    c = RW_WIDTH
    return {
        'x': nrm(ks[0], (BATCH, SEQ, D_MODEL), 1.0),
        'ffn_w_gu': nrm(ks[1], (L, 2, D_MODEL, 2 * D_FF), D_MODEL ** -0.5),
        'ffn_w_down': nrm(ks[2], (L, 2, D_FF, D_MODEL), BETA * D_FF ** -0.5),
        'ln_g': 1.0 + nrm(ks[3], (L, 3, D_MODEL), 0.02),
        'ln_b': nrm(ks[4], (L, 3, D_MODEL), 0.02),
        'w_in': nrm(ks[5], (L, D_MODEL, PROJ_WIDTH), D_MODEL ** -0.5),
        'b_in': nrm(ks[6], (L, PROJ_WIDTH), 0.02),
        'rel_bias': nrm(ks[7], (N_BUCKETS, ATTN_Q_HEADS), 0.5),
        'attn_sinks': nrm(ks[8], (L, ATTN_Q_HEADS), 0.5),
        'shift_mu': jax.random.uniform(ks[9], (L, RW_COLS), jnp.float32),
        'rw_w0': jax.random.uniform(ks[10], (L, c), jnp.float32, -6.0, 1.0),
        'rw_w2': nrm(ks[11], (L, DECAY_RANK, c), 0.5 * DECAY_RANK ** -0.5),
        'rw_a0': nrm(ks[12], (L, c), 0.1),
        'rw_a2': nrm(ks[13], (L, A_RANK, c), 0.5 * A_RANK ** -0.5),
        'rw_g2': nrm(ks[14], (L, G_RANK, c), G_RANK ** -0.5),
        'rw_k_k': 0.85 + nrm(ks[15], (L, c), 0.02),
        'rw_k_a': 1.0 + nrm(ks[16], (L, c), 0.02),
        'rw_r_k': nrm(ks[17], (L, RW_HEADS, RW_HEAD), 0.1),
        'rw_gn_g': 1.0 + nrm(ks[18], (L, c), 0.02),
        'rw_gn_b': nrm(ks[19], (L, c), 0.02),
        'rw_v0': nrm(ks[20], (L - 1, c), 0.1),
        'rw_v1': nrm(ks[21], (L - 1, c, V_RANK), c ** -0.5),
        'rw_v2': nrm(ks[22], (L - 1, V_RANK, c), 0.5 * V_RANK ** -0.5),
        'w_branch_attn': nrm(ks[23], (L, ATTN_WIDTH, D_MODEL), BETA * ATTN_WIDTH ** -0.5),
        'w_branch_rwkv': nrm(ks[24], (L, c, D_MODEL), BETA * c ** -0.5),
        'w_out': nrm(ks[25], (L, D_MODEL, D_MODEL), BETA * D_MODEL ** -0.5),
    }


def reference(x, ffn_w_gu, ffn_w_down, ln_g, ln_b, w_in, b_in, rel_bias, attn_sinks,
              shift_mu, rw_w0, rw_w2, rw_a0, rw_a2, rw_g2, rw_k_k, rw_k_a, rw_r_k,
              rw_gn_g, rw_gn_b, rw_v0, rw_v1, rw_v2, w_branch_attn, w_branch_rwkv, w_out):
    dist_bias = rel_bias[t5_bucket(jnp.arange(WINDOW, dtype=jnp.int32))].T
    v_first = None
    for l in range(DEPTH):
        x = layer_norm(ALPHA * x + 0.5 * swiglu(x, ffn_w_gu[l, 0], ffn_w_down[l, 0]),
                       ln_g[l, 0], ln_b[l, 0])
        vmix = None if l == 0 else (rw_v0[l - 1], rw_v1[l - 1], rw_v2[l - 1])
        mix, v_first = token_mix(x, w_in[l], b_in[l], shift_mu[l], attn_sinks[l], dist_bias,
                                 rw_w0[l], rw_w2[l], rw_a0[l], rw_a2[l], rw_g2[l],
                                 rw_k_k[l], rw_k_a[l], rw_r_k[l], rw_gn_g[l], rw_gn_b[l],
                                 w_branch_attn[l], w_branch_rwkv[l], w_out[l], v_first, vmix)
        x = layer_norm(ALPHA * x + mix, ln_g[l, 1], ln_b[l, 1])
        x = layer_norm(ALPHA * x + 0.5 * swiglu(x, ffn_w_gu[l, 1], ffn_w_down[l, 1]),
                       ln_g[l, 2], ln_b[l, 2])
    return x
```

```python
import math
import numpy as np
import concourse.bass as bass
import concourse.mybir as mybir
from concourse.bass_utils import run_bass_kernel_spmd

F32 = mybir.dt.float32
BF16 = mybir.dt.bfloat16
AF = mybir.ActivationFunctionType
ALU = mybir.AluOpType

PE, ACT, DVE, POOL, SP = "pe", "act", "dve", "pool", "sp"
ENGS = [PE, ACT, DVE, POOL, SP]
SEM_WRAP = 30000

D = 1024
SEQ = 4096
DEPTH = 2
DFF = 2816
NHC = DFF // 128
PROJ = 4608
OFF_Q = 2048
OFF_K = 2560
OFF_V = 2688
OFF_RW = 2816
ALPHA = (2 * DEPTH) ** 0.25
LN_EPS = 1e-5
GN_EPS = 64e-5
T = 512
CH = 64
NCH = T // CH
DECAY_C = math.exp(-0.5)
NSLOT = 4
SLOT = 4096


class Buf:
    def __init__(self, t, n=1, name=""):
        self.t = t
        self.n = n
        self.name = name
        self.lastw = [None] * n
        self.readers = [[] for _ in range(n)]
        self.exclusive = False
        self.extra = [[] for _ in range(n)]


class Op:
    __slots__ = ("eng", "fn", "deps", "signaled", "tok", "chan", "idx", "alldeps", "cost", "lat", "prio", "succ", "indeg", "est")

    def __init__(self, eng, fn, chan=None):
        self.eng = eng
        self.fn = fn
        self.deps = []
        self.signaled = False
        self.tok = None
        self.chan = chan


def _slots(buf, s):
    if s is None:
        return range(buf.n)
    if isinstance(s, int):
        return (s,)
    return s


class Prog:
    def __init__(self, nc, same_engine_sync=True):
        self.nc = nc
        self.ops = {e: [] for e in ENGS}
        self.same_engine_sync = same_engine_sync
        self.nchan = 0
        self.bufs = []
        self.pending = {e: [] for e in ENGS}
        self.last_dma = {}
        self.nops = 0

    def buf(self, t, n=1, name=""):
        b = Buf(t, n, name)
        self.bufs.append(b)
        return b

    def new_chan(self):
        c = self.nchan
        self.nchan += 1
        return c

    def guard(self, old_bufs, new_bufs):
        accs = {}
        for b in old_bufs:
            for i in range(b.n):
                for o in [b.lastw[i]] + b.readers[i]:
                    if o is not None:
                        accs[id(o)] = o
        join = self.add(SP, lambda e: e.nop(), cost=30.0, xdeps=list(accs.values()))
        join.signaled = True
        for b in new_bufs:
            for i in range(b.n):
                b.extra[i] = [join]
                b.lastw[i] = None
                b.readers[i] = []

    def schedule(self):
        import heapq
        allops = []
        for e in ENGS:
            allops.extend(self.ops[e])
        allops.sort(key=lambda o: o.idx)
        for o in allops:
            o.succ = []
            o.indeg = len(o.alldeps)
            o.est = 0.0
        for o in allops:
            for d in o.alldeps:
                d.succ.append(o)
        XL = 400.0
        for o in reversed(allops):
            p = 0.0
            for s_ in o.succ:
                if s_.prio > p:
                    p = s_.prio
            o.prio = p + o.lat + XL
        pend = {e: [] for e in ENGS}
        avail = {e: [] for e in ENGS}
        free = {e: 0.0 for e in ENGS}
        for o in allops:
            if o.indeg == 0:
                heapq.heappush(pend[o.eng], (0.0, o.idx, o))
        order = {e: [] for e in ENGS}
        n_done = 0
        total = len(allops)
        while n_done < total:
            best = None
            for e in ENGS:
                pe_, av = pend[e], avail[e]
                while pe_ and pe_[0][0] <= free[e]:
                    _, _, o = heapq.heappop(pe_)
                    heapq.heappush(av, (-o.prio, o.idx, o))
                if av:
                    t = free[e]
                elif pe_:
                    t = pe_[0][0]
                else:
                    continue
                if best is None or t < best[0]:
                    best = (t, e)
            t, e = best
            if avail[e]:
                _, _, o = heapq.heappop(avail[e])
            else:
                _, _, o = heapq.heappop(pend[e])
            start = max(free[e], o.est)
            free[e] = start + o.cost
            fin = start + o.lat
            order[e].append(o)
            n_done += 1
            for s_ in o.succ:
                v = fin + (XL if s_.eng != e else 60.0)
                if v > s_.est:
                    s_.est = v
                s_.indeg -= 1
                if s_.indeg == 0:
                    heapq.heappush(pend[s_.eng], (s_.est, s_.idx, s_))
        self.ops = order
        return max(free.values())

    def barrier(self):
        lasts = [self.ops[e][-1] for e in ENGS if self.ops[e]]
        lasts += list(self.last_dma.values())
        for o in lasts:
            o.signaled = True
        for e in ENGS:
            self.pending[e] = list(lasts)

    def add(self, eng, fn, reads=(), writes=(), chan=None, cost=100.0, lat=None, xdeps=()):
        op = Op(eng, fn, chan)
        op.idx = self.nops
        self.nops += 1
        op.cost = cost
        op.lat = cost if lat is None else lat
        deps = {}
        for o in xdeps:
            deps[id(o)] = o
        for buf, s in list(reads) + list(writes):
            for i in _slots(buf, s):
                if buf.extra[i]:
                    for o in buf.extra[i]:
                        deps[id(o)] = o
                    buf.extra[i] = []
        for buf, s in reads:
            for i in _slots(buf, s):
                o = buf.lastw[i]
                if o is not None:
                    deps[id(o)] = o
                if buf.exclusive:
                    for r in buf.readers[i]:
                        if r.eng != eng:
                            deps[id(r)] = r
        for buf, s in writes:
            for i in _slots(buf, s):
                o = buf.lastw[i]
                if o is not None:
                    deps[id(o)] = o
                for r in buf.readers[i]:
                    deps[id(r)] = r
        for o in self.pending[eng]:
            deps[id(o)] = o
        self.pending[eng] = []
        op.alldeps = list(deps.values())
        for o in deps.values():
            if o.eng == eng and o.chan is None and chan is None:
                if eng == PE or not self.same_engine_sync:
                    continue
            op.deps.append(o)
            o.signaled = True
        for buf, s in reads:
            for i in _slots(buf, s):
                buf.readers[i].append(op)
        for buf, s in writes:
            for i in _slots(buf, s):
                buf.lastw[i] = op
                buf.readers[i] = []
        if chan is not None:
            op.signaled = True
            self.last_dma[chan] = op
        self.ops[eng].append(op)
        return op

    def emit(self, final_waits=()):
        nc = self.nc
        nsem = {}
        for e in ENGS:
            cnt = 0
            for op in self.ops[e]:
                if op.chan is None and op.signaled:
                    k = cnt // SEM_WRAP
                    op.tok = ((e, k), cnt - k * SEM_WRAP + 1)
                    cnt += 1
            nsem[e] = cnt // SEM_WRAP + 1
        chan_cnt = {}
        for e in ENGS:
            for op in self.ops[e]:
                if op.chan is not None:
                    chan_cnt[op.chan] = chan_cnt.get(op.chan, 0) + 1
                    op.tok = (("chan", op.chan), 16 * chan_cnt[op.chan])
        semh = {}
        for e in ENGS:
            for k in range(nsem[e]):
                semh[(e, k)] = nc.alloc_semaphore(name=f"s_{e}_{k}")
        for c in range(self.nchan):
            semh[("chan", c)] = nc.alloc_semaphore(name=f"s_ch{c}")
        stats = {}

        def run(e, eng):
            known = {}
            nw = 0
            for op in self.ops[e]:
                need = {}
                for d in op.deps:
                    sk, v = d.tok
                    if known.get(sk, 0) >= v:
                        continue
                    if need.get(sk, 0) < v:
                        need[sk] = v
                for sk, v in need.items():
                    eng.wait_ge(semh[sk], v)
                    known[sk] = v
                    nw += 1
                ins = op.fn(eng)
                if op.chan is not None:
                    ins.then_inc(semh[op.tok[0]], 16)
                elif op.signaled:
                    ins.then_inc(semh[op.tok[0]], 1)
            if e == SP:
                for op in final_waits:
                    sk, v = op.tok
                    if known.get(sk, 0) < v:
                        eng.wait_ge(semh[sk], v)
                        known[sk] = v
            stats[e] = (len(self.ops[e]), nw)

        with nc.Block() as block:
            @block.tensor
            def _(eng):
                run(PE, eng)

            @block.scalar
            def _(eng):
                run(ACT, eng)

            @block.vector
            def _(eng):
                run(DVE, eng)

            @block.gpsimd
            def _(eng):
                run(POOL, eng)

            @block.sync
            def _(eng):
                run(SP, eng)
        return stats


def weight_blocks():
    bl = []
    for f in range(2):
        if f == 1:
            bl.append(("Q", 4096))
            bl.append(("KP", 4096))
            bl.append(("VP", 4096))
            bl.append(("XS", 2048))
            bl.append(("V", 4096))
            for fc in range(4):
                bl.append((f"RK{fc}", 2048))
            for o in range(8):
                bl.append((f"MG{o}", 3072))
            for j in range(2):
                bl.append((f"WO{j}", 4096))
        for j in range(NHC // 2):
            bl.append((f"GU{f}_{j}", 4096))
        for o in range(8):
            bl.append((f"WD{f}_{o}", 2816))
    return bl


LAYER_BLOCKS = weight_blocks()
LAYER_W = sum(n for _, n in LAYER_BLOCKS)
TOTW = LAYER_W * DEPTH
NSMALL = 512 + 512 + 128 + 512


def pack_weights(inp):
    wf = np.empty((128, TOTW), np.float32)
    off = 0
    for l in range(DEPTH):
        w_in = inp["w_in"][l]

        def cols(c0, n=128):
            return w_in[:, c0:c0 + n].reshape(8, 128, n).transpose(1, 0, 2)

        for name, n in LAYER_BLOCKS:
            if name.startswith("GU"):
                f, j = int(name[2]), int(name[4:])
                wgu = inp["ffn_w_gu"][l, f]
                blk = np.empty((128, 2, 8, 256), np.float32)
                for cc in range(2):
                    c = 2 * j + cc
                    blk[:, cc, :, 0:128] = wgu[:, c * 128:(c + 1) * 128].reshape(8, 128, 128).transpose(1, 0, 2)
                    blk[:, cc, :, 128:256] = wgu[:, DFF + c * 128:DFF + (c + 1) * 128].reshape(8, 128, 128).transpose(1, 0, 2)
            elif name.startswith("WD"):
                f, o = int(name[2]), int(name[4:])
                wd = inp["ffn_w_down"][l, f]
                blk = wd[:, o * 128:(o + 1) * 128].reshape(NHC, 128, 128).transpose(1, 0, 2)
            elif name == "Q":
                blk = np.stack([cols(OFF_Q + ch * 128) for ch in range(4)], axis=1)
            elif name == "KP":
                parts = []
                z64 = np.zeros((128, 8, 64), np.float32)
                for hk in range(2):
                    kc = cols(OFF_K + hk * 64, 64)
                    parts.append(np.concatenate([kc, z64], axis=2))
                    parts.append(np.concatenate([z64, kc], axis=2))
                blk = np.stack(parts, axis=1)
            elif name == "VP":
                z64 = np.zeros((128, 8, 64), np.float32)
                parts = []
                for hk in range(2):
                    vc = cols(OFF_V + hk * 64, 64)
                    parts.append(np.concatenate([vc, z64], axis=2))
                    parts.append(np.concatenate([z64, vc], axis=2))
                blk = np.concatenate(parts, axis=2)
            elif name == "XS":
                blk = np.stack([cols(OFF_RW + 1536), cols(OFF_RW + 1664)], axis=1)
            elif name == "V":
                blk = np.stack([cols(OFF_RW + 1024 + fc * 128) for fc in range(4)], axis=1)
            elif name.startswith("RK"):
                fc = int(name[2:])
                blk = np.stack([cols(OFF_RW + fc * 128), cols(OFF_RW + 512 + fc * 128)], axis=1)
            elif name.startswith("MG"):
                o = int(name[2:])
                wba = inp["w_branch_attn"][l][:, o * 128:(o + 1) * 128].reshape(4, 128, 128).transpose(1, 0, 2)
                wbb = inp["w_branch_rwkv"][l][:, o * 128:(o + 1) * 128].reshape(4, 128, 128).transpose(1, 0, 2)
                blk = np.concatenate([wba.reshape(128, -1), wbb.reshape(128, -1),
                                      cols(o * 128).reshape(128, -1), cols(D + o * 128).reshape(128, -1)], axis=1)
            elif name.startswith("WO"):
                j = int(name[2:])
                wo = inp["w_out"][l]
                blk = np.stack([wo[:, o * 128:(o + 1) * 128].reshape(8, 128, 128).transpose(1, 0, 2)
                                for o in range(4 * j, 4 * j + 4)], axis=1)
            wf[:, off:off + n] = blk.reshape(128, n)
            off += n
    assert off == TOTW
    return wf


def pack_small(inp):
    ws = np.zeros((128, DEPTH * NSMALL), np.float32)
    for l in range(DEPTH):
        o = l * NSMALL
        ws[0:64, o:o + 512] = inp["rw_w2"][l]
        ws[64:128, o:o + 512] = inp["rw_a2"][l]
        ws[:, o + 512:o + 1024] = inp["rw_g2"][l]
        if l > 0:
            ws[:, o + 1024:o + 1152] = inp["rw_v1"][l - 1].reshape(4, 128, 32).transpose(1, 0, 2).reshape(128, 128)
            ws[0:32, o + 1152:o + 1664] = inp["rw_v2"][l - 1]
    return ws


def vec_table():
    tab = {}
    off = 0

    def add(n, k):
        nonlocal off
        tab[n] = (off, k)
        off += k

    for i in range(3):
        add(f"ln_g{i}", 8)
        add(f"ln_b{i}", 8)
    add("bq", 4)
    add("bk", 4)
    add("b_xs", 2)
    add("b_v", 4)
    add("b_r", 4)
    add("b_k", 4)
    for g_, k_ in (("xs", 2), ("v", 4), ("r", 4), ("k", 4)):
        add("om_" + g_, k_)
        add("bom_" + g_, k_)
        add("bmu_" + g_, k_)
    add("bga", 8)
    add("bgb", 8)
    add("mu_xs", 2)
    add("mu_v", 4)
    add("mu_r", 4)
    add("mu_k", 4)
    for n in ("w0", "a0", "k_k", "k_a", "omka", "r_k", "gn_g", "gn_b", "v0", "sink"):
        add(n, 4)
    add("bvrow", 512)
    return tab, off


VTAB, NVEC = vec_table()


def fm(v):
    return np.ascontiguousarray(v.reshape(-1, 128).T)


def pack_vecs(inp):
    cv = np.zeros((128, DEPTH * NVEC), np.float32)
    for l in range(DEPTH):
        def put(n, a):
            o, k = VTAB[n]
            cv[:, l * NVEC + o:l * NVEC + o + k] = a

        b_in = inp["b_in"][l]
        mu = inp["shift_mu"][l]
        for i in range(3):
            put(f"ln_g{i}", fm(inp["ln_g"][l, i]))
            put(f"ln_b{i}", fm(inp["ln_b"][l, i]))
        put("bq", fm(b_in[OFF_Q:OFF_K]))
        bk = b_in[OFF_K:OFF_V]
        z64 = np.zeros(64, np.float32)
        put("bk", np.stack([np.concatenate([bk[0:64], z64]), np.concatenate([z64, bk[0:64]]),
                            np.concatenate([bk[64:128], z64]), np.concatenate([z64, bk[64:128]])], axis=1))
        brw = b_in[OFF_RW:]
        put("b_xs", fm(brw[1536:1792]))
        put("b_v", fm(brw[1024:1536]))
        put("b_r", fm(brw[0:512]))
        put("b_k", fm(brw[512:1024]))
        put("bga", fm(b_in[0:D]))
        put("bgb", fm(b_in[D:2 * D]))
        put("mu_xs", fm(mu[1536:1792]))
        put("mu_v", fm(mu[1024:1536]))
        put("mu_r", fm(mu[0:512]))
        put("mu_k", fm(mu[512:1024]))
        put("w0", fm(inp["rw_w0"][l]))
        put("a0", fm(inp["rw_a0"][l]))
        put("k_k", fm(inp["rw_k_k"][l]))
        put("k_a", fm(inp["rw_k_a"][l]))
        put("r_k", fm(inp["rw_r_k"][l].reshape(-1)))
        put("gn_g", fm(inp["rw_gn_g"][l]))
        put("gn_b", fm(inp["rw_gn_b"][l]))
        if l > 0:
            put("v0", fm(inp["rw_v0"][l - 1]))
        sk = inp["attn_sinks"][l]
        put("sink", np.stack([np.repeat(sk[2 * ch:2 * ch + 2], 64) for ch in range(4)], axis=1))
        bv = b_in[OFF_V:OFF_RW]
        bvp = np.concatenate([bv[0:64], z64, z64, bv[0:64], bv[64:128], z64, z64, bv[64:128]])
        put("bvrow", np.broadcast_to(bvp[None, :], (128, 512)))
    return cv


def t5_bucket_np(n):
    max_exact = 16
    nf = np.maximum(n, 1).astype(np.float32)
    large = max_exact + (np.log(nf / max_exact) / math.log(128 / max_exact) * (32 - max_exact)).astype(np.int32)
    large = np.minimum(large, 31)
    return np.where(n < max_exact, n, large)


GC = {}
_o = 0
for _n, _k in (("biasg", 2048), ("mask8", 2048), ("ident", 128), ("bdiag", 128), ("reset", 512),
               ("mAB", 1024), ("mT", 512), ("id64", 512)):
    GC[_n] = (_o, _k)
    _o += _k
NGC = _o


def pack_gconst(inp):
    g = np.zeros((128, NGC), np.float32)

    def put(n, a):
        o, k = GC[n]
        g[:a.shape[0], o:o + k] = a.reshape(a.shape[0], k)

    rel = inp["rel_bias"]
    bucket = t5_bucket_np(np.arange(128, dtype=np.int32))
    db = rel[bucket]
    kk = np.arange(128)[:, None]
    qq = np.arange(128)[None, :]
    dprev = np.clip(128 + qq - kk, 0, 127)
    dcur = np.clip(qq - kk, 0, 127)
    bg = np.empty((128, 2, 8, 128), np.float32)
    bg[:, 0] = db[dprev].transpose(0, 2, 1)
    bg[:, 1] = db[dcur].transpose(0, 2, 1)
    m8 = np.empty((128, 2, 8, 128), np.float32)
    m8[:, 0] = np.where(kk > qq, 0.0, -240000.0)[:, None, :]
    m8[:, 1] = np.where(kk <= qq, 0.0, -240000.0)[:, None, :]
    put("biasg", bg)
    put("mask8", m8)
    put("ident", np.eye(128, dtype=np.float32))
    bd = np.zeros((128, 128), np.float32)
    bd[0:64, 0:64] = 1.0
    bd[64:128, 64:128] = 1.0
    put("bdiag", bd)
    rs = np.ones((128, T), np.float32)
    rs[:, ::CH] = 0.0
    put("reset", rs)
    s = np.arange(64)[:, None]
    t = np.arange(64)[None, :]
    mAB = np.empty((64, 8, 2, 64), np.float32)
    mAB[:, :, 0, :] = (s < t).astype(np.float32)[:, None, :]
    mAB[:, :, 1, :] = (s <= t).astype(np.float32)[:, None, :]
    put("mAB", mAB)
    mT = np.broadcast_to((s > t).astype(np.float32)[:, None, :], (64, 8, 64))
    put("mT", np.ascontiguousarray(mT))
    put("id64", np.ascontiguousarray(np.broadcast_to(np.eye(64, dtype=np.float32)[:, None, :], (64, 8, 64))))
    return g


class Builder:
    def __init__(self, NT, stages=("ffn", "attn", "rwkv"), dbg=None):
        self.NT = NT
        self.stages = stages
        self.dbg = dbg
        nc = self.nc = bass.Bass("TRN2", target_bir_lowering=False)
        self.P = Prog(nc)
        S = NT * T
        self.xT = nc.dram_tensor("xT", [D, S], F32, kind="ExternalInput").ap()
        self.wf = nc.dram_tensor("wf", [128, TOTW], F32, kind="ExternalInput").ap()
        self.wsm = nc.dram_tensor("wsm", [128, DEPTH * NSMALL], F32, kind="ExternalInput").ap()
        self.cv = nc.dram_tensor("cv", [128, DEPTH * NVEC], F32, kind="ExternalInput").ap()
        self.gc = nc.dram_tensor("gc", [128, NGC], F32, kind="ExternalInput").ap()
        self.outT = nc.dram_tensor("outT", [D, S], F32, kind="ExternalOutput").ap()
        self.wscr = nc.dram_tensor("wscr", [128, TOTW], BF16, kind="Internal").ap()
        self.sb_off = 16512
        self.sb_top = 229344
        self.n_alloc = 0
        self.all_sb = []
        self.psfree = list(range(8))

    def sb(self, shape, dtype, nslots=1, name=None, at=None):
        nb = int(np.prod(shape[1:])) * (4 if dtype == F32 else 2)
        nb = (nb + 31) // 32 * 32
        if at is None:
            off = self.sb_off
            self.sb_off += nb
            assert self.sb_off <= self.sb_top, f"SBUF overflow {self.sb_off}"
        else:
            off = at
        self.n_alloc += 1
        t = self.nc.alloc_sbuf_tensor_at(name or f"t{self.n_alloc}", list(shape), dtype, offset=off)
        b = self.P.buf(t, nslots, name or "")
        b.off = off
        b.size = nb
        self.all_sb.append(b)
        return b

    def guard(self, new_bufs):
        old = []
        for b in self.all_sb:
            if b.off < self.arena0:
                continue
            for nb in new_bufs:
                if b.off < nb.off + nb.size and nb.off < b.off + b.size:
                    old.append(b)
                    break
        self.P.guard(old, new_bufs)

    def ps_get(self, n=1):
        if n == 1:
            return self.psfree.pop(0)
        for i, b in enumerate(self.psfree):
            if b % 2 == 0 and (b + 1) in self.psfree:
                self.psfree.remove(b)
                self.psfree.remove(b + 1)
                return b
        raise RuntimeError("no psum pair")

    def ps_put(self, b, n=1):
        for i in range(n):
            self.psfree.append(b + i)

    @staticmethod
    def _fs(ap):
        n = 1
        for d in ap.shape[1:]:
            n *= int(d)
        return n

    def _ec(self, eng, out, mult=1.0):
        n = self._fs(out) * mult
        if eng == POOL:
            return 150.0 + 2.4 * n
        return 80.0 + n / 0.96

    def mm(self, bank, out, lhsT, rhs, start, stop, reads):
        banks = bank if isinstance(bank, (list, tuple)) else [bank]
        c = max(self._fs(rhs), 64) / 2.4 * (4.0 if rhs.dtype == F32 else 1.0) + 8.0
        return self.P.add(PE, lambda e: e.matmul(out, lhsT=lhsT, rhs=rhs, start=start, stop=stop),
                          reads=reads, writes=[(self.psb, list(banks))], cost=c)

    def mmx(self, bank, out, lhsT, rhs, reads):
        c = max(self._fs(rhs), 64) / 2.4 + 8.0
        return self.P.add(PE, lambda e: e.matmul(out, lhsT=lhsT, rhs=rhs, start=False, stop=True, skip_group_check=True),
                          reads=reads, writes=[(self.psb, [bank])], cost=c)

    def tr(self, bank, out, in_, reads):
        ident = self.identb.t[0:in_.shape[0], 0:in_.shape[0]]
        return self.P.add(PE, lambda e: e.transpose(out, in_, ident),
                          reads=list(reads) + [(self.identb, None)], writes=[(self.psb, [bank])], cost=60.0)

    def act(self, out, in_, func, reads, writes, bias=None, scale=None):
        kw = {}
        if bias is not None:
            kw["bias"] = bias
        if scale is not None:
            kw["scale"] = scale
        return self.P.add(ACT, lambda e: e.activation(out=out, in_=in_, func=func, **kw), reads=reads, writes=writes,
                          cost=self._ec(ACT, out) + 60.0)

    def tt(self, eng, out, in0, in1, op, reads, writes):
        return self.P.add(eng, lambda e: e.tensor_tensor(out=out, in0=in0, in1=in1, op=op), reads=reads, writes=writes,
                          cost=self._ec(eng, out, 1.25))

    def ts(self, eng, out, in0, s1, op0, reads, writes, s2=None, op1=None):
        if op1 is None:
            if eng == POOL:
                return self.P.add(eng, lambda e: e.tensor_scalar(out=out, in0=in0, scalar1=s1, scalar2=0.0, op0=op0, op1=ALU.add),
                                  reads=reads, writes=writes, cost=self._ec(eng, out))
            return self.P.add(eng, lambda e: e.tensor_scalar(out=out, in0=in0, scalar1=s1, scalar2=None, op0=op0),
                              reads=reads, writes=writes, cost=self._ec(eng, out))
        return self.P.add(eng, lambda e: e.tensor_scalar(out=out, in0=in0, scalar1=s1, scalar2=s2, op0=op0, op1=op1),
                          reads=reads, writes=writes, cost=self._ec(eng, out))

    def stt(self, out, in0, scalar, in1, op0, op1, reads, writes):
        return self.P.add(DVE, lambda e: e.scalar_tensor_tensor(out=out, in0=in0, scalar=scalar, in1=in1, op0=op0, op1=op1),
                          reads=reads, writes=writes, cost=self._ec(DVE, out, 1.25))

    def cp(self, eng, out, in_, reads, writes):
        if eng == ACT:
            return self.P.add(ACT, lambda e: e.copy(out=out, in_=in_), reads=reads, writes=writes, cost=self._ec(ACT, out))
        m = 1.5 if (eng == POOL and out.dtype != in_.dtype) else 1.0
        return self.P.add(eng, lambda e: e.tensor_copy(out=out, in_=in_), reads=reads, writes=writes, cost=self._ec(eng, out, m))

    def memset(self, eng, ap, val, writes):
        return self.P.add(eng, lambda e: e.memset(ap, val), writes=writes, cost=self._ec(eng, ap))

    def dma(self, eng, out, in_, chan, reads=(), writes=(), xdeps=()):
        nb = int(np.prod([int(d) for d in out.shape])) * (4 if out.dtype == F32 else 2)
        return self.P.add(eng, lambda e: e.dma_start(out=out, in_=in_), reads=reads, writes=writes, chan=chan,
                          cost=(700.0 if eng == POOL else 60.0), lat=2500.0 + nb / 150.0, xdeps=xdeps)

    def w_init(self):
        self.ring = self.sb([128, NSLOT, SLOT], BF16, NSLOT, "ring")
        self.ring_ch = [self.P.new_chan() for _ in range(NSLOT)]
        self.wseq = []
        off = 0
        for l in range(DEPTH):
            for name, n in LAYER_BLOCKS:
                self.wseq.append((l, name, off, n))
                off += n
        self.nblk = len(self.wseq)
        self.castd = self.P.buf(None, self.nblk, "castd")
        self.NCC = 16
        self.cast_ch = [self.P.new_chan() for _ in range(self.NCC)]
        self.cast_ops = {}
        self.cast_issued = 0
        self.load_ops = {}
        self.CASTW = 5
        self.w_issued = 0
        self.w_cur = 0

    def w_cast(self, upto):
        while self.cast_issued < min(self.nblk, upto):
            i = self.cast_issued
            l, name, o, n = self.wseq[i]
            xd = [self.load_ops[i - self.CASTW]] if (i - self.CASTW) in self.load_ops else []
            if (i - self.NCC) in self.cast_ops:
                xd.append(self.cast_ops[i - self.NCC])
            self.cast_ops[i] = self.dma(POOL, self.wscr[:, o:o + n], self.wf[:, o:o + n], self.cast_ch[i % self.NCC],
                                        writes=[(self.castd, i)], xdeps=xd)
            self.cast_issued += 1

    def w_get(self, expect):
        total = self.nblk * self.NT
        while self.w_issued < min(total, self.w_cur + NSLOT):
            i = self.w_issued
            l, name, o, n = self.wseq[i % self.nblk]
            s = i % NSLOT
            reads = []
            if i < self.nblk:
                self.w_cast(i + self.CASTW)
                reads = [(self.castd, i)]
            op = self.dma(SP, self.ring.t[:, s, 0:n], self.wscr[:, o:o + n], self.ring_ch[s], reads=reads,
                          writes=[(self.ring, s)])
            if i < self.nblk:
                self.load_ops[i] = op
            self.w_issued += 1
        i = self.w_cur
        l, name, o, n = self.wseq[i % self.nblk]
        assert name == expect, (name, expect)
        self.w_cur += 1
        s = i % NSLOT
        return s, self.ring.t[:, s, 0:n]

    def vec(self, l, name, c=None):
        o, k = VTAB[name]
        if c is None:
            return self.cvs.t[:, l * NVEC + o:l * NVEC + o + k]
        return self.cvs.t[:, l * NVEC + o + c:l * NVEC + o + c + 1]

    def setup(self):
        P = self.P
        self.psum = self.nc.alloc_psum_tensor("ps", [128, 8, 512], F32)
        self.psb = P.buf(self.psum, 8, "psum")
        self.psb.exclusive = True
        self.x = self.sb([128, 8, T], F32, 8, "x")
        self.xb = self.sb([128, 8, T], BF16, 8, "xb")
        self.cvs = self.sb([128, DEPTH * NVEC], F32, 1, "cvs")
        self.wsmall = self.sb([128, DEPTH * NSMALL], BF16, 1, "wsmall")
        self.identb = self.sb([128, 128], BF16, 1, "identb")
        self.bdiagb = self.sb([128, 128], BF16, 1, "bdiagb")
        self.ones32 = self.sb([128, 128], F32, 1, "ones32")
        self.ones32r = self.sb([128, 128], F32, 1, "ones32r")
        self.onesb = self.sb([128, 2, 128], BF16, 1, "onesb")
        self.bias8 = self.sb([128, 2, 8, 128], BF16, 1, "bias8")
        self.resetm = self.sb([128, T], F32, 1, "resetm")
        self.mAB = self.sb([64, 4, 2, 64], BF16, 1, "mAB")
        self.id64 = self.sb([64, 8, 64], BF16, 1, "id64")
        self.sq = self.sb([128, 2, T], F32, 2, "sq")
        self.mean = self.sb([128, T], F32, 1, "mean")
        self.msq = self.sb([128, T], F32, 1, "msq")
        self.rstd = self.sb([128, T], F32, 1, "rstd")
        self.lnt = self.sb([128, 2, T], F32, 2, "lnt")
        self.kTd = [self.sb([128, 4, 128 + T], BF16, 4, f"kTd{l}") for l in range(DEPTH)]
        self.vtok = [self.sb([128, 5, 4, 128], BF16, 5, f"vtok{l}") for l in range(DEPTH)]
        self.attnT = self.sb([128, 4, T], BF16, 4, "attnT")
        self.rwT = self.sb([128, 4, T], BF16, 4, "rwT")
        self.vfirst = self.sb([128, 4, T], BF16, 4, "vfirst")
        self.ST32 = [self.sb([128, 4, 128], F32, 1, f"ST32_{l}") for l in range(DEPTH)]
        self.STb = [self.sb([128, 4, 128], BF16, 1, f"STb_{l}") for l in range(DEPTH)]
        self.ucar = [self.sb([128, 16], F32, 1, f"ucar{l}") for l in range(DEPTH)]
        self.w_init()
        self.arena0 = self.sb_off
        stg = self.sb([128, NGC], F32, 1, "stg", at=self.arena0)
        ch = P.new_chan()
        ch2 = P.new_chan()
        self.dma(SP, stg.t[:, :], self.gc, ch, writes=[(stg, None)])
        self.dma(SP, self.cvs.t[:, :], self.cv, ch2, writes=[(self.cvs, None)])
        self.dma(POOL, self.wsmall.t[:, :], self.wsm, P.new_chan(), writes=[(self.wsmall, None)])

        def g(n, rows=128):
            o, k = GC[n]
            return stg.t[0:rows, o:o + k]

        self.cp(DVE, self.identb.t[:, :], g("ident"), [(stg, None)], [(self.identb, None)])
        self.cp(DVE, self.bdiagb.t[:, :], g("bdiag"), [(stg, None)], [(self.bdiagb, None)])
        self.cp(DVE, self.resetm.t[:, :], g("reset"), [(stg, None)], [(self.resetm, None)])
        self.cp(DVE, self.mAB.t[:, :, :, :].rearrange("p a b c -> p (a b c)"), g("mAB", 64)[:, 0:512], [(stg, None)], [(self.mAB, None)])
        self.cp(DVE, self.id64.t[:, :, :].rearrange("p a b -> p (a b)"), g("id64", 64), [(stg, None)], [(self.id64, None)])
        self.stt(self.bias8.t[:, :, :, :].rearrange("p a b c -> p (a b c)"), g("biasg"), 8.0, g("mask8"), ALU.mult, ALU.add,
                 [(stg, None)], [(self.bias8, None)])
        self.memset(DVE, self.ones32.t[:, :], 1.0, [(self.ones32, None)])
        self.act(self.ones32r.t[:, :].bitcast(mybir.dt.float32r), self.ones32.t[:, :], AF.Identity, [(self.ones32, None)], [(self.ones32r, None)])
        self.memset(DVE, self.onesb.t[:, :, :], 0.0, [(self.onesb, None)])
        self.memset(DVE, self.onesb.t[:, 0, 0:64], 1.0, [(self.onesb, None)])
        self.memset(DVE, self.onesb.t[:, 1, 64:128], 1.0, [(self.onesb, None)])
        for l in range(DEPTH):
            self.ts(DVE, self.vec(l, "omka"), self.vec(l, "k_a"), -1.0, ALU.mult, [(self.cvs, None)], [(self.cvs, None)],
                    s2=1.0, op1=ALU.add)
            self.act(self.vec(l, "sink"), self.vec(l, "sink"), AF.Exp, [(self.cvs, None)], [(self.cvs, None)])
            for g_ in ("xs", "v", "r", "k"):
                C = [(self.cvs, None)]
                self.ts(DVE, self.vec(l, "om_" + g_), self.vec(l, "mu_" + g_), -1.0, ALU.mult, C, C, s2=1.0, op1=ALU.add)
                self.tt(DVE, self.vec(l, "bom_" + g_), self.vec(l, "b_" + g_), self.vec(l, "om_" + g_), ALU.mult, C, C)
                self.tt(DVE, self.vec(l, "bmu_" + g_), self.vec(l, "b_" + g_), self.vec(l, "mu_" + g_), ALU.mult, C, C)
        self.stg = stg

    def ln_stats_begin(self):
        self.bS1 = self.ps_get()
        self.bS2 = self.ps_get()
        self.bS1_ = self.bS1
        self.bS2_ = self.bS2

    def ln_stats_chunk(self, o):
        ps = self.psum
        sl = o % 2
        R = mybir.dt.float32r
        self.act(self.sq.t[:, sl, :].bitcast(R), self.x.t[:, o, :], AF.Square, [(self.x, o)], [(self.sq, sl)])
        self.mm(self.bS1, ps[:, self.bS1, :], self.ones32.t[:, :], self.x.t[:, o, :], o == 0, o == 7,
                [(self.ones32, None), (self.x, o)])
        self.P.add(PE, lambda e, r=self.sq.t[:, sl, :].bitcast(R), o_=ps[:, self.bS2, :], w_=self.ones32r.t[:, :].bitcast(R):
                   e.matmul(o_, lhsT=w_, rhs=r, start=(o == 0), stop=(o == 7)),
                   reads=[(self.ones32r, None), (self.sq, sl)], writes=[(self.psb, [self.bS2])], cost=T / 2.4 + 8.0)

    def ln_finish(self, l, which, make_xb=True):
        ps = self.psum
        eps = LN_EPS / (ALPHA * ALPHA)
        b1, b2 = self.bS1, self.bS2
        self.act(self.msq.t[:, :], ps[:, b1, :], AF.Square, [(self.psb, b1)], [(self.msq, None)], scale=1.0 / D)
        self.stt(self.rstd.t[:, :], ps[:, b2, :], 1.0 / D, self.msq.t[:, :], ALU.mult, ALU.subtract,
                 [(self.psb, b2), (self.msq, None)], [(self.rstd, None)])
        self.ps_put(b2)
        self.ts(DVE, self.rstd.t[:, :], self.rstd.t[:, :], eps, ALU.add, [(self.rstd, None)], [(self.rstd, None)])
        self.act(self.rstd.t[:, :], self.rstd.t[:, :], AF.Ln, [(self.rstd, None)], [(self.rstd, None)])
        self.act(self.rstd.t[:, :], self.rstd.t[:, :], AF.Exp, [(self.rstd, None)], [(self.rstd, None)], scale=-0.5)
        for o in range(8):
            sl = o % 2
            tmp = self.lnt.t[:, sl, :]
            self.stt(tmp, ps[:, b1, :], -1.0 / D, self.x.t[:, o, :], ALU.mult, ALU.add, [(self.psb, b1), (self.x, o)], [(self.lnt, sl)])
            self.tt(DVE, tmp, tmp, self.rstd.t[:, :], ALU.mult, [(self.lnt, sl), (self.rstd, None)], [(self.lnt, sl)])
            if make_xb:
                self.act(self.xb.t[:, o, :], tmp, AF.Identity, [(self.lnt, sl), (self.cvs, None)], [(self.xb, o)],
                         bias=self.vec(l, f"ln_b{which}", o), scale=self.vec(l, f"ln_g{which}", o))
            self.act(self.x.t[:, o, :], tmp, AF.Identity, [(self.lnt, sl), (self.cvs, None)], [(self.x, o)],
                     bias=self.vec(l, f"ln_b{which}", o), scale=self.vec(l, f"ln_g{which}", o))
        self.ps_put(b1)

    def ffn(self, l, f):
        ps = self.psum
        for j in range(NHC // 2):
            s, w = self.w_get(f"GU{f}_{j}")
            wv = w.rearrange("p (c k m) -> p c k m", c=2, k=8)
            for cc in range(2):
                c = 2 * j + cc
                bg = self.ps_get()
                bu = self.ps_get()
                for k in range(8):
                    self.mm(bg, ps[:, bg, :], wv[:, cc, k, 0:128], self.xb.t[:, k, :], k == 0, k == 7,
                            [(self.ring, s), (self.xb, k)])
                for k in range(8):
                    self.mm(bu, ps[:, bu, :], wv[:, cc, k, 128:256], self.xb.t[:, k, :], k == 0, k == 7,
                            [(self.ring, s), (self.xb, k)])
                sl = c % 2
                self.act(self.sg.t[:, sl, :], ps[:, bg, :], AF.Silu, [(self.psb, bg)], [(self.sg, sl)])
                self.tt(DVE, self.h.t[:, c, :], self.sg.t[:, sl, :], ps[:, bu, :], ALU.mult,
                        [(self.sg, sl), (self.psb, bu)], [(self.h, c)])
                self.ps_put(bg)
                self.ps_put(bu)
        self.ln_stats_begin()
        for o in range(8):
            s, w = self.w_get(f"WD{f}_{o}")
            wv = w.rearrange("p (c m) -> p c m", c=NHC)
            bd = self.ps_get()
            for c in range(NHC):
                self.mm(bd, ps[:, bd, :], wv[:, c, :], self.h.t[:, c, :], c == 0, c == NHC - 1,
                        [(self.ring, s), (self.h, c)])
            self.stt(self.x.t[:, o, :], ps[:, bd, :], 0.5 / ALPHA, self.x.t[:, o, :], ALU.mult, ALU.add,
                     [(self.psb, bd), (self.x, o)], [(self.x, o)])
            self.ps_put(bd)
            self.ln_stats_chunk(o)

    def build(self):
        P = self.P
        self.setup()
        a0 = self.arena0
        self.sb_off = a0
        self.h = self.sb([128, NHC, T], BF16, NHC, "h")
        self.sg = self.sb([128, 2, T], F32, 2, "sg")
        ffn_top = self.sb_off
        self.sb_off = a0
        self.AR = self.sb([128, 4, NCH, 2, CH], BF16, 4, "AR")
        self.Bt = self.sb([128, 4, T], BF16, 4, "Bt")
        self.Kt = self.sb([128, 4, T], BF16, 4, "Kt")
        self.vb = self.sb([128, 4, T], BF16, 4, "vb")
        self.gF = self.sb([128, 4, T], BF16, 4, "gF")
        self.bg = self.sb([128, 4, T], BF16, 4, "bg")
        self.yT = self.sb([128, 4, T], BF16, 4, "yT")
        self.gam = self.sb([128, 4, NCH], F32, 4, "gam")
        r1 = self.sb_off
        tops = []
        self.ucur = self.sb([128, 2, T + 1], F32, 2, "ucur")
        self.xs32 = self.sb([128, 2, T], F32, 2, "xs32")
        self.txa = self.sb([128, T], BF16, 1, "txa")
        self.sxg = self.sb([128, T], BF16, 1, "sxg")
        self.v32 = self.sb([128, 4, T], F32, 4, "v32")
        self.lob = self.sb([32, T], BF16, 1, "lob")
        self.tS = self.sb([128, 8, T], F32, 8, "tS")
        self.kk2 = self.sb([128, T], BF16, 1, "kk2")
        self.rkr = self.kk2
        self.qT = self.sb([128, 4, T], BF16, 4, "qT")
        self.pT = self.sb([128, 2, 2, 4, 128], BF16, 2, "pT")
        self.rec = self.sb([128, 2, 128], F32, 1, "rec")
        tops.append(self.sb_off)
        self.sb_off = r1
        self.VT = self.sb([64, 2, 512], BF16, 2, "VT")
        self.VTp = self.sb([64, 2, 4, 2, 128], BF16, 2, "VTp")
        self.BT = self.sb([64, 2, 512], BF16, 2, "BT")
        self.KT = self.sb([64, 2, 512], BF16, 2, "KT")
        self.NMbr = self.sb([64, 3, 8, 2, 64], BF16, 3, "NMbr")
        self.MakMkr = self.sb([64, 3, 8, 2, 64], BF16, 3, "MakMkr")
        self.Ab = self.sb([64, 4, 8, 64], BF16, 4, "Ab")
        self.ATb = self.sb([64, 4, 8, 64], BF16, 4, "ATb")
        self.Pb = self.sb([64, 4, 8, 64], BF16, 4, "Pb")
        self.Tb = self.sb([64, 3, 8, 64], BF16, 3, "Tb")
        self.XTb = self.sb([64, 1, 512], BF16, 1, "XTb")
        self.UT = self.sb([64, 1, 512], BF16, 1, "UT")
        self.UTp = self.sb([64, 1, 4, 2, 128], BF16, 1, "UTp")
        tops.append(self.sb_off)
        self.sb_off = r1
        self.gat = self.sb([128, 2, T], F32, 2, "gat")
        self.m1 = self.sb([128, 2, T], F32, 2, "m1")
        self.merged = self.sb([128, 8, T], BF16, 8, "merged")
        self.gsq = self.sb([128, 4, T], BF16, 4, "gsq")
        self.gmean = self.sb([128, 4, T], F32, 4, "gmean")
        self.gvar = self.sb([128, 4, T], F32, 4, "gvar")
        self.gtmp = self.sb([128, 4, T], F32, 4, "gtmp")
        tops.append(self.sb_off)
        self.sb_off = max(tops)
        self.set_ffn = [self.h, self.sg]
        self.set_r0 = [self.AR, self.Bt, self.Kt, self.vb, self.gF, self.bg, self.yT, self.gam]
        self.set_att = [self.qT, self.pT, self.rec]
        self.set_prep = [self.ucur, self.xs32, self.txa, self.sxg, self.v32, self.lob, self.tS, self.kk2]
        self.set_scan = [self.VT, self.VTp, self.BT, self.KT, self.NMbr, self.MakMkr, self.Ab, self.ATb, self.Pb, self.Tb,
                         self.XTb, self.UT, self.UTp]
        self.set_post = [self.gat, self.m1, self.merged, self.gsq, self.gmean, self.gvar, self.gtmp]
        self.set_all = self.set_ffn + self.set_r0 + self.set_att + self.set_prep + self.set_scan + self.set_post
        self.guard(self.set_all)
        mix_top = self.sb_off
        self.sb_off = max(ffn_top, mix_top)
        for l in range(DEPTH):
            self.memset(DVE, self.ST32[l].t[:, :, :], 0.0, [(self.ST32[l], None)])
            self.memset(DVE, self.STb[l].t[:, :, :], 0.0, [(self.STb[l], None)])
            self.memset(DVE, self.ucar[l].t[:, :], 0.0, [(self.ucar[l], None)])
            self.memset(DVE, self.kTd[l].t[:, :, :], 0.0, [(self.kTd[l], None)])
            self.memset(DVE, self.vtok[l].t[:, :, :, :], 0.0, [(self.vtok[l], None)])
        xin_ch = [P.new_chan() for _ in range(8)]
        out_ch = [P.new_chan() for _ in range(8)]
        last_out = []
        xTv = self.xT.rearrange("(c p) t -> p c t", p=128)
        oTv = self.outT.rearrange("(c p) t -> p c t", p=128)
        for it in range(self.NT):
            tok = slice(it * T, (it + 1) * T)
            for o in range(8):
                self.dma(SP, self.x.t[:, o, :], xTv[:, o, tok], xin_ch[o], writes=[(self.x, o)])
            for o in range(8):
                self.cp(ACT if o % 2 else DVE, self.xb.t[:, o, :], self.x.t[:, o, :], [(self.x, o)], [(self.xb, o)])
            for l in range(DEPTH):
                self.ffn(l, 0)
                self.ln_finish(l, 0)
                self.mixer(l, it)
                self.ffn(l, 1)
                last = (l == DEPTH - 1)
                self.ln_finish(l, 2, make_xb=not last)
            last_out = [self.dma(SP, oTv[:, o, tok], self.x.t[:, o, :], out_ch[o], reads=[(self.x, o)]) for o in range(8)]
        self.sched_ns = P.schedule()
        stats = P.emit(final_waits=last_out)
        return stats

    def mixer(self, l, it):
        P = self.P
        ps = self.psum
        self.guard(self.set_r0 + self.set_att)
        do_attn = "attn" in self.stages
        do_rw = "rwkv" in self.stages
        s, w = self.w_get("Q")
        if do_attn:
            wv = w.rearrange("p (c k m) -> p c k m", c=4, k=8)
            for ch in range(4):
                b = self.ps_get()
                for k in range(8):
                    self.mm(b, ps[:, b, :], wv[:, ch, k, :], self.xb.t[:, k, :], k == 0, k == 7, [(self.ring, s), (self.xb, k)])
                self.act(self.qT.t[:, ch, :], ps[:, b, :], AF.Identity, [(self.psb, b), (self.cvs, None)], [(self.qT, ch)],
                         bias=self.vec(l, "bq", ch))
                self.ps_put(b)
        s, w = self.w_get("KP")
        kT = self.kTd[l]
        vt = self.vtok[l]
        if do_attn:
            wv = w.rearrange("p (c k m) -> p c k m", c=4, k=8)
            for hp in range(4):
                b = self.ps_get()
                for k in range(8):
                    self.mm(b, ps[:, b, :], wv[:, hp, k, :], self.xb.t[:, k, :], k == 0, k == 7, [(self.ring, s), (self.xb, k)])
                self.act(kT.t[:, hp, 128:128 + T], ps[:, b, :], AF.Identity, [(self.psb, b), (self.cvs, None)], [(kT, hp)],
                         bias=self.vec(l, "bk", hp))
                self.ps_put(b)
        s, w = self.w_get("VP")
        if do_attn:
            wv = w.rearrange("p (k m) -> p k m", k=8)
            for tb in range(4):
                b = self.ps_get()
                for k in range(8):
                    self.mm(b, ps[:, b, :], self.xb.t[:, k, tb * 128:(tb + 1) * 128], wv[:, k, :], k == 0, k == 7,
                            [(self.ring, s), (self.xb, k)])
                self.tt(DVE, vt.t[:, 1 + tb, :, :].rearrange("p a b -> p (a b)"), ps[:, b, :], self.vec(l, "bvrow"), ALU.add,
                        [(self.psb, b), (self.cvs, None)], [(vt, 1 + tb)])
                self.ps_put(b)
            for qb in range(4):
                for hk in range(2):
                    gblk = it * 4 + qb
                    kbs = [1] if gblk == 0 else [0, 1]
                    sb_ = self.ps_get(2)
                    Sv = ps[:, sb_:sb_ + 2, :].rearrange("p a (g q) -> p a g q", g=4)
                    for kb in kbs:
                        self.mm(sb_ + kb, ps[:, sb_ + kb, :], self.identb.t[:, :],
                                self.bias8.t[:, kb, 4 * hk:4 * hk + 4, :].rearrange("p g q -> p (g q)"), True, False,
                                [(self.identb, None), (self.bias8, None)])
                        for g in range(4):
                            ch = 2 * hk + g // 2
                            hp = hk * 2 + g % 2
                            self.mm(sb_ + kb, Sv[:, kb, g, :], kT.t[:, hp, (qb + kb) * 128:(qb + kb + 1) * 128],
                                    self.qT.t[:, ch, qb * 128:(qb + 1) * 128], False, g == 3, [(kT, hp), (self.qT, ch)])
                    sl = (qb * 2 + hk) % 2
                    for kb in kbs:
                        self.act(self.pT.t[:, sl, kb, :, :].rearrange("p g q -> p (g q)"), ps[:, sb_ + kb, :], AF.Exp,
                                 [(self.psb, sb_ + kb)], [(self.pT, sl)], scale=0.125)
                    self.ps_put(sb_, 2)
                    ob = self.ps_get()
                    for chl in range(2):
                        combos = [(par, kb) for par in range(2) for kb in kbs]
                        for i, (par, kb) in enumerate(combos):
                            self.mm(ob, ps[:, ob, chl * 128:(chl + 1) * 128], vt.t[:, qb + kb, hk * 2 + par, :],
                                    self.pT.t[:, sl, kb, 2 * chl + par, :], i == 0, i == len(combos) - 1,
                                    [(vt, qb + kb), (self.pT, sl)])
                        for i, (par, kb) in enumerate(combos):
                            self.mm(ob, ps[:, ob, 256 + chl * 128:256 + (chl + 1) * 128], self.onesb.t[:, par, :],
                                    self.pT.t[:, sl, kb, 2 * chl + par, :], i == 0, i == len(combos) - 1,
                                    [(self.onesb, None), (self.pT, sl)])
                    for chl in range(2):
                        ch = 2 * hk + chl
                        self.act(self.rec.t[:, chl, :], ps[:, ob, 256 + chl * 128:256 + (chl + 1) * 128], AF.Ln,
                                 [(self.psb, ob), (self.cvs, None)], [(self.rec, None)], bias=self.vec(l, "sink", ch))
                    self.act(self.rec.t[:, :, :], self.rec.t[:, :, :], AF.Exp, [(self.rec, None)], [(self.rec, None)], scale=-1.0)
                    self.tt(DVE, self.attnT.t[:, 2 * hk:2 * hk + 2, qb * 128:(qb + 1) * 128],
                            ps[:, ob, 0:256].rearrange("p (c q) -> p c q", c=2), self.rec.t[:, :, :], ALU.mult,
                            [(self.psb, ob), (self.rec, None)], [(self.attnT, [2 * hk, 2 * hk + 1])])
                    self.ps_put(ob)
            for hp in range(4):
                self.cp(POOL, kT.t[:, hp, 0:128], kT.t[:, hp, T:T + 128], [(kT, hp)], [(kT, hp)])
            self.cp(POOL, vt.t[:, 0, :, :], vt.t[:, 4, :, :], [(vt, 4)], [(vt, 0)])
        if do_rw:
            self.rwkv(l, it)
        else:
            for name in ["XS", "V", "RK0", "RK1", "RK2", "RK3"]:
                self.w_get(name)
        for o in range(8):
            s, w = self.w_get(f"MG{o}")
            if not (do_attn or do_rw):
                continue
            wba = w[:, 0:512].rearrange("p (k m) -> p k m", k=4)
            wbb = w[:, 512:1024].rearrange("p (k m) -> p k m", k=4)
            wga = w[:, 1024:2048].rearrange("p (k m) -> p k m", k=8)
            wgb = w[:, 2048:3072].rearrange("p (k m) -> p k m", k=8)
            parts = []
            if do_attn:
                parts.append((wga, wba, self.attnT, "bga", 0))
            if do_rw:
                parts.append((wgb, wbb, self.rwT, "bgb", 1))
            for i, (wg, wb, src, bn, gi0) in enumerate(parts):
                gi = gi0
                b1 = self.ps_get()
                for k in range(8):
                    self.mm(b1, ps[:, b1, :], wg[:, k, :], self.xb.t[:, k, :], k == 0, k == 7, [(self.ring, s), (self.xb, k)])
                self.act(self.gat.t[:, gi, :], ps[:, b1, :], AF.Sigmoid, [(self.psb, b1), (self.cvs, None)], [(self.gat, gi)],
                         bias=self.vec(l, bn, o))
                self.ps_put(b1)
                b2 = self.ps_get()
                for k in range(4):
                    self.mm(b2, ps[:, b2, :], wb[:, k, :], src.t[:, k, :], k == 0, k == 3, [(self.ring, s), (src, k)])
                last = (i == len(parts) - 1)
                if i == 0:
                    dst = self.merged.t[:, o, :] if last else self.m1.t[:, o % 2, :]
                    self.tt(DVE, dst, self.gat.t[:, gi, :], ps[:, b2, :], ALU.mult, [(self.gat, gi), (self.psb, b2)],
                            [(self.merged, o)] if last else [(self.m1, o % 2)])
                else:
                    self.tt(DVE, self.gat.t[:, gi, :], self.gat.t[:, gi, :], ps[:, b2, :], ALU.mult, [(self.gat, gi), (self.psb, b2)],
                            [(self.gat, gi)])
                    self.tt(DVE, self.merged.t[:, o, :], self.gat.t[:, gi, :], self.m1.t[:, o % 2, :], ALU.add,
                            [(self.gat, gi), (self.m1, o % 2)], [(self.merged, o)])
                self.ps_put(b2)
        self.ln_stats_begin()
        for j in range(2):
            s, w = self.w_get(f"WO{j}")
            wv = w.rearrange("p (c k m) -> p c k m", c=4, k=8)
            for oo in range(4):
                o = 4 * j + oo
                if do_attn or do_rw:
                    b = self.ps_get()
                    for k in range(8):
                        self.mm(b, ps[:, b, :], wv[:, oo, k, :], self.merged.t[:, k, :], k == 0, k == 7,
                                [(self.ring, s), (self.merged, k)])
                    self.stt(self.x.t[:, o, :], ps[:, b, :], 1.0 / ALPHA, self.x.t[:, o, :], ALU.mult, ALU.add,
                             [(self.psb, b), (self.x, o)], [(self.x, o)])
                    self.ps_put(b)
                self.ln_stats_chunk(o)
        self.ln_finish(l, 1)
        self.guard(self.set_ffn)

    def shift_mix(self, l, bank, grp, gi, car, ci, dest, dest_rw):
        ps = self.psum
        sl = self.uc_i % 2
        self.uc_i += 1
        uc = self.ucur
        V = lambda n: self.vec(l, n + "_" + grp, gi)
        self.act(dest, ps[:, bank, :], AF.Identity, [(self.psb, bank), (self.cvs, None)], dest_rw, bias=V("bom"), scale=V("om"))
        self.act(uc.t[:, sl, 1:T + 1], ps[:, bank, :], AF.Identity, [(self.psb, bank), (self.cvs, None)], [(uc, sl)],
                 bias=V("bmu"), scale=V("mu"))
        self.cp(POOL, uc.t[:, sl, 0:1], car.t[:, ci:ci + 1], [(car, None)], [(uc, sl)])
        self.tt(DVE, dest, dest, uc.t[:, sl, 0:T], ALU.add, list(dest_rw) + [(uc, sl)], dest_rw)
        self.cp(POOL, car.t[:, ci:ci + 1], uc.t[:, sl, T:T + 1], [(uc, sl)], [(car, None)])

    def rwkv(self, l, it):
        P = self.P
        ps = self.psum
        V = lambda n, c=None: self.vec(l, n, c)
        car = self.ucar[l]
        tS = self.tS
        so = l * NSMALL
        w2a2 = self.wsmall.t[:, so:so + 512]
        g2w = self.wsmall.t[:, so + 512:so + 1024]
        v1w = self.wsmall.t[:, so + 1024:so + 1152].rearrange("p (c r) -> p c r", c=4)
        v2w = self.wsmall.t[:, so + 1152:so + 1664]
        self.guard(self.set_prep)
        self.uc_i = 0
        s, w = self.w_get("XS")
        wv = w.rearrange("p (c k m) -> p c k m", c=2, k=8)
        for ci in range(2):
            b = self.ps_get()
            for k in range(8):
                self.mm(b, ps[:, b, :], wv[:, ci, k, :], self.xb.t[:, k, :], k == 0, k == 7, [(self.ring, s), (self.xb, k)])
            self.shift_mix(l, b, "xs", ci, car, ci, self.xs32.t[:, ci, :], [(self.xs32, ci)])
            self.ps_put(b)
        self.act(self.txa.t[0:64, :], self.xs32.t[0:64, 0, :], AF.Tanh, [(self.xs32, 0)], [(self.txa, None)])
        self.cp(ACT, self.txa.t[64:128, :], self.xs32.t[64:128, 0, :], [(self.xs32, 0)], [(self.txa, None)])
        self.act(self.sxg.t[:, :], self.xs32.t[:, 1, :], AF.Sigmoid, [(self.xs32, 1)], [(self.sxg, None)])
        s, w = self.w_get("V")
        wv = w.rearrange("p (c k m) -> p c k m", c=4, k=8)
        for fc in range(4):
            b = self.ps_get()
            for k in range(8):
                self.mm(b, ps[:, b, :], wv[:, fc, k, :], self.xb.t[:, k, :], k == 0, k == 7, [(self.ring, s), (self.xb, k)])
            self.shift_mix(l, b, "v", fc, car, 2 + fc, self.v32.t[:, fc, :], [(self.v32, fc)])
            self.ps_put(b)
        if l == 0:
            for fc in range(4):
                self.cp(ACT, self.vfirst.t[:, fc, :], self.v32.t[:, fc, :], [(self.v32, fc)], [(self.vfirst, fc)])
        else:
            for fc in range(4):
                self.cp(ACT if fc % 2 else DVE, self.vb.t[:, fc, :], self.v32.t[:, fc, :], [(self.v32, fc)], [(self.vb, fc)])
            b = self.ps_get()
            for fc in range(4):
                self.mm(b, ps[0:32, b, :], v1w[:, fc, :], self.vb.t[:, fc, :], fc == 0, fc == 3, [(self.wsmall, None), (self.vb, fc)])
            self.cp(ACT, self.lob.t[:, :], ps[0:32, b, :], [(self.psb, b)], [(self.lob, None)])
            self.ps_put(b)
            for fc in range(4):
                b = self.ps_get()
                self.mm(b, ps[:, b, :], v2w[0:32, fc * 128:(fc + 1) * 128], self.lob.t[:, :], True, True, [(self.wsmall, None), (self.lob, None)])
                self.act(tS.t[:, 0, :], ps[:, b, :], AF.Sigmoid, [(self.psb, b), (self.cvs, None)], [(tS, 0)], bias=V("v0", fc))
                self.ps_put(b)
                self.tt(DVE, tS.t[:, 1, :], self.vfirst.t[:, fc, :], self.v32.t[:, fc, :], ALU.subtract,
                        [(self.vfirst, fc), (self.v32, fc)], [(tS, 1)])
                self.tt(DVE, tS.t[:, 1, :], tS.t[:, 1, :], tS.t[:, 0, :], ALU.mult, [(tS, 0), (tS, 1)], [(tS, 1)])
                self.tt(DVE, self.v32.t[:, fc, :], self.v32.t[:, fc, :], tS.t[:, 1, :], ALU.add, [(self.v32, fc), (tS, 1)], [(self.v32, fc)])
        for fc in range(4):
            self.cp(ACT, self.vb.t[:, fc, :], self.v32.t[:, fc, :], [(self.v32, fc)], [(self.vb, fc)])
        for fc in range(4):
            s, w = self.w_get(f"RK{fc}")
            wv = w.rearrange("p (c k m) -> p c k m", c=2, k=8)
            r32, k32, sw, a32, cs, e1, e2, e3 = [tS.t[:, i, :] for i in range(8)]
            R = lambda *i: [(tS, j) for j in i]
            for ci, (grp, cbase, slot) in enumerate((("r", 6, 0), ("k", 10, 1))):
                b = self.ps_get()
                for k in range(8):
                    self.mm(b, ps[:, b, :], wv[:, ci, k, :], self.xb.t[:, k, :], k == 0, k == 7, [(self.ring, s), (self.xb, k)])
                self.shift_mix(l, b, grp, fc, car, cbase + fc, tS.t[:, slot, :], [(tS, slot)])
                self.ps_put(b)
            fsl = slice(fc * 128, (fc + 1) * 128)
            b = self.ps_get()
            self.mm(b, ps[:, b, :], w2a2[0:64, fsl], self.txa.t[0:64, :], True, True, [(self.wsmall, None), (self.txa, None)])
            self.act(sw, ps[:, b, :], AF.Sigmoid, [(self.psb, b), (self.cvs, None)], R(2), bias=V("w0", fc))
            self.ps_put(b)
            b = self.ps_get()
            self.mm(b, ps[:, b, :], w2a2[64:128, fsl], self.txa.t[64:128, :], True, True, [(self.wsmall, None), (self.txa, None)])
            self.act(a32, ps[:, b, :], AF.Sigmoid, [(self.psb, b), (self.cvs, None)], R(3), bias=V("a0", fc))
            self.ps_put(b)
            bgz = self.ps_get()
            self.mm(bgz, ps[:, bgz, :], g2w[:, fsl], self.sxg.t[:, :], True, True, [(self.wsmall, None), (self.sxg, None)])
            self.cp(ACT, self.gF.t[:, fc, :], ps[:, bgz, :], [(self.psb, bgz)], [(self.gF, fc)])
            self.P.add(DVE, lambda e, o=cs, d1=sw: e.tensor_tensor_scan(out=o, data0=self.resetm.t[:, :], data1=d1, initial=0.0,
                                                                        op0=ALU.mult, op1=ALU.add),
                       reads=R(2) + [(self.resetm, None)], writes=R(4), cost=1150.0)
            self.act(e1, cs, AF.Exp, R(4), R(5), scale=-DECAY_C)
            self.act(e2, cs, AF.Exp, R(4), R(6), scale=DECAY_C)
            self.tt(DVE, e3, cs, sw, ALU.subtract, R(4, 2), R(7))
            self.act(e3, e3, AF.Exp, R(7), R(7), scale=-DECAY_C)
            self.cp(POOL, self.gam.t[:, fc, :], e1[:, CH - 1::CH], R(5), [(self.gam, fc)])
            kk = sw
            self.ts(DVE, kk, k32, V("k_k", fc), ALU.mult, R(1) + [(self.cvs, None)], R(2))
            self.act(self.kk2.t[:, :], kk, AF.Square, R(2), [(self.kk2, None)])
            b = self.ps_get()
            self.mm(b, ps[:, b, :], self.bdiagb.t[:, :], self.kk2.t[:, :], True, True, [(self.bdiagb, None), (self.kk2, None)])
            nrm = cs
            self.ts(DVE, nrm, ps[:, b, :], 1e-18, ALU.max, [(self.psb, b)], R(4))
            self.ps_put(b)
            self.act(nrm, nrm, AF.Ln, R(4), R(4))
            self.act(nrm, nrm, AF.Exp, R(4), R(4), scale=-0.5)
            kkn = kk
            self.tt(DVE, kkn, kk, nrm, ALU.mult, R(2, 4), R(2))
            ARv = self.AR.t[:, fc, :, :, :]
            self.stt(ARv[:, :, 0, :], kkn.rearrange("p (c t) -> p c t", c=NCH), -1.0, e3.rearrange("p (c t) -> p c t", c=NCH),
                     ALU.mult, ALU.mult, R(2, 7), [(self.AR, fc)])
            self.tt(DVE, ARv[:, :, 1, :], r32.rearrange("p (c t) -> p c t", c=NCH), e1.rearrange("p (c t) -> p c t", c=NCH),
                    ALU.mult, R(0, 5), [(self.AR, fc)])
            bp = e3
            self.tt(DVE, bp, kkn, a32, ALU.mult, R(2, 3), R(7))
            self.tt(DVE, self.Bt.t[:, fc, :], bp, e2, ALU.mult, R(7, 6), [(self.Bt, fc)])
            t1 = e1
            self.ts(DVE, t1, a32, V("k_a", fc), ALU.mult, R(3) + [(self.cvs, None)], R(5), s2=V("omka", fc), op1=ALU.add)
            k2 = t1
            self.tt(DVE, k2, k32, t1, ALU.mult, R(1, 5), R(5))
            self.tt(DVE, self.Kt.t[:, fc, :], k2, e2, ALU.mult, R(5, 6), [(self.Kt, fc)])
            rk = k32
            self.tt(DVE, rk, r32, k2, ALU.mult, R(0, 5), R(1))
            self.ts(DVE, self.rkr.t[:, :], rk, V("r_k", fc), ALU.mult, R(1) + [(self.cvs, None)], [(self.rkr, None)])
            b = self.ps_get()
            self.mm(b, ps[:, b, :], self.bdiagb.t[:, :], self.rkr.t[:, :], True, True, [(self.bdiagb, None), (self.rkr, None)])
            bon = e2
            self.tt(DVE, bon, ps[:, b, :], self.v32.t[:, fc, :], ALU.mult, [(self.psb, b), (self.v32, fc)], R(6))
            self.ps_put(b)
            self.tt(DVE, self.bg.t[:, fc, :], ps[:, bgz, :], bon, ALU.mult, [(self.psb, bgz)] + R(6), [(self.bg, fc)])
            self.ps_put(bgz)
        self.guard(self.set_scan)
        for sl in range(2):
            self.memset(POOL, self.VTp.t[:, sl, :, :, :], 0.0, [(self.VTp, sl)])
        self.memset(POOL, self.UTp.t[:, 0, :, :, :], 0.0, [(self.UTp, 0)])
        ST32 = self.ST32[l]
        STb = self.STb[l]

        def psbf(b):
            return ps[:, b, :].bitcast(BF16)

        def l_steps(c):
            sl = c % 2
            m3 = c % 3
            tok = slice(c * CH, (c + 1) * CH)
            st = {}
            steps = []

            def s_tr():
                b1 = self.ps_get()
                b2 = self.ps_get()
                for fc in range(4):
                    self.tr(b1, psbf(b1)[0:64, fc * 128:(fc + 1) * 128], self.vb.t[:, fc, tok], [(self.vb, fc)])
                for fc in range(4):
                    self.tr(b1, psbf(b1)[0:64, 512 + fc * 128:512 + (fc + 1) * 128], self.Bt.t[:, fc, tok], [(self.Bt, fc)])
                for fc in range(4):
                    self.tr(b2, psbf(b2)[0:64, fc * 128:(fc + 1) * 128], self.Kt.t[:, fc, tok], [(self.Kt, fc)])
                self.cp(ACT, self.VT.t[:, sl, :], psbf(b1)[0:64, 0:512], [(self.psb, b1)], [(self.VT, sl)])
                for par in range(2):
                    self.cp(DVE, self.VTp.t[:, sl, :, par, par * 64:(par + 1) * 64],
                            psbf(b1)[0:64, 0:512].rearrange("p (f a i) -> p f a i", f=4, a=2)[:, :, par, :],
                            [(self.psb, b1)], [(self.VTp, sl)])
                self.cp(ACT, self.BT.t[:, sl, :], psbf(b1)[0:64, 512:1024], [(self.psb, b1)], [(self.BT, sl)])
                self.cp(DVE, self.KT.t[:, sl, :], psbf(b2)[0:64, 0:512], [(self.psb, b2)], [(self.KT, sl)])
                self.ps_put(b1)
                self.ps_put(b2)
            steps.append(s_tr)

            def s_m():
                for (src, dst) in ((self.Bt, self.NMbr), (self.Kt, self.MakMkr)):
                    banks = [self.ps_get(), self.ps_get()]
                    for fc in range(4):
                        for par in range(2):
                            pb = par * 64
                            self.mm(banks[par], ps[0:64, banks[par], fc * 128:(fc + 1) * 128], src.t[pb:pb + 64, fc, tok],
                                    self.AR.t[pb:pb + 64, fc, c, :, :].rearrange("p a t -> p (a t)"), True, True,
                                    [(src, fc), (self.AR, fc)])
                    for par in range(2):
                        self.tt(DVE, dst.t[:, m3, par * 4:par * 4 + 4, :, :].rearrange("p f a t -> p (f a t)"), ps[0:64, banks[par], :],
                                self.mAB.t[:, 0:4, :, :].rearrange("p f a t -> p (f a t)"), ALU.mult,
                                [(self.psb, banks[par]), (self.mAB, None)], [(dst, m3)])
                    self.ps_put(banks[0])
                    self.ps_put(banks[1])
            steps.append(s_m)

            def s_nt():
                b = self.ps_get()
                for hs in range(8):
                    self.tr(b, psbf(b)[0:64, hs * 64:(hs + 1) * 64], self.NMbr.t[:, m3, hs, 0, :], [(self.NMbr, m3)])
                i0 = sl * 2
                self.cp(ACT, self.ATb.t[:, i0, :, :].rearrange("p h t -> p (h t)"), psbf(b)[0:64, 0:512], [(self.psb, b)], [(self.ATb, i0)])
                self.ps_put(b)
                self.cp(POOL, self.Ab.t[:, i0, :, :], self.NMbr.t[:, m3, :, 0, :], [(self.NMbr, m3)], [(self.Ab, i0)])
                self.tt(DVE, self.Pb.t[:, i0, :, :], self.NMbr.t[:, m3, :, 0, :], self.id64.t[:, :, :], ALU.add,
                        [(self.NMbr, m3), (self.id64, None)], [(self.Pb, i0)])
            steps.append(s_nt)

            def mk_stage(kst):
                def s_sq():
                    cur = sl * 2 + (kst - 1) % 2
                    nxt = sl * 2 + kst % 2
                    last = (kst == 5)
                    bAT = self.ps_get()
                    bA = None if last else self.ps_get()
                    for hs in range(8):
                        a_ = self.Ab.t[:, cur, hs, :]
                        at_ = self.ATb.t[:, cur, hs, :]
                        hsl = slice(hs * 64, (hs + 1) * 64)
                        self.mm(bAT, ps[0:64, bAT, hsl], a_, at_, True, True, [(self.Ab, cur), (self.ATb, cur)])
                        if not last:
                            self.mm(bA, ps[0:64, bA, hsl], at_, a_, True, True, [(self.Ab, cur), (self.ATb, cur)])
                    self.cp(ACT, self.ATb.t[:, nxt, :, :].rearrange("p h t -> p (h t)"), ps[0:64, bAT, :], [(self.psb, bAT)], [(self.ATb, nxt)])
                    self.ps_put(bAT)
                    if not last:
                        self.cp(DVE, self.Ab.t[:, nxt, :, :].rearrange("p h t -> p (h t)"), ps[0:64, bA, :], [(self.psb, bA)], [(self.Ab, nxt)])
                        self.ps_put(bA)

                def s_p():
                    cur = sl * 2 + (kst - 1) % 2
                    nxt = sl * 2 + kst % 2
                    last = (kst == 5)
                    bP = self.ps_get()
                    for hs in range(8):
                        hsl = slice(hs * 64, (hs + 1) * 64)
                        self.mm(bP, ps[0:64, bP, hsl], self.ATb.t[:, nxt, hs, :], self.Pb.t[:, cur, hs, :], True, True,
                                [(self.ATb, nxt), (self.Pb, cur)])
                    pc = self.Pb.t[:, cur, :, :].rearrange("p h t -> p (h t)")
                    if last:
                        self.tt(DVE, self.Tb.t[:, m3, :, :].rearrange("p h t -> p (h t)"), ps[0:64, bP, :], pc, ALU.add,
                                [(self.psb, bP), (self.Pb, cur)], [(self.Tb, m3)])
                    else:
                        self.tt(DVE, self.Pb.t[:, nxt, :, :].rearrange("p h t -> p (h t)"), ps[0:64, bP, :], pc, ALU.add,
                                [(self.psb, bP), (self.Pb, cur)], [(self.Pb, nxt)])
                    self.ps_put(bP)
                return [s_sq, s_p]
            for kst in range(1, 6):
                steps.extend(mk_stage(kst))
            return steps

        def s_phase(c):
            sl = c % 2
            m3 = c % 3
            tok = slice(c * CH, (c + 1) * CH)
            bx = self.ps_get()
            self.memset(DVE, ps[0:64, bx, :], 0.0, [(self.psb, bx)])
            for fc in range(4):
                self.mmx(bx, ps[0:64, bx, fc * 128:(fc + 1) * 128], self.AR.t[:, fc, c, 0, :], STb.t[:, fc, :],
                         [(self.AR, fc), (STb, None)])
            for fc in range(4):
                for par in range(2):
                    hs = par * 4 + fc
                    cs_ = slice(fc * 128 + par * 64, fc * 128 + par * 64 + 64)
                    self.mmx(bx, ps[0:64, bx, cs_], self.MakMkr.t[:, m3, hs, 0, :], self.VT.t[:, sl, cs_],
                             [(self.MakMkr, m3), (self.VT, sl)])
            self.cp(ACT, self.XTb.t[:, 0, :], ps[0:64, bx, :], [(self.psb, bx)], [(self.XTb, 0)])
            self.ps_put(bx)
            bu = self.ps_get()
            for fc in range(4):
                for par in range(2):
                    hs = par * 4 + fc
                    cs_ = slice(fc * 128 + par * 64, fc * 128 + par * 64 + 64)
                    self.mm(bu, ps[0:64, bu, cs_], self.Tb.t[:, m3, hs, :], self.XTb.t[:, 0, cs_], True, True, [(self.Tb, m3), (self.XTb, 0)])
            self.cp(DVE, self.UT.t[:, 0, :], ps[0:64, bu, :], [(self.psb, bu)], [(self.UT, 0)])
            for par in range(2):
                self.cp(ACT if par else DVE, self.UTp.t[:, 0, :, par, par * 64:(par + 1) * 64],
                        ps[0:64, bu, :].rearrange("p (f a i) -> p f a i", f=4, a=2)[:, :, par, :], [(self.psb, bu)], [(self.UTp, 0)])
            self.ps_put(bu)
            by = self.ps_get()
            self.memset(DVE, ps[:, by, 0:256], 0.0, [(self.psb, by)])
            for fc in range(4):
                self.mmx(by, ps[:, by, fc * 64:(fc + 1) * 64], STb.t[:, fc, :], self.AR.t[:, fc, c, 1, :], [(STb, None), (self.AR, fc)])
            for fc in range(4):
                o_ = ps[:, by, fc * 64:(fc + 1) * 64]
                for par in range(2):
                    hs = par * 4 + fc
                    self.mmx(by, o_, self.UTp.t[:, 0, fc, par, :], self.NMbr.t[:, m3, hs, 1, :], [(self.UTp, 0), (self.NMbr, m3)])
                    self.mmx(by, o_, self.VTp.t[:, sl, fc, par, :], self.MakMkr.t[:, m3, hs, 1, :], [(self.VTp, sl), (self.MakMkr, m3)])
            self.cp(ACT, self.yT.t[:, :, tok], ps[:, by, 0:256].rearrange("p (f t) -> p f t", f=4), [(self.psb, by)], [(self.yT, None)])
            self.ps_put(by)
            bs = self.ps_get()
            for fc in range(4):
                fsl = slice(fc * 128, (fc + 1) * 128)
                self.mm(bs, ps[:, bs, fsl], self.BT.t[:, sl, fsl], self.UT.t[:, 0, fsl], True, False, [(self.BT, sl), (self.UT, 0)])
                self.mm(bs, ps[:, bs, fsl], self.KT.t[:, sl, fsl], self.VT.t[:, sl, fsl], False, True, [(self.KT, sl), (self.VT, sl)])
            for par in range(2):
                pb = par * 64
                dv = ST32.t[pb:pb + 64, :, par * 64:(par + 1) * 64]
                self.tt(DVE, dv, ps[pb:pb + 64, bs, :].rearrange("p (f a i) -> p f a i", f=4, a=2)[:, :, par, :], dv, ALU.add,
                        [(self.psb, bs), (ST32, None)], [(ST32, None)])
            self.ps_put(bs)
            for fc in range(4):
                self.act(ST32.t[:, fc, :], ST32.t[:, fc, :], AF.Identity, [(ST32, None), (self.gam, fc)], [(ST32, None)],
                         scale=self.gam.t[:, fc, c:c + 1])
            self.cp(ACT, STb.t[:, :, :], ST32.t[:, :, :], [(ST32, None)], [(STb, None)])

        for c0 in range(0, NCH, 2):
            sa = l_steps(c0)
            sb2 = l_steps(c0 + 1)
            for fa, fb in zip(sa, sb2):
                fa()
                fb()
            s_phase(c0)
            s_phase(c0 + 1)
        self.guard(self.set_post)
        for fc in range(4):
            yv = self.yT.t[:, fc, :]
            self.act(self.gsq.t[:, fc, :], yv, AF.Square, [(self.yT, fc)], [(self.gsq, fc)])
            b1 = self.ps_get()
            b2 = self.ps_get()
            self.mm(b1, ps[:, b1, :], self.bdiagb.t[:, :], yv, True, True, [(self.bdiagb, None), (self.yT, fc)])
            self.mm(b2, ps[:, b2, :], self.bdiagb.t[:, :], self.gsq.t[:, fc, :], True, True, [(self.bdiagb, None), (self.gsq, fc)])
            self.act(self.gmean.t[:, fc, :], ps[:, b1, :], AF.Identity, [(self.psb, b1)], [(self.gmean, fc)], scale=1.0 / 64)
            self.act(self.gtmp.t[:, fc, :], ps[:, b1, :], AF.Square, [(self.psb, b1)], [(self.gtmp, fc)], scale=1.0 / 64)
            self.stt(self.gvar.t[:, fc, :], ps[:, b2, :], 1.0 / 64, self.gtmp.t[:, fc, :], ALU.mult, ALU.subtract,
                     [(self.psb, b2), (self.gtmp, fc)], [(self.gvar, fc)])
            self.ps_put(b1)
            self.ps_put(b2)
            self.ts(DVE, self.gvar.t[:, fc, :], self.gvar.t[:, fc, :], GN_EPS, ALU.add, [(self.gvar, fc)], [(self.gvar, fc)])
            self.act(self.gvar.t[:, fc, :], self.gvar.t[:, fc, :], AF.Ln, [(self.gvar, fc)], [(self.gvar, fc)])
            self.act(self.gvar.t[:, fc, :], self.gvar.t[:, fc, :], AF.Exp, [(self.gvar, fc)], [(self.gvar, fc)], scale=-0.5)
            self.tt(DVE, self.gtmp.t[:, fc, :], yv, self.gmean.t[:, fc, :], ALU.subtract, [(self.yT, fc), (self.gmean, fc)], [(self.gtmp, fc)])
            self.tt(DVE, self.gtmp.t[:, fc, :], self.gtmp.t[:, fc, :], self.gvar.t[:, fc, :], ALU.mult, [(self.gtmp, fc), (self.gvar, fc)], [(self.gtmp, fc)])
            self.ts(DVE, self.gtmp.t[:, fc, :], self.gtmp.t[:, fc, :], V("gn_g", fc), ALU.mult, [(self.gtmp, fc), (self.cvs, None)], [(self.gtmp, fc)],
                    s2=V("gn_b", fc), op1=ALU.add)
            self.tt(DVE, self.gtmp.t[:, fc, :], self.gtmp.t[:, fc, :], self.gF.t[:, fc, :], ALU.mult, [(self.gtmp, fc), (self.gF, fc)], [(self.gtmp, fc)])
            self.tt(DVE, self.rwT.t[:, fc, :], self.gtmp.t[:, fc, :], self.bg.t[:, fc, :], ALU.add, [(self.gtmp, fc), (self.bg, fc)], [(self.rwT, fc)])


_CACHE = {}


def _run(inp, NT, **kw):
    key = (NT, tuple(sorted(kw.items())))
    b = Builder(NT, **kw)
    stats = b.build()
    return b, stats


def kernel(**inputs):
    inp = {k: np.asarray(v) for k, v in inputs.items()}
    x = inp["x"].astype(np.float32, copy=False)
    B = x.shape[0]
    NT = SEQ // T
    b, stats = _run(inp, NT)
    wf = pack_weights(inp)
    wsm = pack_small(inp)
    cv = pack_vecs(inp)
    gc = pack_gconst(inp)
    in_maps = []
    for c in range(B):
        in_maps.append({"xT": np.ascontiguousarray(x[c].T), "wf": wf, "wsm": wsm, "cv": cv, "gc": gc})
    res = run_bass_kernel_spmd(b.nc, in_maps, core_ids=list(range(B)))
    out = np.stack([np.ascontiguousarray(r["outT"].T) for r in res.results], axis=0)
    return out.astype(np.float32, copy=False)
```

```python
import math
import numpy as np
import concourse.bass as bass
import concourse.mybir as mybir
from concourse.bass_utils import run_bass_kernel_spmd

F32 = mybir.dt.float32
BF16 = mybir.dt.bfloat16
AF = mybir.ActivationFunctionType
ALU = mybir.AluOpType

PE, ACT, DVE, POOL, SP = "pe", "act", "dve", "pool", "sp"
ENGS = [PE, ACT, DVE, POOL, SP]
SEM_WRAP = 30000

D = 1024
SEQ = 4096
DEPTH = 2
DFF = 2816
NHC = DFF // 128
PROJ = 4608
OFF_Q = 2048
OFF_K = 2560
OFF_V = 2688
OFF_RW = 2816
ALPHA = (2 * DEPTH) ** 0.25
LN_EPS = 1e-5
GN_EPS = 64e-5
T = 512
CH = 64
NCH = T // CH
DECAY_C = math.exp(-0.5)
NSLOT = 4
SLOT = 4096


class Buf:
    def __init__(self, t, n=1, name=""):
        self.t = t
        self.n = n
        self.name = name
        self.lastw = [None] * n
        self.readers = [[] for _ in range(n)]
        self.exclusive = False
        self.extra = [[] for _ in range(n)]


class Op:
    __slots__ = ("eng", "fn", "deps", "signaled", "tok", "chan", "idx", "alldeps", "cost", "lat", "prio", "succ", "indeg", "est")

    def __init__(self, eng, fn, chan=None):
        self.eng = eng
        self.fn = fn
        self.deps = []
        self.signaled = False
        self.tok = None
        self.chan = chan


def _slots(buf, s):
    if s is None:
        return range(buf.n)
    if isinstance(s, int):
        return (s,)
    return s


class Prog:
    def __init__(self, nc, same_engine_sync=True):
        self.nc = nc
        self.ops = {e: [] for e in ENGS}
        self.same_engine_sync = same_engine_sync
        self.nchan = 0
        self.bufs = []
        self.pending = {e: [] for e in ENGS}
        self.last_dma = {}
        self.nops = 0

    def buf(self, t, n=1, name=""):
        b = Buf(t, n, name)
        self.bufs.append(b)
        return b

    def new_chan(self):
        c = self.nchan
        self.nchan += 1
        return c

    def guard(self, old_bufs, new_bufs):
        accs = {}
        for b in old_bufs:
            for i in range(b.n):
                for o in [b.lastw[i]] + b.readers[i]:
                    if o is not None:
                        accs[id(o)] = o
        join = self.add(SP, lambda e: e.nop(), cost=30.0, xdeps=list(accs.values()))
        join.signaled = True
        for b in new_bufs:
            for i in range(b.n):
                b.extra[i] = [join]
                b.lastw[i] = None
                b.readers[i] = []

    def schedule(self):
        import heapq
        allops = []
        for e in ENGS:
            allops.extend(self.ops[e])
        allops.sort(key=lambda o: o.idx)
        for o in allops:
            o.succ = []
            o.indeg = len(o.alldeps)
            o.est = 0.0
        for o in allops:
            for d in o.alldeps:
                d.succ.append(o)
        XL = 400.0
        for o in reversed(allops):
            p = 0.0
            for s_ in o.succ:
                if s_.prio > p:
                    p = s_.prio
            o.prio = p + o.lat + XL
        pend = {e: [] for e in ENGS}
        avail = {e: [] for e in ENGS}
        free = {e: 0.0 for e in ENGS}
        for o in allops:
            if o.indeg == 0:
                heapq.heappush(pend[o.eng], (0.0, o.idx, o))
        order = {e: [] for e in ENGS}
        n_done = 0
        total = len(allops)
        while n_done < total:
            best = None
            for e in ENGS:
                pe_, av = pend[e], avail[e]
                while pe_ and pe_[0][0] <= free[e]:
                    _, _, o = heapq.heappop(pe_)
                    heapq.heappush(av, (-o.prio, o.idx, o))
                if av:
                    t = free[e]
                elif pe_:
                    t = pe_[0][0]
                else:
                    continue
                if best is None or t < best[0]:
                    best = (t, e)
            t, e = best
            if avail[e]:
                _, _, o = heapq.heappop(avail[e])
            else:
                _, _, o = heapq.heappop(pend[e])
            start = max(free[e], o.est)
            free[e] = start + o.cost
            fin = start + o.lat
            order[e].append(o)
            n_done += 1
            for s_ in o.succ:
                v = fin + (XL if s_.eng != e else 60.0)
                if v > s_.est:
                    s_.est = v
                s_.indeg -= 1
                if s_.indeg == 0:
                    heapq.heappush(pend[s_.eng], (s_.est, s_.idx, s_))
        self.ops = order
        return max(free.values())

    def barrier(self):
        lasts = [self.ops[e][-1] for e in ENGS if self.ops[e]]
        lasts += list(self.last_dma.values())
        for o in lasts:
            o.signaled = True
        for e in ENGS:
            self.pending[e] = list(lasts)

    def add(self, eng, fn, reads=(), writes=(), chan=None, cost=100.0, lat=None, xdeps=()):
        op = Op(eng, fn, chan)
        op.idx = self.nops
        self.nops += 1
        op.cost = cost
        op.lat = cost if lat is None else lat
        deps = {}
        for o in xdeps:
            deps[id(o)] = o
        for buf, s in list(reads) + list(writes):
            for i in _slots(buf, s):
                if buf.extra[i]:
                    for o in buf.extra[i]:
                        deps[id(o)] = o
                    buf.extra[i] = []
        for buf, s in reads:
            for i in _slots(buf, s):
                o = buf.lastw[i]
                if o is not None:
                    deps[id(o)] = o
                if buf.exclusive:
                    for r in buf.readers[i]:
                        if r.eng != eng:
                            deps[id(r)] = r
        for buf, s in writes:
            for i in _slots(buf, s):
                o = buf.lastw[i]
                if o is not None:
                    deps[id(o)] = o
                for r in buf.readers[i]:
                    deps[id(r)] = r
        for o in self.pending[eng]:
            deps[id(o)] = o
        self.pending[eng] = []
        op.alldeps = list(deps.values())
        for o in deps.values():
            if o.eng == eng and o.chan is None and chan is None:
                if eng == PE or not self.same_engine_sync:
                    continue
            op.deps.append(o)
            o.signaled = True
        for buf, s in reads:
            for i in _slots(buf, s):
                buf.readers[i].append(op)
        for buf, s in writes:
            for i in _slots(buf, s):
                buf.lastw[i] = op
                buf.readers[i] = []
        if chan is not None:
            op.signaled = True
            self.last_dma[chan] = op
        self.ops[eng].append(op)
        return op

    def emit(self, final_waits=()):
        nc = self.nc
        nsem = {}
        for e in ENGS:
            cnt = 0
            for op in self.ops[e]:
                if op.chan is None and op.signaled:
                    k = cnt // SEM_WRAP
                    op.tok = ((e, k), cnt - k * SEM_WRAP + 1)
                    cnt += 1
            nsem[e] = cnt // SEM_WRAP + 1
        chan_cnt = {}
        for e in ENGS:
            for op in self.ops[e]:
                if op.chan is not None:
                    chan_cnt[op.chan] = chan_cnt.get(op.chan, 0) + 1
                    op.tok = (("chan", op.chan), 16 * chan_cnt[op.chan])
        semh = {}
        for e in ENGS:
            for k in range(nsem[e]):
                semh[(e, k)] = nc.alloc_semaphore(name=f"s_{e}_{k}")
        for c in range(self.nchan):
            semh[("chan", c)] = nc.alloc_semaphore(name=f"s_ch{c}")
        stats = {}

        def run(e, eng):
            known = {}
            nw = 0
            for op in self.ops[e]:
                need = {}
                for d in op.deps:
                    sk, v = d.tok
                    if known.get(sk, 0) >= v:
                        continue
                    if need.get(sk, 0) < v:
                        need[sk] = v
                for sk, v in need.items():
                    eng.wait_ge(semh[sk], v)
                    known[sk] = v
                    nw += 1
                ins = op.fn(eng)
                if op.chan is not None:
                    ins.then_inc(semh[op.tok[0]], 16)
                elif op.signaled:
                    ins.then_inc(semh[op.tok[0]], 1)
            if e == SP:
                for op in final_waits:
                    sk, v = op.tok
                    if known.get(sk, 0) < v:
                        eng.wait_ge(semh[sk], v)
                        known[sk] = v
            stats[e] = (len(self.ops[e]), nw)

        with nc.Block() as block:
            @block.tensor
            def _(eng):
                run(PE, eng)

            @block.scalar
            def _(eng):
                run(ACT, eng)

            @block.vector
            def _(eng):
                run(DVE, eng)

            @block.gpsimd
            def _(eng):
                run(POOL, eng)

            @block.sync
            def _(eng):
                run(SP, eng)
        return stats


def weight_blocks():
    bl = []
    for f in range(2):
        if f == 1:
            bl.append(("Q", 4096))
            bl.append(("KP", 4096))
            bl.append(("VP", 4096))
            bl.append(("XS", 2048))
            bl.append(("V", 4096))
            for fc in range(4):
                bl.append((f"RK{fc}", 2048))
            for o in range(8):
                bl.append((f"MG{o}", 3072))
            for j in range(2):
                bl.append((f"WO{j}", 4096))
        for j in range(NHC // 2):
            bl.append((f"GU{f}_{j}", 4096))
        for o in range(8):
            bl.append((f"WD{f}_{o}", 2816))
    return bl


LAYER_BLOCKS = weight_blocks()
LAYER_W = sum(n for _, n in LAYER_BLOCKS)
TOTW = LAYER_W * DEPTH
NSMALL = 512 + 512 + 128 + 512


def pack_weights(inp):
    wf = np.empty((128, TOTW), np.float32)
    off = 0
    for l in range(DEPTH):
        w_in = inp["w_in"][l]

        def cols(c0, n=128):
            return w_in[:, c0:c0 + n].reshape(8, 128, n).transpose(1, 0, 2)

        for name, n in LAYER_BLOCKS:
            if name.startswith("GU"):
                f, j = int(name[2]), int(name[4:])
                wgu = inp["ffn_w_gu"][l, f]
                blk = np.empty((128, 2, 8, 256), np.float32)
                for cc in range(2):
                    c = 2 * j + cc
                    blk[:, cc, :, 0:128] = wgu[:, c * 128:(c + 1) * 128].reshape(8, 128, 128).transpose(1, 0, 2)
                    blk[:, cc, :, 128:256] = wgu[:, DFF + c * 128:DFF + (c + 1) * 128].reshape(8, 128, 128).transpose(1, 0, 2)
            elif name.startswith("WD"):
                f, o = int(name[2]), int(name[4:])
                wd = inp["ffn_w_down"][l, f]
                blk = wd[:, o * 128:(o + 1) * 128].reshape(NHC, 128, 128).transpose(1, 0, 2)
            elif name == "Q":
                blk = np.stack([cols(OFF_Q + ch * 128) for ch in range(4)], axis=1)
            elif name == "KP":
                parts = []
                z64 = np.zeros((128, 8, 64), np.float32)
                for hk in range(2):
                    kc = cols(OFF_K + hk * 64, 64)
                    parts.append(np.concatenate([kc, z64], axis=2))
                    parts.append(np.concatenate([z64, kc], axis=2))
                blk = np.stack(parts, axis=1)
            elif name == "VP":
                z64 = np.zeros((128, 8, 64), np.float32)
                parts = []
                for hk in range(2):
                    vc = cols(OFF_V + hk * 64, 64)
                    parts.append(np.concatenate([vc, z64], axis=2))
                    parts.append(np.concatenate([z64, vc], axis=2))
                blk = np.concatenate(parts, axis=2)
            elif name == "XS":
                blk = np.stack([cols(OFF_RW + 1536), cols(OFF_RW + 1664)], axis=1)
            elif name == "V":
                blk = np.stack([cols(OFF_RW + 1024 + fc * 128) for fc in range(4)], axis=1)
            elif name.startswith("RK"):
                fc = int(name[2:])
                blk = np.stack([cols(OFF_RW + fc * 128), cols(OFF_RW + 512 + fc * 128)], axis=1)
            elif name.startswith("MG"):
                o = int(name[2:])
                wba = inp["w_branch_attn"][l][:, o * 128:(o + 1) * 128].reshape(4, 128, 128).transpose(1, 0, 2)
                wbb = inp["w_branch_rwkv"][l][:, o * 128:(o + 1) * 128].reshape(4, 128, 128).transpose(1, 0, 2)
                blk = np.concatenate([wba.reshape(128, -1), wbb.reshape(128, -1),
                                      cols(o * 128).reshape(128, -1), cols(D + o * 128).reshape(128, -1)], axis=1)
            elif name.startswith("WO"):
                j = int(name[2:])
                wo = inp["w_out"][l]
                blk = np.stack([wo[:, o * 128:(o + 1) * 128].reshape(8, 128, 128).transpose(1, 0, 2)
                                for o in range(4 * j, 4 * j + 4)], axis=1)
            wf[:, off:off + n] = blk.reshape(128, n)
            off += n
    assert off == TOTW
    return wf


def pack_small(inp):
    ws = np.zeros((128, DEPTH * NSMALL), np.float32)
    for l in range(DEPTH):
        o = l * NSMALL
        ws[0:64, o:o + 512] = inp["rw_w2"][l]
        ws[64:128, o:o + 512] = inp["rw_a2"][l]
        ws[:, o + 512:o + 1024] = inp["rw_g2"][l]
        if l > 0:
            ws[:, o + 1024:o + 1152] = inp["rw_v1"][l - 1].reshape(4, 128, 32).transpose(1, 0, 2).reshape(128, 128)
            ws[0:32, o + 1152:o + 1664] = inp["rw_v2"][l - 1]
    return ws


def vec_table():
    tab = {}
    off = 0

    def add(n, k):
        nonlocal off
        tab[n] = (off, k)
        off += k

    for i in range(3):
        add(f"ln_g{i}", 8)
        add(f"ln_b{i}", 8)
    add("bq", 4)
    add("bk", 4)
    add("b_xs", 2)
    add("b_v", 4)
    add("b_r", 4)
    add("b_k", 4)
    for g_, k_ in (("xs", 2), ("v", 4), ("r", 4), ("k", 4)):
        add("om_" + g_, k_)
        add("bom_" + g_, k_)
        add("bmu_" + g_, k_)
    add("bga", 8)
    add("bgb", 8)
    add("mu_xs", 2)
    add("mu_v", 4)
    add("mu_r", 4)
    add("mu_k", 4)
    for n in ("w0", "a0", "k_k", "k_a", "omka", "r_k", "gn_g", "gn_b", "v0", "sink"):
        add(n, 4)
    add("bvrow", 512)
    return tab, off


VTAB, NVEC = vec_table()


def fm(v):
    return np.ascontiguousarray(v.reshape(-1, 128).T)


def pack_vecs(inp):
    cv = np.zeros((128, DEPTH * NVEC), np.float32)
    for l in range(DEPTH):
        def put(n, a):
            o, k = VTAB[n]
            cv[:, l * NVEC + o:l * NVEC + o + k] = a

        b_in = inp["b_in"][l]
        mu = inp["shift_mu"][l]
        for i in range(3):
            put(f"ln_g{i}", fm(inp["ln_g"][l, i]))
            put(f"ln_b{i}", fm(inp["ln_b"][l, i]))
        put("bq", fm(b_in[OFF_Q:OFF_K]))
        bk = b_in[OFF_K:OFF_V]
        z64 = np.zeros(64, np.float32)
        put("bk", np.stack([np.concatenate([bk[0:64], z64]), np.concatenate([z64, bk[0:64]]),
                            np.concatenate([bk[64:128], z64]), np.concatenate([z64, bk[64:128]])], axis=1))
        brw = b_in[OFF_RW:]
        put("b_xs", fm(brw[1536:1792]))
        put("b_v", fm(brw[1024:1536]))
        put("b_r", fm(brw[0:512]))
        put("b_k", fm(brw[512:1024]))
        put("bga", fm(b_in[0:D]))
        put("bgb", fm(b_in[D:2 * D]))
        put("mu_xs", fm(mu[1536:1792]))
        put("mu_v", fm(mu[1024:1536]))
        put("mu_r", fm(mu[0:512]))
        put("mu_k", fm(mu[512:1024]))
        put("w0", fm(inp["rw_w0"][l]))
        put("a0", fm(inp["rw_a0"][l]))
        put("k_k", fm(inp["rw_k_k"][l]))
        put("k_a", fm(inp["rw_k_a"][l]))
        put("r_k", fm(inp["rw_r_k"][l].reshape(-1)))
        put("gn_g", fm(inp["rw_gn_g"][l]))
        put("gn_b", fm(inp["rw_gn_b"][l]))
        if l > 0:
            put("v0", fm(inp["rw_v0"][l - 1]))
        sk = inp["attn_sinks"][l]
        put("sink", np.stack([np.repeat(sk[2 * ch:2 * ch + 2], 64) for ch in range(4)], axis=1))
        bv = b_in[OFF_V:OFF_RW]
        bvp = np.concatenate([bv[0:64], z64, z64, bv[0:64], bv[64:128], z64, z64, bv[64:128]])
        put("bvrow", np.broadcast_to(bvp[None, :], (128, 512)))
    return cv


def t5_bucket_np(n):
    max_exact = 16
    nf = np.maximum(n, 1).astype(np.float32)
    large = max_exact + (np.log(nf / max_exact) / math.log(128 / max_exact) * (32 - max_exact)).astype(np.int32)
    large = np.minimum(large, 31)
    return np.where(n < max_exact, n, large)


GC = {}
_o = 0
for _n, _k in (("biasg", 2048), ("mask8", 2048), ("ident", 128), ("bdiag", 128), ("reset", 512),
               ("mAB", 1024), ("mT", 512), ("id64", 512)):
    GC[_n] = (_o, _k)
    _o += _k
NGC = _o


def pack_gconst(inp):
    g = np.zeros((128, NGC), np.float32)

    def put(n, a):
        o, k = GC[n]
        g[:a.shape[0], o:o + k] = a.reshape(a.shape[0], k)

    rel = inp["rel_bias"]
    bucket = t5_bucket_np(np.arange(128, dtype=np.int32))
    db = rel[bucket]
    kk = np.arange(128)[:, None]
    qq = np.arange(128)[None, :]
    dprev = np.clip(128 + qq - kk, 0, 127)
    dcur = np.clip(qq - kk, 0, 127)
    bg = np.empty((128, 2, 8, 128), np.float32)
    bg[:, 0] = db[dprev].transpose(0, 2, 1)
    bg[:, 1] = db[dcur].transpose(0, 2, 1)
    m8 = np.empty((128, 2, 8, 128), np.float32)
    m8[:, 0] = np.where(kk > qq, 0.0, -240000.0)[:, None, :]
    m8[:, 1] = np.where(kk <= qq, 0.0, -240000.0)[:, None, :]
    put("biasg", bg)
    put("mask8", m8)
    put("ident", np.eye(128, dtype=np.float32))
    bd = np.zeros((128, 128), np.float32)
    bd[0:64, 0:64] = 1.0
    bd[64:128, 64:128] = 1.0
    put("bdiag", bd)
    rs = np.ones((128, T), np.float32)
    rs[:, ::CH] = 0.0
    put("reset", rs)
    s = np.arange(64)[:, None]
    t = np.arange(64)[None, :]
    mAB = np.empty((64, 8, 2, 64), np.float32)
    mAB[:, :, 0, :] = (s < t).astype(np.float32)[:, None, :]
    mAB[:, :, 1, :] = (s <= t).astype(np.float32)[:, None, :]
    put("mAB", mAB)
    mT = np.broadcast_to((s > t).astype(np.float32)[:, None, :], (64, 8, 64))
    put("mT", np.ascontiguousarray(mT))
    put("id64", np.ascontiguousarray(np.broadcast_to(np.eye(64, dtype=np.float32)[:, None, :], (64, 8, 64))))
    return g


class Builder:
    def __init__(self, NT, stages=("ffn", "attn", "rwkv"), dbg=None):
        self.NT = NT
        self.stages = stages
        self.dbg = dbg
        nc = self.nc = bass.Bass("TRN2", target_bir_lowering=False)
        self.P = Prog(nc)
        S = NT * T
        self.xT = nc.dram_tensor("xT", [D, S], F32, kind="ExternalInput").ap()
        self.wf = nc.dram_tensor("wf", [128, TOTW], F32, kind="ExternalInput").ap()
        self.wsm = nc.dram_tensor("wsm", [128, DEPTH * NSMALL], F32, kind="ExternalInput").ap()
        self.cv = nc.dram_tensor("cv", [128, DEPTH * NVEC], F32, kind="ExternalInput").ap()
        self.gc = nc.dram_tensor("gc", [128, NGC], F32, kind="ExternalInput").ap()
        self.outT = nc.dram_tensor("outT", [D, S], F32, kind="ExternalOutput").ap()
        self.wscr = nc.dram_tensor("wscr", [128, TOTW], BF16, kind="Internal").ap()
        self.sb_off = 16512
        self.sb_top = 229344
        self.n_alloc = 0
        self.all_sb = []
        self.psfree = list(range(8))

    def sb(self, shape, dtype, nslots=1, name=None, at=None):
        nb = int(np.prod(shape[1:])) * (4 if dtype == F32 else 2)
        nb = (nb + 31) // 32 * 32
        if at is None:
            off = self.sb_off
            self.sb_off += nb
            assert self.sb_off <= self.sb_top, f"SBUF overflow {self.sb_off}"
        else:
            off = at
        self.n_alloc += 1
        t = self.nc.alloc_sbuf_tensor_at(name or f"t{self.n_alloc}", list(shape), dtype, offset=off)
        b = self.P.buf(t, nslots, name or "")
        b.off = off
        b.size = nb
        self.all_sb.append(b)
        return b

    def guard(self, new_bufs):
        old = []
        for b in self.all_sb:
            if b.off < self.arena0:
                continue
            for nb in new_bufs:
                if b.off < nb.off + nb.size and nb.off < b.off + b.size:
                    old.append(b)
                    break
        self.P.guard(old, new_bufs)

    def ps_get(self, n=1):
        if n == 1:
            return self.psfree.pop(0)
        for i, b in enumerate(self.psfree):
            if b % 2 == 0 and (b + 1) in self.psfree:
                self.psfree.remove(b)
                self.psfree.remove(b + 1)
                return b
        raise RuntimeError("no psum pair")

    def ps_put(self, b, n=1):
        for i in range(n):
            self.psfree.append(b + i)

    @staticmethod
    def _fs(ap):
        n = 1
        for d in ap.shape[1:]:
            n *= int(d)
        return n

    def _ec(self, eng, out, mult=1.0):
        n = self._fs(out) * mult
        if eng == POOL:
            return 150.0 + 2.4 * n
        return 80.0 + n / 0.96

    def mm(self, bank, out, lhsT, rhs, start, stop, reads):
        banks = bank if isinstance(bank, (list, tuple)) else [bank]
        c = max(self._fs(rhs), 64) / 2.4 * (4.0 if rhs.dtype == F32 else 1.0) + 8.0
        return self.P.add(PE, lambda e: e.matmul(out, lhsT=lhsT, rhs=rhs, start=start, stop=stop),
                          reads=reads, writes=[(self.psb, list(banks))], cost=c)

    def mmx(self, bank, out, lhsT, rhs, reads):
        c = max(self._fs(rhs), 64) / 2.4 + 8.0
        return self.P.add(PE, lambda e: e.matmul(out, lhsT=lhsT, rhs=rhs, start=False, stop=True, skip_group_check=True),
                          reads=reads, writes=[(self.psb, [bank])], cost=c)

    def tr(self, bank, out, in_, reads):
        ident = self.identb.t[0:in_.shape[0], 0:in_.shape[0]]
        return self.P.add(PE, lambda e: e.transpose(out, in_, ident),
                          reads=list(reads) + [(self.identb, None)], writes=[(self.psb, [bank])], cost=60.0)

    def act(self, out, in_, func, reads, writes, bias=None, scale=None):
        kw = {}
        if bias is not None:
            kw["bias"] = bias
        if scale is not None:
            kw["scale"] = scale
        return self.P.add(ACT, lambda e: e.activation(out=out, in_=in_, func=func, **kw), reads=reads, writes=writes,
                          cost=self._ec(ACT, out) + 60.0)

    def tt(self, eng, out, in0, in1, op, reads, writes):
        return self.P.add(eng, lambda e: e.tensor_tensor(out=out, in0=in0, in1=in1, op=op), reads=reads, writes=writes,
                          cost=self._ec(eng, out, 1.25))

    def ts(self, eng, out, in0, s1, op0, reads, writes, s2=None, op1=None):
        if op1 is None:
            if eng == POOL:
                return self.P.add(eng, lambda e: e.tensor_scalar(out=out, in0=in0, scalar1=s1, scalar2=0.0, op0=op0, op1=ALU.add),
                                  reads=reads, writes=writes, cost=self._ec(eng, out))
            return self.P.add(eng, lambda e: e.tensor_scalar(out=out, in0=in0, scalar1=s1, scalar2=None, op0=op0),
                              reads=reads, writes=writes, cost=self._ec(eng, out))
        return self.P.add(eng, lambda e: e.tensor_scalar(out=out, in0=in0, scalar1=s1, scalar2=s2, op0=op0, op1=op1),
                          reads=reads, writes=writes, cost=self._ec(eng, out))

    def stt(self, out, in0, scalar, in1, op0, op1, reads, writes):
        return self.P.add(DVE, lambda e: e.scalar_tensor_tensor(out=out, in0=in0, scalar=scalar, in1=in1, op0=op0, op1=op1),
                          reads=reads, writes=writes, cost=self._ec(DVE, out, 1.25))

    def cp(self, eng, out, in_, reads, writes):
        if eng == ACT:
            return self.P.add(ACT, lambda e: e.copy(out=out, in_=in_), reads=reads, writes=writes, cost=self._ec(ACT, out))
        m = 1.5 if (eng == POOL and out.dtype != in_.dtype) else 1.0
        return self.P.add(eng, lambda e: e.tensor_copy(out=out, in_=in_), reads=reads, writes=writes, cost=self._ec(eng, out, m))

    def memset(self, eng, ap, val, writes):
        return self.P.add(eng, lambda e: e.memset(ap, val), writes=writes, cost=self._ec(eng, ap))

    def dma(self, eng, out, in_, chan, reads=(), writes=(), xdeps=()):
        nb = int(np.prod([int(d) for d in out.shape])) * (4 if out.dtype == F32 else 2)
        return self.P.add(eng, lambda e: e.dma_start(out=out, in_=in_), reads=reads, writes=writes, chan=chan,
                          cost=(700.0 if eng == POOL else 60.0), lat=2500.0 + nb / 150.0, xdeps=xdeps)

    def w_init(self):
        self.ring = self.sb([128, NSLOT, SLOT], BF16, NSLOT, "ring")
        self.ring_ch = [self.P.new_chan() for _ in range(NSLOT)]
        self.wseq = []
        off = 0
        for l in range(DEPTH):
            for name, n in LAYER_BLOCKS:
                self.wseq.append((l, name, off, n))
                off += n
        self.nblk = len(self.wseq)
        self.castd = self.P.buf(None, self.nblk, "castd")
        self.NCC = 16
        self.cast_ch = [self.P.new_chan() for _ in range(self.NCC)]
        self.cast_ops = {}
        self.cast_issued = 0
        self.load_ops = {}
        self.CASTW = 5
        self.w_issued = 0
        self.w_cur = 0

    def w_cast(self, upto):
        while self.cast_issued < min(self.nblk, upto):
            i = self.cast_issued
            l, name, o, n = self.wseq[i]
            xd = [self.load_ops[i - self.CASTW]] if (i - self.CASTW) in self.load_ops else []
            if (i - self.NCC) in self.cast_ops:
                xd.append(self.cast_ops[i - self.NCC])
            self.cast_ops[i] = self.dma(POOL, self.wscr[:, o:o + n], self.wf[:, o:o + n], self.cast_ch[i % self.NCC],
                                        writes=[(self.castd, i)], xdeps=xd)
            self.cast_issued += 1

    def w_get(self, expect):
        total = self.nblk * self.NT
        while self.w_issued < min(total, self.w_cur + NSLOT):
            i = self.w_issued
            l, name, o, n = self.wseq[i % self.nblk]
            s = i % NSLOT
            reads = []
            if i < self.nblk:
                self.w_cast(i + self.CASTW)
                reads = [(self.castd, i)]
            op = self.dma(SP, self.ring.t[:, s, 0:n], self.wscr[:, o:o + n], self.ring_ch[s], reads=reads,
                          writes=[(self.ring, s)])
            if i < self.nblk:
                self.load_ops[i] = op
            self.w_issued += 1
        i = self.w_cur
        l, name, o, n = self.wseq[i % self.nblk]
        assert name == expect, (name, expect)
        self.w_cur += 1
        s = i % NSLOT
        return s, self.ring.t[:, s, 0:n]

    def vec(self, l, name, c=None):
        o, k = VTAB[name]
        if c is None:
            return self.cvs.t[:, l * NVEC + o:l * NVEC + o + k]
        return self.cvs.t[:, l * NVEC + o + c:l * NVEC + o + c + 1]

    def setup(self):
        P = self.P
        self.psum = self.nc.alloc_psum_tensor("ps", [128, 8, 512], F32)
        self.psb = P.buf(self.psum, 8, "psum")
        self.psb.exclusive = True
        self.x = self.sb([128, 8, T], F32, 8, "x")
        self.xb = self.sb([128, 8, T], BF16, 8, "xb")
        self.cvs = self.sb([128, DEPTH * NVEC], F32, 1, "cvs")
        self.wsmall = self.sb([128, DEPTH * NSMALL], BF16, 1, "wsmall")
        self.identb = self.sb([128, 128], BF16, 1, "identb")
        self.bdiagb = self.sb([128, 128], BF16, 1, "bdiagb")
        self.ones32 = self.sb([128, 128], F32, 1, "ones32")
        self.ones32r = self.sb([128, 128], F32, 1, "ones32r")
        self.epsv = self.sb([128, 2], F32, 1, "epsv")
        self.onesb = self.sb([128, 2, 128], BF16, 1, "onesb")
        self.bias8 = self.sb([128, 2, 8, 128], BF16, 1, "bias8")
        self.resetm = self.sb([128, T], F32, 1, "resetm")
        self.mAB = self.sb([64, 4, 2, 64], BF16, 1, "mAB")
        self.id64 = self.sb([64, 8, 64], BF16, 1, "id64")
        self.sq = self.sb([128, 2, T], F32, 2, "sq")
        self.mean = self.sb([128, T], F32, 1, "mean")
        self.msq = self.sb([128, T], F32, 1, "msq")
        self.rstd = self.sb([128, T], F32, 1, "rstd")
        self.lnt = self.sb([128, 2, T], F32, 2, "lnt")
        self.kTd = [self.sb([128, 4, 128 + T], BF16, 4, f"kTd{l}") for l in range(DEPTH)]
        self.vtok = [self.sb([128, 5, 4, 128], BF16, 5, f"vtok{l}") for l in range(DEPTH)]
        self.attnT = self.sb([128, 4, T], BF16, 4, "attnT")
        self.rwT = self.sb([128, 4, T], BF16, 4, "rwT")
        self.vfirst = self.sb([128, 4, T], BF16, 4, "vfirst")
        self.ST32 = [self.sb([128, 4, 128], F32, 1, f"ST32_{l}") for l in range(DEPTH)]
        self.STb = [self.sb([128, 4, 128], BF16, 1, f"STb_{l}") for l in range(DEPTH)]
        self.ucar = [self.sb([128, 16], F32, 1, f"ucar{l}") for l in range(DEPTH)]
        self.w_init()
        self.arena0 = self.sb_off
        stg = self.sb([128, NGC], F32, 1, "stg", at=self.arena0)
        ch = P.new_chan()
        ch2 = P.new_chan()
        self.dma(SP, stg.t[:, :], self.gc, ch, writes=[(stg, None)])
        self.dma(SP, self.cvs.t[:, :], self.cv, ch2, writes=[(self.cvs, None)])
        self.dma(POOL, self.wsmall.t[:, :], self.wsm, P.new_chan(), writes=[(self.wsmall, None)])

        def g(n, rows=128):
            o, k = GC[n]
            return stg.t[0:rows, o:o + k]

        self.cp(DVE, self.identb.t[:, :], g("ident"), [(stg, None)], [(self.identb, None)])
        self.cp(DVE, self.bdiagb.t[:, :], g("bdiag"), [(stg, None)], [(self.bdiagb, None)])
        self.cp(DVE, self.resetm.t[:, :], g("reset"), [(stg, None)], [(self.resetm, None)])
        self.cp(DVE, self.mAB.t[:, :, :, :].rearrange("p a b c -> p (a b c)"), g("mAB", 64)[:, 0:512], [(stg, None)], [(self.mAB, None)])
        self.cp(DVE, self.id64.t[:, :, :].rearrange("p a b -> p (a b)"), g("id64", 64), [(stg, None)], [(self.id64, None)])
        self.stt(self.bias8.t[:, :, :, :].rearrange("p a b c -> p (a b c)"), g("biasg"), 8.0, g("mask8"), ALU.mult, ALU.add,
                 [(stg, None)], [(self.bias8, None)])
        self.memset(DVE, self.ones32.t[:, :], 1.0, [(self.ones32, None)])
        self.memset(DVE, self.epsv.t[:, 0:1], LN_EPS / (ALPHA * ALPHA), [(self.epsv, None)])
        self.memset(DVE, self.epsv.t[:, 1:2], GN_EPS, [(self.epsv, None)])
        self.act(self.ones32r.t[:, :].bitcast(mybir.dt.float32r), self.ones32.t[:, :], AF.Identity, [(self.ones32, None)], [(self.ones32r, None)])
        self.memset(DVE, self.onesb.t[:, :, :], 0.0, [(self.onesb, None)])
        self.memset(DVE, self.onesb.t[:, 0, 0:64], 1.0, [(self.onesb, None)])
        self.memset(DVE, self.onesb.t[:, 1, 64:128], 1.0, [(self.onesb, None)])
        for l in range(DEPTH):
            self.ts(DVE, self.vec(l, "omka"), self.vec(l, "k_a"), -1.0, ALU.mult, [(self.cvs, None)], [(self.cvs, None)],
                    s2=1.0, op1=ALU.add)
            self.act(self.vec(l, "sink"), self.vec(l, "sink"), AF.Exp, [(self.cvs, None)], [(self.cvs, None)])
            for g_ in ("xs", "v", "r", "k"):
                C = [(self.cvs, None)]
                self.ts(DVE, self.vec(l, "om_" + g_), self.vec(l, "mu_" + g_), -1.0, ALU.mult, C, C, s2=1.0, op1=ALU.add)
                self.tt(DVE, self.vec(l, "bom_" + g_), self.vec(l, "b_" + g_), self.vec(l, "om_" + g_), ALU.mult, C, C)
                self.tt(DVE, self.vec(l, "bmu_" + g_), self.vec(l, "b_" + g_), self.vec(l, "mu_" + g_), ALU.mult, C, C)
        self.stg = stg

    def ln_stats_begin(self):
        self.bS1 = self.ps_get()
        self.bS2 = self.ps_get()
        self.bS1_ = self.bS1
        self.bS2_ = self.bS2

    def ln_stats_chunk(self, o):
        ps = self.psum
        R = mybir.dt.float32r
        self.act(self.sq.t[:, 0, :].bitcast(R), self.x.t[:, o, :], AF.Square, [(self.x, o)], [(self.sq, 0)])
        self.act(self.sq.t[:, 1, :].bitcast(R), self.x.t[:, o, :], AF.Identity, [(self.x, o)], [(self.sq, 1)])
        w_ = self.ones32r.t[:, :].bitcast(R)
        c = T / 2.4 + 8.0
        self.P.add(PE, lambda e, r=self.sq.t[:, 1, :].bitcast(R), o_=ps[:, self.bS1, :]: e.matmul(o_, lhsT=w_, rhs=r, start=(o == 0), stop=(o == 7)),
                   reads=[(self.ones32r, None), (self.sq, 1)], writes=[(self.psb, [self.bS1])], cost=c)
        self.P.add(PE, lambda e, r=self.sq.t[:, 0, :].bitcast(R), o_=ps[:, self.bS2, :]: e.matmul(o_, lhsT=w_, rhs=r, start=(o == 0), stop=(o == 7)),
                   reads=[(self.ones32r, None), (self.sq, 0)], writes=[(self.psb, [self.bS2])], cost=c)

    def ln_finish(self, l, which, make_xb=True):
        ps = self.psum
        eps = LN_EPS / (ALPHA * ALPHA)
        b1, b2 = self.bS1, self.bS2
        self.act(self.msq.t[:, :], ps[:, b1, :], AF.Square, [(self.psb, b1)], [(self.msq, None)], scale=1.0 / D)
        self.stt(self.rstd.t[:, :], ps[:, b2, :], 1.0 / D, self.msq.t[:, :], ALU.mult, ALU.subtract,
                 [(self.psb, b2), (self.msq, None)], [(self.rstd, None)])
        self.ps_put(b2)
        self.act(self.rstd.t[:, :], self.rstd.t[:, :], AF.Ln, [(self.rstd, None), (self.epsv, None)], [(self.rstd, None)], bias=self.epsv.t[:, 0:1])
        self.act(self.rstd.t[:, :], self.rstd.t[:, :], AF.Exp, [(self.rstd, None)], [(self.rstd, None)], scale=-0.5)
        for o in range(8):
            sl = o % 2
            tmp = self.lnt.t[:, sl, :]
            self.stt(tmp, ps[:, b1, :], -1.0 / D, self.x.t[:, o, :], ALU.mult, ALU.add, [(self.psb, b1), (self.x, o)], [(self.lnt, sl)])
            self.tt(DVE, tmp, tmp, self.rstd.t[:, :], ALU.mult, [(self.lnt, sl), (self.rstd, None)], [(self.lnt, sl)])
            if make_xb:
                self.act(self.xb.t[:, o, :], tmp, AF.Identity, [(self.lnt, sl), (self.cvs, None)], [(self.xb, o)],
                         bias=self.vec(l, f"ln_b{which}", o), scale=self.vec(l, f"ln_g{which}", o))
            self.act(self.x.t[:, o, :], tmp, AF.Identity, [(self.lnt, sl), (self.cvs, None)], [(self.x, o)],
                     bias=self.vec(l, f"ln_b{which}", o), scale=self.vec(l, f"ln_g{which}", o))
        self.ps_put(b1)

    def ffn(self, l, f):
        ps = self.psum
        for j in range(NHC // 2):
            s, w = self.w_get(f"GU{f}_{j}")
            wv = w.rearrange("p (c k m) -> p c k m", c=2, k=8)
            for cc in range(2):
                c = 2 * j + cc
                bg = self.ps_get()
                bu = self.ps_get()
                for k in range(8):
                    self.mm(bg, ps[:, bg, :], wv[:, cc, k, 0:128], self.xb.t[:, k, :], k == 0, k == 7,
                            [(self.ring, s), (self.xb, k)])
                for k in range(8):
                    self.mm(bu, ps[:, bu, :], wv[:, cc, k, 128:256], self.xb.t[:, k, :], k == 0, k == 7,
                            [(self.ring, s), (self.xb, k)])
                sl = c % 2
                self.act(self.sg.t[:, sl, :], ps[:, bg, :], AF.Silu, [(self.psb, bg)], [(self.sg, sl)])
                self.tt(DVE, self.h.t[:, c, :], self.sg.t[:, sl, :], ps[:, bu, :], ALU.mult,
                        [(self.sg, sl), (self.psb, bu)], [(self.h, c)])
                self.ps_put(bg)
                self.ps_put(bu)
        self.ln_stats_begin()
        for o in range(8):
            s, w = self.w_get(f"WD{f}_{o}")
            wv = w.rearrange("p (c m) -> p c m", c=NHC)
            bd = self.ps_get()
            for c in range(NHC):
                self.mm(bd, ps[:, bd, :], wv[:, c, :], self.h.t[:, c, :], c == 0, c == NHC - 1,
                        [(self.ring, s), (self.h, c)])
            self.stt(self.x.t[:, o, :], ps[:, bd, :], 0.5 / ALPHA, self.x.t[:, o, :], ALU.mult, ALU.add,
                     [(self.psb, bd), (self.x, o)], [(self.x, o)])
            self.ps_put(bd)
            self.ln_stats_chunk(o)

    def build(self):
        P = self.P
        self.setup()
        a0 = self.arena0
        self.sb_off = a0
        self.h = self.sb([128, NHC, T], BF16, NHC, "h")
        self.sg = self.sb([128, 2, T], F32, 2, "sg")
        ffn_top = self.sb_off
        self.sb_off = a0
        self.AR = self.sb([128, 4, NCH, 2, CH], BF16, 4, "AR")
        self.Bt = self.sb([128, 4, T], BF16, 4, "Bt")
        self.Kt = self.sb([128, 4, T], BF16, 4, "Kt")
        self.vb = self.sb([128, 4, T], BF16, 4, "vb")
        self.gF = self.sb([128, 4, T], BF16, 4, "gF")
        self.bg = self.sb([128, 4, T], BF16, 4, "bg")
        self.yT = self.sb([128, 4, T], BF16, 4, "yT")
        self.gam = self.sb([128, 4, NCH], F32, 4, "gam")
        r1 = self.sb_off
        tops = []
        self.ucur = self.sb([128, 2, T + 1], F32, 2, "ucur")
        self.xs32 = self.sb([128, 2, T], F32, 2, "xs32")
        self.txa = self.sb([128, T], BF16, 1, "txa")
        self.sxg = self.sb([128, T], BF16, 1, "sxg")
        self.v32 = self.sb([128, 4, T], F32, 4, "v32")
        self.lob = self.sb([32, T], BF16, 1, "lob")
        self.tS = self.sb([128, 8, T], F32, 8, "tS")
        self.kk2 = self.sb([128, T], BF16, 1, "kk2")
        self.rkr = self.kk2
        self.qT = self.sb([128, 4, T], BF16, 4, "qT")
        self.pT = self.sb([128, 2, 2, 4, 128], BF16, 2, "pT")
        self.rec = self.sb([128, 2, 128], F32, 1, "rec")
        tops.append(self.sb_off)
        self.sb_off = r1
        self.VT = self.sb([64, 2, 512], BF16, 2, "VT")
        self.VTp = self.sb([64, 2, 4, 2, 128], BF16, 2, "VTp")
        self.BT = self.sb([64, 2, 512], BF16, 2, "BT")
        self.KT = self.sb([64, 2, 512], BF16, 2, "KT")
        self.NMbr = self.sb([64, 3, 8, 2, 64], BF16, 3, "NMbr")
        self.MakMkr = self.sb([64, 3, 8, 2, 64], BF16, 3, "MakMkr")
        self.Ab = self.sb([64, 4, 8, 64], BF16, 4, "Ab")
        self.ATb = self.sb([64, 4, 8, 64], BF16, 4, "ATb")
        self.Pb = self.sb([64, 4, 8, 64], BF16, 4, "Pb")
        self.Tb = self.sb([64, 3, 8, 64], BF16, 3, "Tb")
        self.XTb = self.sb([64, 1, 512], BF16, 1, "XTb")
        self.UT = self.sb([64, 1, 512], BF16, 1, "UT")
        self.UTp = self.sb([64, 1, 4, 2, 128], BF16, 1, "UTp")
        tops.append(self.sb_off)
        self.sb_off = r1
        self.gat = self.sb([128, 2, T], F32, 2, "gat")
        self.m1 = self.sb([128, 2, T], F32, 2, "m1")
        self.merged = self.sb([128, 8, T], BF16, 8, "merged")
        self.gsq = self.sb([128, 4, T], BF16, 4, "gsq")
        self.gmean = self.sb([128, 4, T], F32, 4, "gmean")
        self.gvar = self.sb([128, 4, T], F32, 4, "gvar")
        self.gtmp = self.sb([128, 4, T], F32, 4, "gtmp")
        tops.append(self.sb_off)
        self.sb_off = max(tops)
        self.set_ffn = [self.h, self.sg]
        self.set_r0 = [self.AR, self.Bt, self.Kt, self.vb, self.gF, self.bg, self.yT, self.gam]
        self.set_att = [self.qT, self.pT, self.rec]
        self.set_prep = [self.ucur, self.xs32, self.txa, self.sxg, self.v32, self.lob, self.tS, self.kk2]
        self.set_scan = [self.VT, self.VTp, self.BT, self.KT, self.NMbr, self.MakMkr, self.Ab, self.ATb, self.Pb, self.Tb,
                         self.XTb, self.UT, self.UTp]
        self.set_post = [self.gat, self.m1, self.merged, self.gsq, self.gmean, self.gvar, self.gtmp]
        self.set_all = self.set_ffn + self.set_r0 + self.set_att + self.set_prep + self.set_scan + self.set_post
        self.guard(self.set_all)
        mix_top = self.sb_off
        self.sb_off = max(ffn_top, mix_top)
        for l in range(DEPTH):
            self.memset(DVE, self.ST32[l].t[:, :, :], 0.0, [(self.ST32[l], None)])
            self.memset(DVE, self.STb[l].t[:, :, :], 0.0, [(self.STb[l], None)])
            self.memset(DVE, self.ucar[l].t[:, :], 0.0, [(self.ucar[l], None)])
            self.memset(DVE, self.kTd[l].t[:, :, :], 0.0, [(self.kTd[l], None)])
            self.memset(DVE, self.vtok[l].t[:, :, :, :], 0.0, [(self.vtok[l], None)])
        xin_ch = [P.new_chan() for _ in range(8)]
        out_ch = [P.new_chan() for _ in range(8)]
        last_out = []
        xTv = self.xT.rearrange("(c p) t -> p c t", p=128)
        oTv = self.outT.rearrange("(c p) t -> p c t", p=128)
        for it in range(self.NT):
            tok = slice(it * T, (it + 1) * T)
            for o in range(8):
                self.dma(SP, self.x.t[:, o, :], xTv[:, o, tok], xin_ch[o], writes=[(self.x, o)])
            for o in range(8):
                self.cp(ACT if o % 2 else DVE, self.xb.t[:, o, :], self.x.t[:, o, :], [(self.x, o)], [(self.xb, o)])
            for l in range(DEPTH):
                self.ffn(l, 0)
                self.ln_finish(l, 0)
                self.mixer(l, it)
                self.ffn(l, 1)
                last = (l == DEPTH - 1)
                self.ln_finish(l, 2, make_xb=not last)
            last_out = [self.dma(SP, oTv[:, o, tok], self.x.t[:, o, :], out_ch[o], reads=[(self.x, o)]) for o in range(8)]
        self.sched_ns = P.schedule()
        stats = P.emit(final_waits=last_out)
        return stats

    def mixer(self, l, it):
        P = self.P
        ps = self.psum
        self.guard(self.set_r0 + self.set_att)
        do_attn = "attn" in self.stages
        do_rw = "rwkv" in self.stages
        s, w = self.w_get("Q")
        if do_attn:
            wv = w.rearrange("p (c k m) -> p c k m", c=4, k=8)
            for ch in range(4):
                b = self.ps_get()
                for k in range(8):
                    self.mm(b, ps[:, b, :], wv[:, ch, k, :], self.xb.t[:, k, :], k == 0, k == 7, [(self.ring, s), (self.xb, k)])
                self.act(self.qT.t[:, ch, :], ps[:, b, :], AF.Identity, [(self.psb, b), (self.cvs, None)], [(self.qT, ch)],
                         bias=self.vec(l, "bq", ch))
                self.ps_put(b)
        s, w = self.w_get("KP")
        kT = self.kTd[l]
        vt = self.vtok[l]
        if do_attn:
            wv = w.rearrange("p (c k m) -> p c k m", c=4, k=8)
            for hp in range(4):
                b = self.ps_get()
                for k in range(8):
                    self.mm(b, ps[:, b, :], wv[:, hp, k, :], self.xb.t[:, k, :], k == 0, k == 7, [(self.ring, s), (self.xb, k)])
                self.act(kT.t[:, hp, 128:128 + T], ps[:, b, :], AF.Identity, [(self.psb, b), (self.cvs, None)], [(kT, hp)],
                         bias=self.vec(l, "bk", hp))
                self.ps_put(b)
        s, w = self.w_get("VP")
        if do_attn:
            wv = w.rearrange("p (k m) -> p k m", k=8)
            for tb in range(4):
                b = self.ps_get()
                for k in range(8):
                    self.mm(b, ps[:, b, :], self.xb.t[:, k, tb * 128:(tb + 1) * 128], wv[:, k, :], k == 0, k == 7,
                            [(self.ring, s), (self.xb, k)])
                self.tt(DVE, vt.t[:, 1 + tb, :, :].rearrange("p a b -> p (a b)"), ps[:, b, :], self.vec(l, "bvrow"), ALU.add,
                        [(self.psb, b), (self.cvs, None)], [(vt, 1 + tb)])
                self.ps_put(b)
            for qb in range(4):
                for hk in range(2):
                    gblk = it * 4 + qb
                    kbs = [1] if gblk == 0 else [0, 1]
                    sb_ = self.ps_get(2)
                    Sv = ps[:, sb_:sb_ + 2, :].rearrange("p a (g q) -> p a g q", g=4)
                    for kb in kbs:
                        self.mm(sb_ + kb, ps[:, sb_ + kb, :], self.identb.t[:, :],
                                self.bias8.t[:, kb, 4 * hk:4 * hk + 4, :].rearrange("p g q -> p (g q)"), True, False,
                                [(self.identb, None), (self.bias8, None)])
                        for g in range(4):
                            ch = 2 * hk + g // 2
                            hp = hk * 2 + g % 2
                            self.mm(sb_ + kb, Sv[:, kb, g, :], kT.t[:, hp, (qb + kb) * 128:(qb + kb + 1) * 128],
                                    self.qT.t[:, ch, qb * 128:(qb + 1) * 128], False, g == 3, [(kT, hp), (self.qT, ch)])
                    sl = (qb * 2 + hk) % 2
                    for kb in kbs:
                        self.act(self.pT.t[:, sl, kb, :, :].rearrange("p g q -> p (g q)"), ps[:, sb_ + kb, :], AF.Exp,
                                 [(self.psb, sb_ + kb)], [(self.pT, sl)], scale=0.125)
                    self.ps_put(sb_, 2)
                    ob = self.ps_get()
                    for chl in range(2):
                        combos = [(par, kb) for par in range(2) for kb in kbs]
                        for i, (par, kb) in enumerate(combos):
                            self.mm(ob, ps[:, ob, chl * 128:(chl + 1) * 128], vt.t[:, qb + kb, hk * 2 + par, :],
                                    self.pT.t[:, sl, kb, 2 * chl + par, :], i == 0, i == len(combos) - 1,
                                    [(vt, qb + kb), (self.pT, sl)])
                        for i, (par, kb) in enumerate(combos):
                            self.mm(ob, ps[:, ob, 256 + chl * 128:256 + (chl + 1) * 128], self.onesb.t[:, par, :],
                                    self.pT.t[:, sl, kb, 2 * chl + par, :], i == 0, i == len(combos) - 1,
                                    [(self.onesb, None), (self.pT, sl)])
                    for chl in range(2):
                        ch = 2 * hk + chl
                        self.act(self.rec.t[:, chl, :], ps[:, ob, 256 + chl * 128:256 + (chl + 1) * 128], AF.Ln,
                                 [(self.psb, ob), (self.cvs, None)], [(self.rec, None)], bias=self.vec(l, "sink", ch))
                    self.act(self.rec.t[:, :, :], self.rec.t[:, :, :], AF.Exp, [(self.rec, None)], [(self.rec, None)], scale=-1.0)
                    self.tt(DVE, self.attnT.t[:, 2 * hk:2 * hk + 2, qb * 128:(qb + 1) * 128],
                            ps[:, ob, 0:256].rearrange("p (c q) -> p c q", c=2), self.rec.t[:, :, :], ALU.mult,
                            [(self.psb, ob), (self.rec, None)], [(self.attnT, [2 * hk, 2 * hk + 1])])
                    self.ps_put(ob)
            for hp in range(4):
                self.cp(POOL, kT.t[:, hp, 0:128], kT.t[:, hp, T:T + 128], [(kT, hp)], [(kT, hp)])
            self.cp(POOL, vt.t[:, 0, :, :], vt.t[:, 4, :, :], [(vt, 4)], [(vt, 0)])
        if do_rw:
            self.rwkv(l, it)
        else:
            for name in ["XS", "V", "RK0", "RK1", "RK2", "RK3"]:
                self.w_get(name)
        for o in range(8):
            s, w = self.w_get(f"MG{o}")
            if not (do_attn or do_rw):
                continue
            wba = w[:, 0:512].rearrange("p (k m) -> p k m", k=4)
            wbb = w[:, 512:1024].rearrange("p (k m) -> p k m", k=4)
            wga = w[:, 1024:2048].rearrange("p (k m) -> p k m", k=8)
            wgb = w[:, 2048:3072].rearrange("p (k m) -> p k m", k=8)
            parts = []
            if do_attn:
                parts.append((wga, wba, self.attnT, "bga", 0))
            if do_rw:
                parts.append((wgb, wbb, self.rwT, "bgb", 1))
            for i, (wg, wb, src, bn, gi0) in enumerate(parts):
                gi = gi0
                b1 = self.ps_get()
                for k in range(8):
                    self.mm(b1, ps[:, b1, :], wg[:, k, :], self.xb.t[:, k, :], k == 0, k == 7, [(self.ring, s), (self.xb, k)])
                self.act(self.gat.t[:, gi, :], ps[:, b1, :], AF.Sigmoid, [(self.psb, b1), (self.cvs, None)], [(self.gat, gi)],
                         bias=self.vec(l, bn, o))
                self.ps_put(b1)
                b2 = self.ps_get()
                for k in range(4):
                    self.mm(b2, ps[:, b2, :], wb[:, k, :], src.t[:, k, :], k == 0, k == 3, [(self.ring, s), (src, k)])
                last = (i == len(parts) - 1)
                if i == 0:
                    dst = self.merged.t[:, o, :] if last else self.m1.t[:, o % 2, :]
                    self.tt(DVE, dst, self.gat.t[:, gi, :], ps[:, b2, :], ALU.mult, [(self.gat, gi), (self.psb, b2)],
                            [(self.merged, o)] if last else [(self.m1, o % 2)])
                else:
                    self.tt(DVE, self.gat.t[:, gi, :], self.gat.t[:, gi, :], ps[:, b2, :], ALU.mult, [(self.gat, gi), (self.psb, b2)],
                            [(self.gat, gi)])
                    self.tt(DVE, self.merged.t[:, o, :], self.gat.t[:, gi, :], self.m1.t[:, o % 2, :], ALU.add,
                            [(self.gat, gi), (self.m1, o % 2)], [(self.merged, o)])
                self.ps_put(b2)
        self.ln_stats_begin()
        for j in range(2):
            s, w = self.w_get(f"WO{j}")
            wv = w.rearrange("p (c k m) -> p c k m", c=4, k=8)
            for oo in range(4):
                o = 4 * j + oo
                if do_attn or do_rw:
                    b = self.ps_get()
                    for k in range(8):
                        self.mm(b, ps[:, b, :], wv[:, oo, k, :], self.merged.t[:, k, :], k == 0, k == 7,
                                [(self.ring, s), (self.merged, k)])
                    self.stt(self.x.t[:, o, :], ps[:, b, :], 1.0 / ALPHA, self.x.t[:, o, :], ALU.mult, ALU.add,
                             [(self.psb, b), (self.x, o)], [(self.x, o)])
                    self.ps_put(b)
                self.ln_stats_chunk(o)
        self.ln_finish(l, 1)
        self.guard(self.set_ffn)

    def shift_mix(self, l, bank, grp, gi, car, ci, dest, dest_rw):
        ps = self.psum
        sl = self.uc_i % 2
        self.uc_i += 1
        uc = self.ucur
        V = lambda n: self.vec(l, n + "_" + grp, gi)
        self.act(dest, ps[:, bank, :], AF.Identity, [(self.psb, bank), (self.cvs, None)], dest_rw, bias=V("bom"), scale=V("om"))
        self.act(uc.t[:, sl, 1:T + 1], ps[:, bank, :], AF.Identity, [(self.psb, bank), (self.cvs, None)], [(uc, sl)],
                 bias=V("bmu"), scale=V("mu"))
        self.cp(POOL, uc.t[:, sl, 0:1], car.t[:, ci:ci + 1], [(car, None)], [(uc, sl)])
        self.tt(DVE, dest, dest, uc.t[:, sl, 0:T], ALU.add, list(dest_rw) + [(uc, sl)], dest_rw)
        self.cp(POOL, car.t[:, ci:ci + 1], uc.t[:, sl, T:T + 1], [(uc, sl)], [(car, None)])

    def rwkv(self, l, it):
        P = self.P
        ps = self.psum
        V = lambda n, c=None: self.vec(l, n, c)
        car = self.ucar[l]
        tS = self.tS
        so = l * NSMALL
        w2a2 = self.wsmall.t[:, so:so + 512]
        g2w = self.wsmall.t[:, so + 512:so + 1024]
        v1w = self.wsmall.t[:, so + 1024:so + 1152].rearrange("p (c r) -> p c r", c=4)
        v2w = self.wsmall.t[:, so + 1152:so + 1664]
        self.guard(self.set_prep)
        self.uc_i = 0
        s, w = self.w_get("XS")
        wv = w.rearrange("p (c k m) -> p c k m", c=2, k=8)
        for ci in range(2):
            b = self.ps_get()
            for k in range(8):
                self.mm(b, ps[:, b, :], wv[:, ci, k, :], self.xb.t[:, k, :], k == 0, k == 7, [(self.ring, s), (self.xb, k)])
            self.shift_mix(l, b, "xs", ci, car, ci, self.xs32.t[:, ci, :], [(self.xs32, ci)])
            self.ps_put(b)
        self.act(self.txa.t[0:64, :], self.xs32.t[0:64, 0, :], AF.Tanh, [(self.xs32, 0)], [(self.txa, None)])
        self.cp(ACT, self.txa.t[64:128, :], self.xs32.t[64:128, 0, :], [(self.xs32, 0)], [(self.txa, None)])
        self.act(self.sxg.t[:, :], self.xs32.t[:, 1, :], AF.Sigmoid, [(self.xs32, 1)], [(self.sxg, None)])
        s, w = self.w_get("V")
        wv = w.rearrange("p (c k m) -> p c k m", c=4, k=8)
        for fc in range(4):
            b = self.ps_get()
            for k in range(8):
                self.mm(b, ps[:, b, :], wv[:, fc, k, :], self.xb.t[:, k, :], k == 0, k == 7, [(self.ring, s), (self.xb, k)])
            self.shift_mix(l, b, "v", fc, car, 2 + fc, self.v32.t[:, fc, :], [(self.v32, fc)])
            self.ps_put(b)
        if l == 0:
            for fc in range(4):
                self.cp(ACT, self.vfirst.t[:, fc, :], self.v32.t[:, fc, :], [(self.v32, fc)], [(self.vfirst, fc)])
        else:
            for fc in range(4):
                self.cp(ACT if fc % 2 else DVE, self.vb.t[:, fc, :], self.v32.t[:, fc, :], [(self.v32, fc)], [(self.vb, fc)])
            b = self.ps_get()
            for fc in range(4):
                self.mm(b, ps[0:32, b, :], v1w[:, fc, :], self.vb.t[:, fc, :], fc == 0, fc == 3, [(self.wsmall, None), (self.vb, fc)])
            self.cp(ACT, self.lob.t[:, :], ps[0:32, b, :], [(self.psb, b)], [(self.lob, None)])
            self.ps_put(b)
            for fc in range(4):
                b = self.ps_get()
                self.mm(b, ps[:, b, :], v2w[0:32, fc * 128:(fc + 1) * 128], self.lob.t[:, :], True, True, [(self.wsmall, None), (self.lob, None)])
                self.act(tS.t[:, 0, :], ps[:, b, :], AF.Sigmoid, [(self.psb, b), (self.cvs, None)], [(tS, 0)], bias=V("v0", fc))
                self.ps_put(b)
                self.tt(DVE, tS.t[:, 1, :], self.vfirst.t[:, fc, :], self.v32.t[:, fc, :], ALU.subtract,
                        [(self.vfirst, fc), (self.v32, fc)], [(tS, 1)])
                self.tt(DVE, tS.t[:, 1, :], tS.t[:, 1, :], tS.t[:, 0, :], ALU.mult, [(tS, 0), (tS, 1)], [(tS, 1)])
                self.tt(DVE, self.v32.t[:, fc, :], self.v32.t[:, fc, :], tS.t[:, 1, :], ALU.add, [(self.v32, fc), (tS, 1)], [(self.v32, fc)])
        for fc in range(4):
            self.cp(ACT, self.vb.t[:, fc, :], self.v32.t[:, fc, :], [(self.v32, fc)], [(self.vb, fc)])
        for fc in range(4):
            s, w = self.w_get(f"RK{fc}")
            wv = w.rearrange("p (c k m) -> p c k m", c=2, k=8)
            r32, k32, sw, a32, cs, e1, e2, e3 = [tS.t[:, i, :] for i in range(8)]
            R = lambda *i: [(tS, j) for j in i]
            for ci, (grp, cbase, slot) in enumerate((("r", 6, 0), ("k", 10, 1))):
                b = self.ps_get()
                for k in range(8):
                    self.mm(b, ps[:, b, :], wv[:, ci, k, :], self.xb.t[:, k, :], k == 0, k == 7, [(self.ring, s), (self.xb, k)])
                self.shift_mix(l, b, grp, fc, car, cbase + fc, tS.t[:, slot, :], [(tS, slot)])
                self.ps_put(b)
            fsl = slice(fc * 128, (fc + 1) * 128)
            b = self.ps_get()
            self.mm(b, ps[:, b, :], w2a2[0:64, fsl], self.txa.t[0:64, :], True, True, [(self.wsmall, None), (self.txa, None)])
            self.act(sw, ps[:, b, :], AF.Sigmoid, [(self.psb, b), (self.cvs, None)], R(2), bias=V("w0", fc))
            self.ps_put(b)
            b = self.ps_get()
            self.mm(b, ps[:, b, :], w2a2[64:128, fsl], self.txa.t[64:128, :], True, True, [(self.wsmall, None), (self.txa, None)])
            self.act(a32, ps[:, b, :], AF.Sigmoid, [(self.psb, b), (self.cvs, None)], R(3), bias=V("a0", fc))
            self.ps_put(b)
            bgz = self.ps_get()
            self.mm(bgz, ps[:, bgz, :], g2w[:, fsl], self.sxg.t[:, :], True, True, [(self.wsmall, None), (self.sxg, None)])
            self.cp(ACT, self.gF.t[:, fc, :], ps[:, bgz, :], [(self.psb, bgz)], [(self.gF, fc)])
            self.P.add(DVE, lambda e, o=cs, d1=sw: e.tensor_tensor_scan(out=o, data0=self.resetm.t[:, :], data1=d1, initial=0.0,
                                                                        op0=ALU.mult, op1=ALU.add),
                       reads=R(2) + [(self.resetm, None)], writes=R(4), cost=1150.0)
            self.act(e1, cs, AF.Exp, R(4), R(5), scale=-DECAY_C)
            self.act(e2, cs, AF.Exp, R(4), R(6), scale=DECAY_C)
            self.tt(DVE, e3, cs, sw, ALU.subtract, R(4, 2), R(7))
            self.act(e3, e3, AF.Exp, R(7), R(7), scale=-DECAY_C)
            self.cp(POOL, self.gam.t[:, fc, :], e1[:, CH - 1::CH], R(5), [(self.gam, fc)])
            kk = sw
            self.ts(DVE, kk, k32, V("k_k", fc), ALU.mult, R(1) + [(self.cvs, None)], R(2))
            self.act(self.kk2.t[:, :], kk, AF.Square, R(2), [(self.kk2, None)])
            b = self.ps_get()
            self.mm(b, ps[:, b, :], self.bdiagb.t[:, :], self.kk2.t[:, :], True, True, [(self.bdiagb, None), (self.kk2, None)])
            nrm = cs
            self.ts(DVE, nrm, ps[:, b, :], 1e-18, ALU.max, [(self.psb, b)], R(4))
            self.ps_put(b)
            self.act(nrm, nrm, AF.Ln, R(4), R(4))
            self.act(nrm, nrm, AF.Exp, R(4), R(4), scale=-0.5)
            kkn = kk
            self.tt(DVE, kkn, kk, nrm, ALU.mult, R(2, 4), R(2))
            ARv = self.AR.t[:, fc, :, :, :]
            self.stt(ARv[:, :, 0, :], kkn.rearrange("p (c t) -> p c t", c=NCH), -1.0, e3.rearrange("p (c t) -> p c t", c=NCH),
                     ALU.mult, ALU.mult, R(2, 7), [(self.AR, fc)])
            self.tt(DVE, ARv[:, :, 1, :], r32.rearrange("p (c t) -> p c t", c=NCH), e1.rearrange("p (c t) -> p c t", c=NCH),
                    ALU.mult, R(0, 5), [(self.AR, fc)])
            bp = e3
            self.tt(DVE, bp, kkn, a32, ALU.mult, R(2, 3), R(7))
            self.tt(DVE, self.Bt.t[:, fc, :], bp, e2, ALU.mult, R(7, 6), [(self.Bt, fc)])
            t1 = e1
            self.ts(DVE, t1, a32, V("k_a", fc), ALU.mult, R(3) + [(self.cvs, None)], R(5), s2=V("omka", fc), op1=ALU.add)
            k2 = t1
            self.tt(DVE, k2, k32, t1, ALU.mult, R(1, 5), R(5))
            self.tt(DVE, self.Kt.t[:, fc, :], k2, e2, ALU.mult, R(5, 6), [(self.Kt, fc)])
            rk = k32
            self.tt(DVE, rk, r32, k2, ALU.mult, R(0, 5), R(1))
            self.ts(DVE, self.rkr.t[:, :], rk, V("r_k", fc), ALU.mult, R(1) + [(self.cvs, None)], [(self.rkr, None)])
            b = self.ps_get()
            self.mm(b, ps[:, b, :], self.bdiagb.t[:, :], self.rkr.t[:, :], True, True, [(self.bdiagb, None), (self.rkr, None)])
            bon = e2
            self.tt(DVE, bon, ps[:, b, :], self.v32.t[:, fc, :], ALU.mult, [(self.psb, b), (self.v32, fc)], R(6))
            self.ps_put(b)
            self.tt(DVE, self.bg.t[:, fc, :], ps[:, bgz, :], bon, ALU.mult, [(self.psb, bgz)] + R(6), [(self.bg, fc)])
            self.ps_put(bgz)
        self.guard(self.set_scan)
        for sl in range(2):
            self.memset(POOL, self.VTp.t[:, sl, :, :, :], 0.0, [(self.VTp, sl)])
        self.memset(POOL, self.UTp.t[:, 0, :, :, :], 0.0, [(self.UTp, 0)])
        ST32 = self.ST32[l]
        STb = self.STb[l]

        def psbf(b):
            return ps[:, b, :].bitcast(BF16)

        def l_steps(c):
            sl = c % 2
            m3 = c % 3
            tok = slice(c * CH, (c + 1) * CH)
            st = {}
            steps = []

            def s_tr():
                b1 = self.ps_get()
                b2 = self.ps_get()
                for fc in range(4):
                    self.tr(b1, psbf(b1)[0:64, fc * 128:(fc + 1) * 128], self.vb.t[:, fc, tok], [(self.vb, fc)])
                for fc in range(4):
                    self.tr(b1, psbf(b1)[0:64, 512 + fc * 128:512 + (fc + 1) * 128], self.Bt.t[:, fc, tok], [(self.Bt, fc)])
                for fc in range(4):
                    self.tr(b2, psbf(b2)[0:64, fc * 128:(fc + 1) * 128], self.Kt.t[:, fc, tok], [(self.Kt, fc)])
                self.cp(ACT, self.VT.t[:, sl, :], psbf(b1)[0:64, 0:512], [(self.psb, b1)], [(self.VT, sl)])
                for par in range(2):
                    self.cp(DVE, self.VTp.t[:, sl, :, par, par * 64:(par + 1) * 64],
                            psbf(b1)[0:64, 0:512].rearrange("p (f a i) -> p f a i", f=4, a=2)[:, :, par, :],
                            [(self.psb, b1)], [(self.VTp, sl)])
                self.cp(ACT, self.BT.t[:, sl, :], psbf(b1)[0:64, 512:1024], [(self.psb, b1)], [(self.BT, sl)])
                self.cp(DVE, self.KT.t[:, sl, :], psbf(b2)[0:64, 0:512], [(self.psb, b2)], [(self.KT, sl)])
                self.ps_put(b1)
                self.ps_put(b2)
            steps.append(s_tr)

            def s_m():
                for (src, dst) in ((self.Bt, self.NMbr), (self.Kt, self.MakMkr)):
                    banks = [self.ps_get(), self.ps_get()]
                    for fc in range(4):
                        for par in range(2):
                            pb = par * 64
                            self.mm(banks[par], ps[0:64, banks[par], fc * 128:(fc + 1) * 128], src.t[pb:pb + 64, fc, tok],
                                    self.AR.t[pb:pb + 64, fc, c, :, :].rearrange("p a t -> p (a t)"), True, True,
                                    [(src, fc), (self.AR, fc)])
                    for par in range(2):
                        self.tt(DVE, dst.t[:, m3, par * 4:par * 4 + 4, :, :].rearrange("p f a t -> p (f a t)"), ps[0:64, banks[par], :],
                                self.mAB.t[:, 0:4, :, :].rearrange("p f a t -> p (f a t)"), ALU.mult,
                                [(self.psb, banks[par]), (self.mAB, None)], [(dst, m3)])
                    self.ps_put(banks[0])
                    self.ps_put(banks[1])
            steps.append(s_m)

            def s_nt():
                b = self.ps_get()
                for hs in range(8):
                    self.tr(b, psbf(b)[0:64, hs * 64:(hs + 1) * 64], self.NMbr.t[:, m3, hs, 0, :], [(self.NMbr, m3)])
                i0 = sl * 2
                self.cp(ACT, self.ATb.t[:, i0, :, :].rearrange("p h t -> p (h t)"), psbf(b)[0:64, 0:512], [(self.psb, b)], [(self.ATb, i0)])
                self.ps_put(b)
                self.cp(POOL, self.Ab.t[:, i0, :, :], self.NMbr.t[:, m3, :, 0, :], [(self.NMbr, m3)], [(self.Ab, i0)])
                self.tt(DVE, self.Pb.t[:, i0, :, :], self.NMbr.t[:, m3, :, 0, :], self.id64.t[:, :, :], ALU.add,
                        [(self.NMbr, m3), (self.id64, None)], [(self.Pb, i0)])
            steps.append(s_nt)

            def mk_stage(kst):
                def s_sq():
                    cur = sl * 2 + (kst - 1) % 2
                    nxt = sl * 2 + kst % 2
                    last = (kst == 5)
                    bAT = self.ps_get()
                    bA = None if last else self.ps_get()
                    for hs in range(8):
                        a_ = self.Ab.t[:, cur, hs, :]
                        at_ = self.ATb.t[:, cur, hs, :]
                        hsl = slice(hs * 64, (hs + 1) * 64)
                        self.mm(bAT, ps[0:64, bAT, hsl], a_, at_, True, True, [(self.Ab, cur), (self.ATb, cur)])
                        if not last:
                            self.mm(bA, ps[0:64, bA, hsl], at_, a_, True, True, [(self.Ab, cur), (self.ATb, cur)])
                    self.cp(ACT, self.ATb.t[:, nxt, :, :].rearrange("p h t -> p (h t)"), ps[0:64, bAT, :], [(self.psb, bAT)], [(self.ATb, nxt)])
                    self.ps_put(bAT)
                    if not last:
                        self.cp(DVE, self.Ab.t[:, nxt, :, :].rearrange("p h t -> p (h t)"), ps[0:64, bA, :], [(self.psb, bA)], [(self.Ab, nxt)])
                        self.ps_put(bA)

                def s_p():
                    cur = sl * 2 + (kst - 1) % 2
                    nxt = sl * 2 + kst % 2
                    last = (kst == 5)
                    bP = self.ps_get()
                    for hs in range(8):
                        hsl = slice(hs * 64, (hs + 1) * 64)
                        self.mm(bP, ps[0:64, bP, hsl], self.ATb.t[:, nxt, hs, :], self.Pb.t[:, cur, hs, :], True, True,
                                [(self.ATb, nxt), (self.Pb, cur)])
                    pc = self.Pb.t[:, cur, :, :].rearrange("p h t -> p (h t)")
                    if last:
                        self.tt(DVE, self.Tb.t[:, m3, :, :].rearrange("p h t -> p (h t)"), ps[0:64, bP, :], pc, ALU.add,
                                [(self.psb, bP), (self.Pb, cur)], [(self.Tb, m3)])
                    else:
                        self.tt(DVE, self.Pb.t[:, nxt, :, :].rearrange("p h t -> p (h t)"), ps[0:64, bP, :], pc, ALU.add,
                                [(self.psb, bP), (self.Pb, cur)], [(self.Pb, nxt)])
                    self.ps_put(bP)
                return [s_sq, s_p]
            for kst in range(1, 6):
                steps.extend(mk_stage(kst))
            return steps

        def s_phase(c):
            sl = c % 2
            m3 = c % 3
            tok = slice(c * CH, (c + 1) * CH)
            bx = self.ps_get()
            self.memset(DVE, ps[0:64, bx, :], 0.0, [(self.psb, bx)])
            for fc in range(4):
                self.mmx(bx, ps[0:64, bx, fc * 128:(fc + 1) * 128], self.AR.t[:, fc, c, 0, :], STb.t[:, fc, :],
                         [(self.AR, fc), (STb, None)])
            for fc in range(4):
                for par in range(2):
                    hs = par * 4 + fc
                    cs_ = slice(fc * 128 + par * 64, fc * 128 + par * 64 + 64)
                    self.mmx(bx, ps[0:64, bx, cs_], self.MakMkr.t[:, m3, hs, 0, :], self.VT.t[:, sl, cs_],
                             [(self.MakMkr, m3), (self.VT, sl)])
            self.cp(ACT, self.XTb.t[:, 0, :], ps[0:64, bx, :], [(self.psb, bx)], [(self.XTb, 0)])
            self.ps_put(bx)
            bu = self.ps_get()
            for fc in range(4):
                for par in range(2):
                    hs = par * 4 + fc
                    cs_ = slice(fc * 128 + par * 64, fc * 128 + par * 64 + 64)
                    self.mm(bu, ps[0:64, bu, cs_], self.Tb.t[:, m3, hs, :], self.XTb.t[:, 0, cs_], True, True, [(self.Tb, m3), (self.XTb, 0)])
            self.cp(DVE, self.UT.t[:, 0, :], ps[0:64, bu, :], [(self.psb, bu)], [(self.UT, 0)])
            for par in range(2):
                self.cp(ACT if par else DVE, self.UTp.t[:, 0, :, par, par * 64:(par + 1) * 64],
                        ps[0:64, bu, :].rearrange("p (f a i) -> p f a i", f=4, a=2)[:, :, par, :], [(self.psb, bu)], [(self.UTp, 0)])
            self.ps_put(bu)
            by = self.ps_get()
            self.memset(DVE, ps[:, by, 0:256], 0.0, [(self.psb, by)])
            for fc in range(4):
                self.mmx(by, ps[:, by, fc * 64:(fc + 1) * 64], STb.t[:, fc, :], self.AR.t[:, fc, c, 1, :], [(STb, None), (self.AR, fc)])
            for fc in range(4):
                o_ = ps[:, by, fc * 64:(fc + 1) * 64]
                for par in range(2):
                    hs = par * 4 + fc
                    self.mmx(by, o_, self.UTp.t[:, 0, fc, par, :], self.NMbr.t[:, m3, hs, 1, :], [(self.UTp, 0), (self.NMbr, m3)])
                    self.mmx(by, o_, self.VTp.t[:, sl, fc, par, :], self.MakMkr.t[:, m3, hs, 1, :], [(self.VTp, sl), (self.MakMkr, m3)])
            self.cp(ACT, self.yT.t[:, :, tok], ps[:, by, 0:256].rearrange("p (f t) -> p f t", f=4), [(self.psb, by)], [(self.yT, None)])
            self.ps_put(by)
            bs = self.ps_get()
            for fc in range(4):
                fsl = slice(fc * 128, (fc + 1) * 128)
                self.mm(bs, ps[:, bs, fsl], self.BT.t[:, sl, fsl], self.UT.t[:, 0, fsl], True, False, [(self.BT, sl), (self.UT, 0)])
                self.mm(bs, ps[:, bs, fsl], self.KT.t[:, sl, fsl], self.VT.t[:, sl, fsl], False, True, [(self.KT, sl), (self.VT, sl)])
            for par in range(2):
                pb = par * 64
                dv = ST32.t[pb:pb + 64, :, par * 64:(par + 1) * 64]
                self.tt(DVE, dv, ps[pb:pb + 64, bs, :].rearrange("p (f a i) -> p f a i", f=4, a=2)[:, :, par, :], dv, ALU.add,
                        [(self.psb, bs), (ST32, None)], [(ST32, None)])
            self.ps_put(bs)
            for fc in range(4):
                self.act(ST32.t[:, fc, :], ST32.t[:, fc, :], AF.Identity, [(ST32, None), (self.gam, fc)], [(ST32, None)],
                         scale=self.gam.t[:, fc, c:c + 1])
            self.cp(ACT, STb.t[:, :, :], ST32.t[:, :, :], [(ST32, None)], [(STb, None)])

        for c0 in range(0, NCH, 2):
            sa = l_steps(c0)
            sb2 = l_steps(c0 + 1)
            for fa, fb in zip(sa, sb2):
                fa()
                fb()
            s_phase(c0)
            s_phase(c0 + 1)
        self.guard(self.set_post)
        for fc in range(4):
            yv = self.yT.t[:, fc, :]
            self.act(self.gsq.t[:, fc, :], yv, AF.Square, [(self.yT, fc)], [(self.gsq, fc)])
            b1 = self.ps_get()
            b2 = self.ps_get()
            self.mm(b1, ps[:, b1, :], self.bdiagb.t[:, :], yv, True, True, [(self.bdiagb, None), (self.yT, fc)])
            self.mm(b2, ps[:, b2, :], self.bdiagb.t[:, :], self.gsq.t[:, fc, :], True, True, [(self.bdiagb, None), (self.gsq, fc)])
            self.act(self.gmean.t[:, fc, :], ps[:, b1, :], AF.Identity, [(self.psb, b1)], [(self.gmean, fc)], scale=1.0 / 64)
            self.act(self.gtmp.t[:, fc, :], ps[:, b1, :], AF.Square, [(self.psb, b1)], [(self.gtmp, fc)], scale=1.0 / 64)
            self.stt(self.gvar.t[:, fc, :], ps[:, b2, :], 1.0 / 64, self.gtmp.t[:, fc, :], ALU.mult, ALU.subtract,
                     [(self.psb, b2), (self.gtmp, fc)], [(self.gvar, fc)])
            self.ps_put(b1)
            self.ps_put(b2)
            self.act(self.gvar.t[:, fc, :], self.gvar.t[:, fc, :], AF.Ln, [(self.gvar, fc), (self.epsv, None)], [(self.gvar, fc)], bias=self.epsv.t[:, 1:2])
            self.act(self.gvar.t[:, fc, :], self.gvar.t[:, fc, :], AF.Exp, [(self.gvar, fc)], [(self.gvar, fc)], scale=-0.5)
            self.tt(DVE, self.gtmp.t[:, fc, :], yv, self.gmean.t[:, fc, :], ALU.subtract, [(self.yT, fc), (self.gmean, fc)], [(self.gtmp, fc)])
            self.tt(DVE, self.gtmp.t[:, fc, :], self.gtmp.t[:, fc, :], self.gvar.t[:, fc, :], ALU.mult, [(self.gtmp, fc), (self.gvar, fc)], [(self.gtmp, fc)])
            self.ts(DVE, self.gtmp.t[:, fc, :], self.gtmp.t[:, fc, :], V("gn_g", fc), ALU.mult, [(self.gtmp, fc), (self.cvs, None)], [(self.gtmp, fc)],
                    s2=V("gn_b", fc), op1=ALU.add)
            self.tt(DVE, self.gtmp.t[:, fc, :], self.gtmp.t[:, fc, :], self.gF.t[:, fc, :], ALU.mult, [(self.gtmp, fc), (self.gF, fc)], [(self.gtmp, fc)])
            self.tt(DVE, self.rwT.t[:, fc, :], self.gtmp.t[:, fc, :], self.bg.t[:, fc, :], ALU.add, [(self.gtmp, fc), (self.bg, fc)], [(self.rwT, fc)])


_CACHE = {}


def _run(inp, NT, **kw):
    key = (NT, tuple(sorted(kw.items())))
    b = Builder(NT, **kw)
    stats = b.build()
    return b, stats


def kernel(**inputs):
    inp = {k: np.asarray(v) for k, v in inputs.items()}
    x = inp["x"].astype(np.float32, copy=False)
    B = x.shape[0]
    NT = SEQ // T
    b, stats = _run(inp, NT)
    wf = pack_weights(inp)
    wsm = pack_small(inp)
    cv = pack_vecs(inp)
    gc = pack_gconst(inp)
    in_maps = []
    for c in range(B):
        in_maps.append({"xT": np.ascontiguousarray(x[c].T), "wf": wf, "wsm": wsm, "cv": cv, "gc": gc})
    res = run_bass_kernel_spmd(b.nc, in_maps, core_ids=list(range(B)))
    out = np.stack([np.ascontiguousarray(r["outT"].T) for r in res.results], axis=0)
    return out.astype(np.float32, copy=False)
```

```python
import math
import numpy as np
import concourse.bass as bass
import concourse.mybir as mybir
from concourse.bass_utils import run_bass_kernel_spmd

F32 = mybir.dt.float32
BF16 = mybir.dt.bfloat16
AF = mybir.ActivationFunctionType
ALU = mybir.AluOpType

PE, ACT, DVE, POOL, SP = "pe", "act", "dve", "pool", "sp"
ENGS = [PE, ACT, DVE, POOL, SP]
SEM_WRAP = 30000

D = 1024
SEQ = 4096
DEPTH = 2
DFF = 2816
NHC = DFF // 128
PROJ = 4608
OFF_Q = 2048
OFF_K = 2560
OFF_V = 2688
OFF_RW = 2816
ALPHA = (2 * DEPTH) ** 0.25
LN_EPS = 1e-5
GN_EPS = 64e-5
T = 512
CH = 64
NCH = T // CH
DECAY_C = math.exp(-0.5)
NSLOT = 4
SLOT = 4096


class Buf:
    def __init__(self, t, n=1, name=""):
        self.t = t
        self.n = n
        self.name = name
        self.lastw = [None] * n
        self.readers = [[] for _ in range(n)]
        self.exclusive = False
        self.extra = [[] for _ in range(n)]


class Op:
    __slots__ = ("eng", "fn", "deps", "signaled", "tok", "chan", "idx", "alldeps", "cost", "lat", "prio", "succ", "indeg", "est")

    def __init__(self, eng, fn, chan=None):
        self.eng = eng
        self.fn = fn
        self.deps = []
        self.signaled = False
        self.tok = None
        self.chan = chan


def _slots(buf, s):
    if s is None:
        return range(buf.n)
    if isinstance(s, int):
        return (s,)
    return s


class Prog:
    def __init__(self, nc, same_engine_sync=True):
        self.nc = nc
        self.ops = {e: [] for e in ENGS}
        self.same_engine_sync = same_engine_sync
        self.nchan = 0
        self.bufs = []
        self.pending = {e: [] for e in ENGS}
        self.last_dma = {}
        self.nops = 0

    def buf(self, t, n=1, name=""):
        b = Buf(t, n, name)
        self.bufs.append(b)
        return b

    def new_chan(self):
        c = self.nchan
        self.nchan += 1
        return c

    def guard(self, old_bufs, new_bufs):
        accs = {}
        for b in old_bufs:
            for i in range(b.n):
                for o in [b.lastw[i]] + b.readers[i]:
                    if o is not None:
                        accs[id(o)] = o
        join = self.add(SP, lambda e: e.nop(), cost=30.0, xdeps=list(accs.values()))
        join.signaled = True
        for b in new_bufs:
            for i in range(b.n):
                b.extra[i] = [join]
                b.lastw[i] = None
                b.readers[i] = []

    def schedule(self):
        import heapq
        allops = []
        for e in ENGS:
            allops.extend(self.ops[e])
        allops.sort(key=lambda o: o.idx)
        for o in allops:
            o.succ = []
            o.indeg = len(o.alldeps)
            o.est = 0.0
        for o in allops:
            for d in o.alldeps:
                d.succ.append(o)
        XL = 400.0
        for o in reversed(allops):
            p = 0.0
            for s_ in o.succ:
                if s_.prio > p:
                    p = s_.prio
            o.prio = p + o.lat + XL
        pend = {e: [] for e in ENGS}
        avail = {e: [] for e in ENGS}
        free = {e: 0.0 for e in ENGS}
        for o in allops:
            if o.indeg == 0:
                heapq.heappush(pend[o.eng], (0.0, o.idx, o))
        order = {e: [] for e in ENGS}
        n_done = 0
        total = len(allops)
        while n_done < total:
            best = None
            for e in ENGS:
                pe_, av = pend[e], avail[e]
                while pe_ and pe_[0][0] <= free[e]:
                    _, _, o = heapq.heappop(pe_)
                    heapq.heappush(av, (-o.prio, o.idx, o))
                if av:
                    t = free[e]
                elif pe_:
                    t = pe_[0][0]
                else:
                    continue
                if best is None or t < best[0]:
                    best = (t, e)
            t, e = best
            if avail[e]:
                _, _, o = heapq.heappop(avail[e])
            else:
                _, _, o = heapq.heappop(pend[e])
            start = max(free[e], o.est)
            free[e] = start + o.cost
            fin = start + o.lat
            order[e].append(o)
            n_done += 1
            for s_ in o.succ:
                v = fin + (XL if s_.eng != e else 60.0)
                if v > s_.est:
                    s_.est = v
                s_.indeg -= 1
                if s_.indeg == 0:
                    heapq.heappush(pend[s_.eng], (s_.est, s_.idx, s_))
        self.ops = order
        return max(free.values())

    def barrier(self):
        lasts = [self.ops[e][-1] for e in ENGS if self.ops[e]]
        lasts += list(self.last_dma.values())
        for o in lasts:
            o.signaled = True
        for e in ENGS:
            self.pending[e] = list(lasts)

    def add(self, eng, fn, reads=(), writes=(), chan=None, cost=100.0, lat=None, xdeps=()):
        op = Op(eng, fn, chan)
        op.idx = self.nops
        self.nops += 1
        op.cost = cost
        op.lat = cost if lat is None else lat
        deps = {}
        for o in xdeps:
            deps[id(o)] = o
        for buf, s in list(reads) + list(writes):
            for i in _slots(buf, s):
                if buf.extra[i]:
                    for o in buf.extra[i]:
                        deps[id(o)] = o
                    buf.extra[i] = []
        for buf, s in reads:
            for i in _slots(buf, s):
                o = buf.lastw[i]
                if o is not None:
                    deps[id(o)] = o
                if buf.exclusive:
                    for r in buf.readers[i]:
                        if r.eng != eng:
                            deps[id(r)] = r
        for buf, s in writes:
            for i in _slots(buf, s):
                o = buf.lastw[i]
                if o is not None:
                    deps[id(o)] = o
                for r in buf.readers[i]:
                    deps[id(r)] = r
        for o in self.pending[eng]:
            deps[id(o)] = o
        self.pending[eng] = []
        op.alldeps = list(deps.values())
        for o in deps.values():
            if o.eng == eng and o.chan is None and chan is None:
                if eng == PE or not self.same_engine_sync:
                    continue
            op.deps.append(o)
            o.signaled = True
        for buf, s in reads:
            for i in _slots(buf, s):
                buf.readers[i].append(op)
        for buf, s in writes:
            for i in _slots(buf, s):
                buf.lastw[i] = op
                buf.readers[i] = []
        if chan is not None:
            op.signaled = True
            self.last_dma[chan] = op
        self.ops[eng].append(op)
        return op

    def emit(self, final_waits=()):
        nc = self.nc
        nsem = {}
        for e in ENGS:
            cnt = 0
            for op in self.ops[e]:
                if op.chan is None and op.signaled:
                    k = cnt // SEM_WRAP
                    op.tok = ((e, k), cnt - k * SEM_WRAP + 1)
                    cnt += 1
            nsem[e] = cnt // SEM_WRAP + 1
        chan_cnt = {}
        for e in ENGS:
            for op in self.ops[e]:
                if op.chan is not None:
                    chan_cnt[op.chan] = chan_cnt.get(op.chan, 0) + 1
                    op.tok = (("chan", op.chan), 16 * chan_cnt[op.chan])
        semh = {}
        for e in ENGS:
            for k in range(nsem[e]):
                semh[(e, k)] = nc.alloc_semaphore(name=f"s_{e}_{k}")
        for c in range(self.nchan):
            semh[("chan", c)] = nc.alloc_semaphore(name=f"s_ch{c}")
        stats = {}

        def run(e, eng):
            known = {}
            nw = 0
            for op in self.ops[e]:
                need = {}
                for d in op.deps:
                    sk, v = d.tok
                    if known.get(sk, 0) >= v:
                        continue
                    if need.get(sk, 0) < v:
                        need[sk] = v
                for sk, v in need.items():
                    eng.wait_ge(semh[sk], v)
                    known[sk] = v
                    nw += 1
                ins = op.fn(eng)
                if op.chan is not None:
                    ins.then_inc(semh[op.tok[0]], 16)
                elif op.signaled:
                    ins.then_inc(semh[op.tok[0]], 1)
            if e == SP:
                for op in final_waits:
                    sk, v = op.tok
                    if known.get(sk, 0) < v:
                        eng.wait_ge(semh[sk], v)
                        known[sk] = v
            stats[e] = (len(self.ops[e]), nw)

        with nc.Block() as block:
            @block.tensor
            def _(eng):
                run(PE, eng)

            @block.scalar
            def _(eng):
                run(ACT, eng)

            @block.vector
            def _(eng):
                run(DVE, eng)

            @block.gpsimd
            def _(eng):
                run(POOL, eng)

            @block.sync
            def _(eng):
                run(SP, eng)
        return stats


def weight_blocks():
    bl = []
    for f in range(2):
        if f == 1:
            bl.append(("Q", 4096))
            bl.append(("KP", 4096))
            bl.append(("VP", 4096))
            bl.append(("XS", 2048))
            bl.append(("V", 4096))
            for fc in range(4):
                bl.append((f"RK{fc}", 2048))
            for o in range(8):
                bl.append((f"MG{o}", 3072))
            for j in range(2):
                bl.append((f"WO{j}", 4096))
        for j in range(NHC // 2):
            bl.append((f"GU{f}_{j}", 4096))
        for o in range(8):
            bl.append((f"WD{f}_{o}", 2816))
    return bl


LAYER_BLOCKS = weight_blocks()
LAYER_W = sum(n for _, n in LAYER_BLOCKS)
TOTW = LAYER_W * DEPTH
NSMALL = 512 + 512 + 128 + 512


def pack_weights(inp):
    wf = np.empty((128, TOTW), np.float32)
    off = 0
    for l in range(DEPTH):
        w_in = inp["w_in"][l]

        def cols(c0, n=128):
            return w_in[:, c0:c0 + n].reshape(8, 128, n).transpose(1, 0, 2)

        for name, n in LAYER_BLOCKS:
            if name.startswith("GU"):
                f, j = int(name[2]), int(name[4:])
                wgu = inp["ffn_w_gu"][l, f]
                blk = np.empty((128, 2, 8, 256), np.float32)
                for cc in range(2):
                    c = 2 * j + cc
                    blk[:, cc, :, 0:128] = wgu[:, c * 128:(c + 1) * 128].reshape(8, 128, 128).transpose(1, 0, 2)
                    blk[:, cc, :, 128:256] = wgu[:, DFF + c * 128:DFF + (c + 1) * 128].reshape(8, 128, 128).transpose(1, 0, 2)
            elif name.startswith("WD"):
                f, o = int(name[2]), int(name[4:])
                wd = inp["ffn_w_down"][l, f]
                blk = wd[:, o * 128:(o + 1) * 128].reshape(NHC, 128, 128).transpose(1, 0, 2)
            elif name == "Q":
                blk = np.stack([cols(OFF_Q + ch * 128) for ch in range(4)], axis=1)
            elif name == "KP":
                parts = []
                z64 = np.zeros((128, 8, 64), np.float32)
                for hk in range(2):
                    kc = cols(OFF_K + hk * 64, 64)
                    parts.append(np.concatenate([kc, z64], axis=2))
                    parts.append(np.concatenate([z64, kc], axis=2))
                blk = np.stack(parts, axis=1)
            elif name == "VP":
                z64 = np.zeros((128, 8, 64), np.float32)
                parts = []
                for hk in range(2):
                    vc = cols(OFF_V + hk * 64, 64)
                    parts.append(np.concatenate([vc, z64], axis=2))
                    parts.append(np.concatenate([z64, vc], axis=2))
                blk = np.concatenate(parts, axis=2)
            elif name == "XS":
                blk = np.stack([cols(OFF_RW + 1536), cols(OFF_RW + 1664)], axis=1)
            elif name == "V":
                blk = np.stack([cols(OFF_RW + 1024 + fc * 128) for fc in range(4)], axis=1)
            elif name.startswith("RK"):
                fc = int(name[2:])
                blk = np.stack([cols(OFF_RW + fc * 128), cols(OFF_RW + 512 + fc * 128)], axis=1)
            elif name.startswith("MG"):
                o = int(name[2:])
                wba = inp["w_branch_attn"][l][:, o * 128:(o + 1) * 128].reshape(4, 128, 128).transpose(1, 0, 2)
                wbb = inp["w_branch_rwkv"][l][:, o * 128:(o + 1) * 128].reshape(4, 128, 128).transpose(1, 0, 2)
                blk = np.concatenate([wba.reshape(128, -1), wbb.reshape(128, -1),
                                      cols(o * 128).reshape(128, -1), cols(D + o * 128).reshape(128, -1)], axis=1)
            elif name.startswith("WO"):
                j = int(name[2:])
                wo = inp["w_out"][l]
                blk = np.stack([wo[:, o * 128:(o + 1) * 128].reshape(8, 128, 128).transpose(1, 0, 2)
                                for o in range(4 * j, 4 * j + 4)], axis=1)
            wf[:, off:off + n] = blk.reshape(128, n)
            off += n
    assert off == TOTW
    return wf


def pack_small(inp):
    ws = np.zeros((128, DEPTH * NSMALL), np.float32)
    for l in range(DEPTH):
        o = l * NSMALL
        ws[0:64, o:o + 512] = inp["rw_w2"][l]
        ws[64:128, o:o + 512] = inp["rw_a2"][l]
        ws[:, o + 512:o + 1024] = inp["rw_g2"][l]
        if l > 0:
            ws[:, o + 1024:o + 1152] = inp["rw_v1"][l - 1].reshape(4, 128, 32).transpose(1, 0, 2).reshape(128, 128)
            ws[0:32, o + 1152:o + 1664] = inp["rw_v2"][l - 1]
    return ws


def vec_table():
    tab = {}
    off = 0

    def add(n, k):
        nonlocal off
        tab[n] = (off, k)
        off += k

    for i in range(3):
        add(f"ln_g{i}", 8)
        add(f"ln_b{i}", 8)
    add("bq", 4)
    add("bk", 4)
    add("b_xs", 2)
    add("b_v", 4)
    add("b_r", 4)
    add("b_k", 4)
    for g_, k_ in (("xs", 2), ("v", 4), ("r", 4), ("k", 4)):
        add("om_" + g_, k_)
        add("bom_" + g_, k_)
        add("bmu_" + g_, k_)
    add("bga", 8)
    add("bgb", 8)
    add("mu_xs", 2)
    add("mu_v", 4)
    add("mu_r", 4)
    add("mu_k", 4)
    for n in ("w0", "a0", "k_k", "k_a", "omka", "r_k", "gn_g", "gn_b", "v0", "sink"):
        add(n, 4)
    add("bvrow", 512)
    return tab, off


VTAB, NVEC = vec_table()


def fm(v):
    return np.ascontiguousarray(v.reshape(-1, 128).T)


def pack_vecs(inp):
    cv = np.zeros((128, DEPTH * NVEC), np.float32)
    for l in range(DEPTH):
        def put(n, a):
            o, k = VTAB[n]
            cv[:, l * NVEC + o:l * NVEC + o + k] = a

        b_in = inp["b_in"][l]
        mu = inp["shift_mu"][l]
        for i in range(3):
            put(f"ln_g{i}", fm(inp["ln_g"][l, i]))
            put(f"ln_b{i}", fm(inp["ln_b"][l, i]))
        put("bq", fm(b_in[OFF_Q:OFF_K]))
        bk = b_in[OFF_K:OFF_V]
        z64 = np.zeros(64, np.float32)
        put("bk", np.stack([np.concatenate([bk[0:64], z64]), np.concatenate([z64, bk[0:64]]),
                            np.concatenate([bk[64:128], z64]), np.concatenate([z64, bk[64:128]])], axis=1))
        brw = b_in[OFF_RW:]
        put("b_xs", fm(brw[1536:1792]))
        put("b_v", fm(brw[1024:1536]))
        put("b_r", fm(brw[0:512]))
        put("b_k", fm(brw[512:1024]))
        put("bga", fm(b_in[0:D]))
        put("bgb", fm(b_in[D:2 * D]))
        put("mu_xs", fm(mu[1536:1792]))
        put("mu_v", fm(mu[1024:1536]))
        put("mu_r", fm(mu[0:512]))
        put("mu_k", fm(mu[512:1024]))
        put("w0", fm(inp["rw_w0"][l]))
        put("a0", fm(inp["rw_a0"][l]))
        put("k_k", fm(inp["rw_k_k"][l]))
        put("k_a", fm(inp["rw_k_a"][l]))
        put("r_k", fm(inp["rw_r_k"][l].reshape(-1)))
        put("gn_g", fm(inp["rw_gn_g"][l]))
        put("gn_b", fm(inp["rw_gn_b"][l]))
        if l > 0:
            put("v0", fm(inp["rw_v0"][l - 1]))
        sk = inp["attn_sinks"][l]
        put("sink", np.stack([np.repeat(sk[2 * ch:2 * ch + 2], 64) for ch in range(4)], axis=1))
        bv = b_in[OFF_V:OFF_RW]
        bvp = np.concatenate([bv[0:64], z64, z64, bv[0:64], bv[64:128], z64, z64, bv[64:128]])
        put("bvrow", np.broadcast_to(bvp[None, :], (128, 512)))
    return cv


def t5_bucket_np(n):
    max_exact = 16
    nf = np.maximum(n, 1).astype(np.float32)
    large = max_exact + (np.log(nf / max_exact) / math.log(128 / max_exact) * (32 - max_exact)).astype(np.int32)
    large = np.minimum(large, 31)
    return np.where(n < max_exact, n, large)


GC = {}
_o = 0
for _n, _k in (("biasg", 2048), ("mask8", 2048), ("ident", 128), ("bdiag", 128), ("reset", 512),
               ("mAB", 1024), ("mT", 512), ("id64", 512)):
    GC[_n] = (_o, _k)
    _o += _k
NGC = _o


def pack_gconst(inp):
    g = np.zeros((128, NGC), np.float32)

    def put(n, a):
        o, k = GC[n]
        g[:a.shape[0], o:o + k] = a.reshape(a.shape[0], k)

    rel = inp["rel_bias"]
    bucket = t5_bucket_np(np.arange(128, dtype=np.int32))
    db = rel[bucket]
    kk = np.arange(128)[:, None]
    qq = np.arange(128)[None, :]
    dprev = np.clip(128 + qq - kk, 0, 127)
    dcur = np.clip(qq - kk, 0, 127)
    bg = np.empty((128, 2, 8, 128), np.float32)
    bg[:, 0] = db[dprev].transpose(0, 2, 1)
    bg[:, 1] = db[dcur].transpose(0, 2, 1)
    m8 = np.empty((128, 2, 8, 128), np.float32)
    m8[:, 0] = np.where(kk > qq, 0.0, -240000.0)[:, None, :]
    m8[:, 1] = np.where(kk <= qq, 0.0, -240000.0)[:, None, :]
    put("biasg", bg)
    put("mask8", m8)
    put("ident", np.eye(128, dtype=np.float32))
    bd = np.zeros((128, 128), np.float32)
    bd[0:64, 0:64] = 1.0
    bd[64:128, 64:128] = 1.0
    put("bdiag", bd)
    rs = np.ones((128, T), np.float32)
    rs[:, ::CH] = 0.0
    put("reset", rs)
    s = np.arange(64)[:, None]
    t = np.arange(64)[None, :]
    mAB = np.empty((64, 8, 2, 64), np.float32)
    mAB[:, :, 0, :] = (s < t).astype(np.float32)[:, None, :]
    mAB[:, :, 1, :] = (s <= t).astype(np.float32)[:, None, :]
    put("mAB", mAB)
    mT = np.broadcast_to((s > t).astype(np.float32)[:, None, :], (64, 8, 64))
    put("mT", np.ascontiguousarray(mT))
    put("id64", np.ascontiguousarray(np.broadcast_to(np.eye(64, dtype=np.float32)[:, None, :], (64, 8, 64))))
    return g


class Builder:
    def __init__(self, NT, stages=("ffn", "attn", "rwkv"), dbg=None):
        self.NT = NT
        self.stages = stages
        self.dbg = dbg
        nc = self.nc = bass.Bass("TRN2", target_bir_lowering=False)
        self.P = Prog(nc)
        S = NT * T
        self.xT = nc.dram_tensor("xT", [D, S], F32, kind="ExternalInput").ap()
        self.wf = nc.dram_tensor("wf", [128, TOTW], F32, kind="ExternalInput").ap()
        self.wsm = nc.dram_tensor("wsm", [128, DEPTH * NSMALL], F32, kind="ExternalInput").ap()
        self.cv = nc.dram_tensor("cv", [128, DEPTH * NVEC], F32, kind="ExternalInput").ap()
        self.gc = nc.dram_tensor("gc", [128, NGC], F32, kind="ExternalInput").ap()
        self.outT = nc.dram_tensor("outT", [D, S], F32, kind="ExternalOutput").ap()
        self.wscr = nc.dram_tensor("wscr", [128, TOTW], BF16, kind="Internal").ap()
        self.sb_off = 16512
        self.sb_top = 229344
        self.n_alloc = 0
        self.all_sb = []
        self.psfree = list(range(8))

    def sb(self, shape, dtype, nslots=1, name=None, at=None):
        nb = int(np.prod(shape[1:])) * (4 if dtype == F32 else 2)
        nb = (nb + 31) // 32 * 32
        if at is None:
            off = self.sb_off
            self.sb_off += nb
            assert self.sb_off <= self.sb_top, f"SBUF overflow {self.sb_off}"
        else:
            off = at
        self.n_alloc += 1
        t = self.nc.alloc_sbuf_tensor_at(name or f"t{self.n_alloc}", list(shape), dtype, offset=off)
        b = self.P.buf(t, nslots, name or "")
        b.off = off
        b.size = nb
        self.all_sb.append(b)
        return b

    def guard(self, new_bufs):
        old = []
        for b in self.all_sb:
            if b.off < self.arena0:
                continue
            for nb in new_bufs:
                if b.off < nb.off + nb.size and nb.off < b.off + b.size:
                    old.append(b)
                    break
        self.P.guard(old, new_bufs)

    def ps_get(self, n=1):
        if n == 1:
            return self.psfree.pop(0)
        for i, b in enumerate(self.psfree):
            if b % 2 == 0 and (b + 1) in self.psfree:
                self.psfree.remove(b)
                self.psfree.remove(b + 1)
                return b
        raise RuntimeError("no psum pair")

    def ps_put(self, b, n=1):
        for i in range(n):
            self.psfree.append(b + i)

    @staticmethod
    def _fs(ap):
        n = 1
        for d in ap.shape[1:]:
            n *= int(d)
        return n

    def _ec(self, eng, out, mult=1.0):
        n = self._fs(out) * mult
        if eng == POOL:
            return 150.0 + 2.4 * n
        return 80.0 + n / 0.96

    def mm(self, bank, out, lhsT, rhs, start, stop, reads):
        banks = bank if isinstance(bank, (list, tuple)) else [bank]
        c = max(self._fs(rhs), 64) / 2.4 * (4.0 if rhs.dtype == F32 else 1.0) + 8.0
        return self.P.add(PE, lambda e: e.matmul(out, lhsT=lhsT, rhs=rhs, start=start, stop=stop),
                          reads=reads, writes=[(self.psb, list(banks))], cost=c)

    def mmx(self, bank, out, lhsT, rhs, reads):
        c = max(self._fs(rhs), 64) / 2.4 + 8.0
        return self.P.add(PE, lambda e: e.matmul(out, lhsT=lhsT, rhs=rhs, start=False, stop=True, skip_group_check=True),
                          reads=reads, writes=[(self.psb, [bank])], cost=c)

    def tr(self, bank, out, in_, reads):
        ident = self.identb.t[0:in_.shape[0], 0:in_.shape[0]]
        return self.P.add(PE, lambda e: e.transpose(out, in_, ident),
                          reads=list(reads) + [(self.identb, None)], writes=[(self.psb, [bank])], cost=60.0)

    def act(self, out, in_, func, reads, writes, bias=None, scale=None):
        kw = {}
        if bias is not None:
            kw["bias"] = bias
        if scale is not None:
            kw["scale"] = scale
        return self.P.add(ACT, lambda e: e.activation(out=out, in_=in_, func=func, **kw), reads=reads, writes=writes,
                          cost=self._ec(ACT, out) + 60.0)

    def tt(self, eng, out, in0, in1, op, reads, writes):
        return self.P.add(eng, lambda e: e.tensor_tensor(out=out, in0=in0, in1=in1, op=op), reads=reads, writes=writes,
                          cost=self._ec(eng, out, 1.25))

    def ts(self, eng, out, in0, s1, op0, reads, writes, s2=None, op1=None):
        if op1 is None:
            if eng == POOL:
                return self.P.add(eng, lambda e: e.tensor_scalar(out=out, in0=in0, scalar1=s1, scalar2=0.0, op0=op0, op1=ALU.add),
                                  reads=reads, writes=writes, cost=self._ec(eng, out))
            return self.P.add(eng, lambda e: e.tensor_scalar(out=out, in0=in0, scalar1=s1, scalar2=None, op0=op0),
                              reads=reads, writes=writes, cost=self._ec(eng, out))
        return self.P.add(eng, lambda e: e.tensor_scalar(out=out, in0=in0, scalar1=s1, scalar2=s2, op0=op0, op1=op1),
                          reads=reads, writes=writes, cost=self._ec(eng, out))

    def stt(self, out, in0, scalar, in1, op0, op1, reads, writes):
        return self.P.add(DVE, lambda e: e.scalar_tensor_tensor(out=out, in0=in0, scalar=scalar, in1=in1, op0=op0, op1=op1),
                          reads=reads, writes=writes, cost=self._ec(DVE, out, 1.25))

    def cp(self, eng, out, in_, reads, writes):
        if eng == ACT:
            return self.P.add(ACT, lambda e: e.copy(out=out, in_=in_), reads=reads, writes=writes, cost=self._ec(ACT, out))
        m = 1.5 if (eng == POOL and out.dtype != in_.dtype) else 1.0
        return self.P.add(eng, lambda e: e.tensor_copy(out=out, in_=in_), reads=reads, writes=writes, cost=self._ec(eng, out, m))

    def memset(self, eng, ap, val, writes):
        return self.P.add(eng, lambda e: e.memset(ap, val), writes=writes, cost=self._ec(eng, ap))

    def dma(self, eng, out, in_, chan, reads=(), writes=(), xdeps=()):
        nb = int(np.prod([int(d) for d in out.shape])) * (4 if out.dtype == F32 else 2)
        return self.P.add(eng, lambda e: e.dma_start(out=out, in_=in_), reads=reads, writes=writes, chan=chan,
                          cost=(700.0 if eng == POOL else 60.0), lat=2500.0 + nb / 150.0, xdeps=xdeps)

    def w_init(self):
        self.ring = self.sb([128, NSLOT, SLOT], BF16, NSLOT, "ring")
        self.ring_ch = [self.P.new_chan() for _ in range(NSLOT)]
        self.wseq = []
        off = 0
        for l in range(DEPTH):
            for name, n in LAYER_BLOCKS:
                self.wseq.append((l, name, off, n))
                off += n
        self.nblk = len(self.wseq)
        self.castd = self.P.buf(None, self.nblk, "castd")
        self.NCC = 16
        self.cast_ch = [self.P.new_chan() for _ in range(self.NCC)]
        self.cast_ops = {}
        self.cast_issued = 0
        self.load_ops = {}
        self.CASTW = 5
        self.w_issued = 0
        self.w_cur = 0

    def w_cast(self, upto):
        while self.cast_issued < min(self.nblk, upto):
            i = self.cast_issued
            l, name, o, n = self.wseq[i]
            xd = [self.load_ops[i - self.CASTW]] if (i - self.CASTW) in self.load_ops else []
            if (i - self.NCC) in self.cast_ops:
                xd.append(self.cast_ops[i - self.NCC])
            self.cast_ops[i] = self.dma(POOL, self.wscr[:, o:o + n], self.wf[:, o:o + n], self.cast_ch[i % self.NCC],
                                        writes=[(self.castd, i)], xdeps=xd)
            self.cast_issued += 1

    def w_get(self, expect):
        total = self.nblk * self.NT
        while self.w_issued < min(total, self.w_cur + NSLOT):
            i = self.w_issued
            l, name, o, n = self.wseq[i % self.nblk]
            s = i % NSLOT
            reads = []
            if i < self.nblk:
                self.w_cast(i + self.CASTW)
                reads = [(self.castd, i)]
            op = self.dma(SP, self.ring.t[:, s, 0:n], self.wscr[:, o:o + n], self.ring_ch[s], reads=reads,
                          writes=[(self.ring, s)])
            if i < self.nblk:
                self.load_ops[i] = op
            self.w_issued += 1
        i = self.w_cur
        l, name, o, n = self.wseq[i % self.nblk]
        assert name == expect, (name, expect)
        self.w_cur += 1
        s = i % NSLOT
        return s, self.ring.t[:, s, 0:n]

    def vec(self, l, name, c=None):
        o, k = VTAB[name]
        if c is None:
            return self.cvs.t[:, l * NVEC + o:l * NVEC + o + k]
        return self.cvs.t[:, l * NVEC + o + c:l * NVEC + o + c + 1]

    def setup(self):
        P = self.P
        self.psum = self.nc.alloc_psum_tensor("ps", [128, 8, 512], F32)
        self.psb = P.buf(self.psum, 8, "psum")
        self.psb.exclusive = True
        self.x = self.sb([128, 8, T], F32, 8, "x")
        self.xb = self.sb([128, 8, T], BF16, 8, "xb")
        self.cvs = self.sb([128, DEPTH * NVEC], F32, 1, "cvs")
        self.wsmall = self.sb([128, DEPTH * NSMALL], BF16, 1, "wsmall")
        self.identb = self.sb([128, 128], BF16, 1, "identb")
        self.bdiagb = self.sb([128, 128], BF16, 1, "bdiagb")
        self.ones32 = self.sb([128, 128], F32, 1, "ones32")
        self.ones32r = self.sb([128, 128], F32, 1, "ones32r")
        self.epsv = self.sb([128, 2], F32, 1, "epsv")
        self.onesb = self.sb([128, 2, 128], BF16, 1, "onesb")
        self.bias8 = self.sb([128, 2, 8, 128], BF16, 1, "bias8")
        self.resetm = self.sb([128, T], F32, 1, "resetm")
        self.mAB = self.sb([64, 4, 2, 64], BF16, 1, "mAB")
        self.id64 = self.sb([64, 8, 64], BF16, 1, "id64")
        self.sq = self.sb([128, 2, T], F32, 2, "sq")
        self.mean = self.sb([128, T], F32, 1, "mean")
        self.msq = self.sb([128, T], F32, 1, "msq")
        self.rstd = self.sb([128, T], F32, 1, "rstd")
        self.lnt = self.sb([128, 2, T], F32, 2, "lnt")
        self.kTd = [self.sb([128, 4, 128 + T], BF16, 4, f"kTd{l}") for l in range(DEPTH)]
        self.vtok = [self.sb([128, 5, 4, 128], BF16, 5, f"vtok{l}") for l in range(DEPTH)]
        self.attnT = self.sb([128, 4, T], BF16, 4, "attnT")
        self.rwT = self.sb([128, 4, T], BF16, 4, "rwT")
        self.vfirst = self.sb([128, 4, T], BF16, 4, "vfirst")
        self.ST32 = [self.sb([128, 4, 128], F32, 1, f"ST32_{l}") for l in range(DEPTH)]
        self.STb = [self.sb([128, 4, 128], BF16, 1, f"STb_{l}") for l in range(DEPTH)]
        self.ucar = [self.sb([128, 16], F32, 1, f"ucar{l}") for l in range(DEPTH)]
        self.w_init()
        self.arena0 = self.sb_off
        stg = self.sb([128, NGC], F32, 1, "stg", at=self.arena0)
        ch = P.new_chan()
        ch2 = P.new_chan()
        self.dma(SP, stg.t[:, :], self.gc, ch, writes=[(stg, None)])
        self.dma(SP, self.cvs.t[:, :], self.cv, ch2, writes=[(self.cvs, None)])
        self.dma(POOL, self.wsmall.t[:, :], self.wsm, P.new_chan(), writes=[(self.wsmall, None)])

        def g(n, rows=128):
            o, k = GC[n]
            return stg.t[0:rows, o:o + k]

        self.cp(DVE, self.identb.t[:, :], g("ident"), [(stg, None)], [(self.identb, None)])
        self.cp(DVE, self.bdiagb.t[:, :], g("bdiag"), [(stg, None)], [(self.bdiagb, None)])
        self.cp(DVE, self.resetm.t[:, :], g("reset"), [(stg, None)], [(self.resetm, None)])
        self.cp(DVE, self.mAB.t[:, :, :, :].rearrange("p a b c -> p (a b c)"), g("mAB", 64)[:, 0:512], [(stg, None)], [(self.mAB, None)])
        self.cp(DVE, self.id64.t[:, :, :].rearrange("p a b -> p (a b)"), g("id64", 64), [(stg, None)], [(self.id64, None)])
        self.stt(self.bias8.t[:, :, :, :].rearrange("p a b c -> p (a b c)"), g("biasg"), 8.0, g("mask8"), ALU.mult, ALU.add,
                 [(stg, None)], [(self.bias8, None)])
        self.memset(DVE, self.ones32.t[:, :], 1.0, [(self.ones32, None)])
        self.memset(DVE, self.epsv.t[:, 0:1], LN_EPS / (ALPHA * ALPHA), [(self.epsv, None)])
        self.memset(DVE, self.epsv.t[:, 1:2], GN_EPS, [(self.epsv, None)])
        self.act(self.ones32r.t[:, :].bitcast(mybir.dt.float32r), self.ones32.t[:, :], AF.Identity, [(self.ones32, None)], [(self.ones32r, None)])
        self.memset(DVE, self.onesb.t[:, :, :], 0.0, [(self.onesb, None)])
        self.memset(DVE, self.onesb.t[:, 0, 0:64], 1.0, [(self.onesb, None)])
        self.memset(DVE, self.onesb.t[:, 1, 64:128], 1.0, [(self.onesb, None)])
        for l in range(DEPTH):
            self.ts(DVE, self.vec(l, "omka"), self.vec(l, "k_a"), -1.0, ALU.mult, [(self.cvs, None)], [(self.cvs, None)],
                    s2=1.0, op1=ALU.add)
            self.act(self.vec(l, "sink"), self.vec(l, "sink"), AF.Exp, [(self.cvs, None)], [(self.cvs, None)])
            for g_ in ("xs", "v", "r", "k"):
                C = [(self.cvs, None)]
                self.ts(DVE, self.vec(l, "om_" + g_), self.vec(l, "mu_" + g_), -1.0, ALU.mult, C, C, s2=1.0, op1=ALU.add)
                self.tt(DVE, self.vec(l, "bom_" + g_), self.vec(l, "b_" + g_), self.vec(l, "om_" + g_), ALU.mult, C, C)
                self.tt(DVE, self.vec(l, "bmu_" + g_), self.vec(l, "b_" + g_), self.vec(l, "mu_" + g_), ALU.mult, C, C)
        self.stg = stg

    def ln_stats_begin(self):
        self.bS1 = self.ps_get()
        self.bS2 = self.ps_get()
        self.bS1_ = self.bS1
        self.bS2_ = self.bS2

    def ln_stats_chunk(self, o):
        ps = self.psum
        R = mybir.dt.float32r
        self.act(self.sq.t[:, 0, :].bitcast(R), self.x.t[:, o, :], AF.Square, [(self.x, o)], [(self.sq, 0)])
        self.act(self.sq.t[:, 1, :].bitcast(R), self.x.t[:, o, :], AF.Identity, [(self.x, o)], [(self.sq, 1)])
        w_ = self.ones32r.t[:, :].bitcast(R)
        c = T / 2.4 + 8.0
        self.P.add(PE, lambda e, r=self.sq.t[:, 1, :].bitcast(R), o_=ps[:, self.bS1, :]: e.matmul(o_, lhsT=w_, rhs=r, start=(o == 0), stop=(o == 7)),
                   reads=[(self.ones32r, None), (self.sq, 1)], writes=[(self.psb, [self.bS1])], cost=c)
        self.P.add(PE, lambda e, r=self.sq.t[:, 0, :].bitcast(R), o_=ps[:, self.bS2, :]: e.matmul(o_, lhsT=w_, rhs=r, start=(o == 0), stop=(o == 7)),
                   reads=[(self.ones32r, None), (self.sq, 0)], writes=[(self.psb, [self.bS2])], cost=c)

    def ln_finish(self, l, which, make_xb=True):
        ps = self.psum
        eps = LN_EPS / (ALPHA * ALPHA)
        b1, b2 = self.bS1, self.bS2
        self.act(self.msq.t[:, :], ps[:, b1, :], AF.Square, [(self.psb, b1)], [(self.msq, None)], scale=1.0 / D)
        self.stt(self.rstd.t[:, :], ps[:, b2, :], 1.0 / D, self.msq.t[:, :], ALU.mult, ALU.subtract,
                 [(self.psb, b2), (self.msq, None)], [(self.rstd, None)])
        self.ps_put(b2)
        self.act(self.rstd.t[:, :], self.rstd.t[:, :], AF.Ln, [(self.rstd, None), (self.epsv, None)], [(self.rstd, None)], bias=self.epsv.t[:, 0:1])
        self.act(self.rstd.t[:, :], self.rstd.t[:, :], AF.Exp, [(self.rstd, None)], [(self.rstd, None)], scale=-0.5)
        for o in range(8):
            sl = o % 2
            tmp = self.lnt.t[:, sl, :]
            self.stt(tmp, ps[:, b1, :], -1.0 / D, self.x.t[:, o, :], ALU.mult, ALU.add, [(self.psb, b1), (self.x, o)], [(self.lnt, sl)])
            self.tt(DVE, tmp, tmp, self.rstd.t[:, :], ALU.mult, [(self.lnt, sl), (self.rstd, None)], [(self.lnt, sl)])
            if make_xb:
                self.act(self.xb.t[:, o, :], tmp, AF.Identity, [(self.lnt, sl), (self.cvs, None)], [(self.xb, o)],
                         bias=self.vec(l, f"ln_b{which}", o), scale=self.vec(l, f"ln_g{which}", o))
            self.act(self.x.t[:, o, :], tmp, AF.Identity, [(self.lnt, sl), (self.cvs, None)], [(self.x, o)],
                     bias=self.vec(l, f"ln_b{which}", o), scale=self.vec(l, f"ln_g{which}", o))
        self.ps_put(b1)

    def ffn(self, l, f):
        ps = self.psum
        for j in range(NHC // 2):
            s, w = self.w_get(f"GU{f}_{j}")
            wv = w.rearrange("p (c k m) -> p c k m", c=2, k=8)
            for cc in range(2):
                c = 2 * j + cc
                bg = self.ps_get()
                bu = self.ps_get()
                for k in range(8):
                    self.mm(bg, ps[:, bg, :], wv[:, cc, k, 0:128], self.xb.t[:, k, :], k == 0, k == 7,
                            [(self.ring, s), (self.xb, k)])
                for k in range(8):
                    self.mm(bu, ps[:, bu, :], wv[:, cc, k, 128:256], self.xb.t[:, k, :], k == 0, k == 7,
                            [(self.ring, s), (self.xb, k)])
                sl = c % 2
                self.act(self.sg.t[:, sl, :], ps[:, bg, :], AF.Silu, [(self.psb, bg)], [(self.sg, sl)])
                self.tt(DVE, self.h.t[:, c, :], self.sg.t[:, sl, :], ps[:, bu, :], ALU.mult,
                        [(self.sg, sl), (self.psb, bu)], [(self.h, c)])
                self.ps_put(bg)
                self.ps_put(bu)
        self.ln_stats_begin()
        for o in range(8):
            s, w = self.w_get(f"WD{f}_{o}")
            wv = w.rearrange("p (c m) -> p c m", c=NHC)
            bd = self.ps_get()
            for c in range(NHC):
                self.mm(bd, ps[:, bd, :], wv[:, c, :], self.h.t[:, c, :], c == 0, c == NHC - 1,
                        [(self.ring, s), (self.h, c)])
            self.stt(self.x.t[:, o, :], ps[:, bd, :], 0.5 / ALPHA, self.x.t[:, o, :], ALU.mult, ALU.add,
                     [(self.psb, bd), (self.x, o)], [(self.x, o)])
            self.ps_put(bd)
            self.ln_stats_chunk(o)

    def build(self):
        P = self.P
        self.setup()
        a0 = self.arena0
        self.sb_off = a0
        self.h = self.sb([128, NHC, T], BF16, NHC, "h")
        self.sg = self.sb([128, 2, T], F32, 2, "sg")
        ffn_top = self.sb_off
        self.sb_off = a0
        self.AR = self.sb([128, 4, NCH, 2, CH], BF16, 4, "AR")
        self.Bt = self.sb([128, 4, T], BF16, 4, "Bt")
        self.Kt = self.sb([128, 4, T], BF16, 4, "Kt")
        self.vb = self.sb([128, 4, T], BF16, 4, "vb")
        self.gF = self.sb([128, 4, T], BF16, 4, "gF")
        self.bg = self.sb([128, 4, T], BF16, 4, "bg")
        self.yT = self.sb([128, 4, T], BF16, 4, "yT")
        self.gam = self.sb([128, 4, NCH], F32, 4, "gam")
        r1 = self.sb_off
        tops = []
        self.ucur = self.sb([128, 2, T + 1], F32, 2, "ucur")
        self.xs32 = self.sb([128, 2, T], F32, 2, "xs32")
        self.txa = self.sb([128, T], BF16, 1, "txa")
        self.sxg = self.sb([128, T], BF16, 1, "sxg")
        self.v32 = self.sb([128, 4, T], F32, 4, "v32")
        self.lob = self.sb([32, T], BF16, 1, "lob")
        self.tS = self.sb([128, 8, T], F32, 8, "tS")
        self.kk2 = self.sb([128, T], BF16, 1, "kk2")
        self.rkr = self.kk2
        self.qT = self.sb([128, 4, T], BF16, 4, "qT")
        self.pT = self.sb([128, 2, 2, 4, 128], BF16, 2, "pT")
        self.rec = self.sb([128, 2, 128], F32, 1, "rec")
        tops.append(self.sb_off)
        self.sb_off = r1
        self.VT = self.sb([64, 2, 512], BF16, 2, "VT")
        self.VTp = self.sb([64, 2, 4, 2, 128], BF16, 2, "VTp")
        self.BT = self.sb([64, 2, 512], BF16, 2, "BT")
        self.KT = self.sb([64, 2, 512], BF16, 2, "KT")
        self.NMbr = self.sb([64, 3, 8, 2, 64], BF16, 3, "NMbr")
        self.MakMkr = self.sb([64, 3, 8, 2, 64], BF16, 3, "MakMkr")
        self.Ab = self.sb([64, 4, 8, 64], BF16, 4, "Ab")
        self.ATb = self.sb([64, 4, 8, 64], BF16, 4, "ATb")
        self.Pb = self.sb([64, 4, 8, 64], BF16, 4, "Pb")
        self.Tb = self.sb([64, 3, 8, 64], BF16, 3, "Tb")
        self.XTb = self.sb([64, 1, 512], BF16, 1, "XTb")
        self.UT = self.sb([64, 1, 512], BF16, 1, "UT")
        self.UTp = self.sb([64, 1, 4, 2, 128], BF16, 1, "UTp")
        tops.append(self.sb_off)
        self.sb_off = r1
        self.gat = self.sb([128, 2, T], F32, 2, "gat")
        self.m1 = self.sb([128, 2, T], F32, 2, "m1")
        self.merged = self.sb([128, 8, T], BF16, 8, "merged")
        self.gsq = self.sb([128, 4, T], BF16, 4, "gsq")
        self.gmean = self.sb([128, 4, T], F32, 4, "gmean")
        self.gvar = self.sb([128, 4, T], F32, 4, "gvar")
        self.gtmp = self.sb([128, 4, T], F32, 4, "gtmp")
        tops.append(self.sb_off)
        self.sb_off = max(tops)
        self.set_ffn = [self.h, self.sg]
        self.set_r0 = [self.AR, self.Bt, self.Kt, self.vb, self.gF, self.bg, self.yT, self.gam]
        self.set_att = [self.qT, self.pT, self.rec]
        self.set_prep = [self.ucur, self.xs32, self.txa, self.sxg, self.v32, self.lob, self.tS, self.kk2]
        self.set_scan = [self.VT, self.VTp, self.BT, self.KT, self.NMbr, self.MakMkr, self.Ab, self.ATb, self.Pb, self.Tb,
                         self.XTb, self.UT, self.UTp]
        self.set_post = [self.gat, self.m1, self.merged, self.gsq, self.gmean, self.gvar, self.gtmp]
        self.set_all = self.set_ffn + self.set_r0 + self.set_att + self.set_prep + self.set_scan + self.set_post
        self.guard(self.set_all)
        mix_top = self.sb_off
        self.sb_off = max(ffn_top, mix_top)
        for l in range(DEPTH):
            self.memset(DVE, self.ST32[l].t[:, :, :], 0.0, [(self.ST32[l], None)])
            self.memset(DVE, self.STb[l].t[:, :, :], 0.0, [(self.STb[l], None)])
            self.memset(DVE, self.ucar[l].t[:, :], 0.0, [(self.ucar[l], None)])
            self.memset(DVE, self.kTd[l].t[:, :, :], 0.0, [(self.kTd[l], None)])
            self.memset(DVE, self.vtok[l].t[:, :, :, :], 0.0, [(self.vtok[l], None)])
        xin_ch = [P.new_chan() for _ in range(8)]
        out_ch = [P.new_chan() for _ in range(8)]
        last_out = []
        xTv = self.xT.rearrange("(c p) t -> p c t", p=128)
        oTv = self.outT.rearrange("(c p) t -> p c t", p=128)
        for it in range(self.NT):
            tok = slice(it * T, (it + 1) * T)
            for o in range(8):
                self.dma(SP, self.x.t[:, o, :], xTv[:, o, tok], xin_ch[o], writes=[(self.x, o)])
            for o in range(8):
                self.cp(ACT if o % 2 else DVE, self.xb.t[:, o, :], self.x.t[:, o, :], [(self.x, o)], [(self.xb, o)])
            for l in range(DEPTH):
                self.ffn(l, 0)
                self.ln_finish(l, 0)
                self.mixer(l, it)
                self.ffn(l, 1)
                last = (l == DEPTH - 1)
                self.ln_finish(l, 2, make_xb=not last)
            last_out = [self.dma(SP, oTv[:, o, tok], self.x.t[:, o, :], out_ch[o], reads=[(self.x, o)]) for o in range(8)]
        self.sched_ns = P.schedule()
        stats = P.emit(final_waits=last_out)
        return stats

    def mixer(self, l, it):
        P = self.P
        ps = self.psum
        self.guard(self.set_r0 + self.set_att)
        do_attn = "attn" in self.stages
        do_rw = "rwkv" in self.stages
        s, w = self.w_get("Q")
        if do_attn:
            wv = w.rearrange("p (c k m) -> p c k m", c=4, k=8)
            for ch in range(4):
                b = self.ps_get()
                for k in range(8):
                    self.mm(b, ps[:, b, :], wv[:, ch, k, :], self.xb.t[:, k, :], k == 0, k == 7, [(self.ring, s), (self.xb, k)])
                self.act(self.qT.t[:, ch, :], ps[:, b, :], AF.Identity, [(self.psb, b), (self.cvs, None)], [(self.qT, ch)],
                         bias=self.vec(l, "bq", ch))
                self.ps_put(b)
        s, w = self.w_get("KP")
        kT = self.kTd[l]
        vt = self.vtok[l]
        if do_attn:
            wv = w.rearrange("p (c k m) -> p c k m", c=4, k=8)
            for hp in range(4):
                b = self.ps_get()
                for k in range(8):
                    self.mm(b, ps[:, b, :], wv[:, hp, k, :], self.xb.t[:, k, :], k == 0, k == 7, [(self.ring, s), (self.xb, k)])
                self.act(kT.t[:, hp, 128:128 + T], ps[:, b, :], AF.Identity, [(self.psb, b), (self.cvs, None)], [(kT, hp)],
                         bias=self.vec(l, "bk", hp))
                self.ps_put(b)
        s, w = self.w_get("VP")
        if do_attn:
            wv = w.rearrange("p (k m) -> p k m", k=8)
            for tb in range(4):
                b = self.ps_get()
                for k in range(8):
                    self.mm(b, ps[:, b, :], self.xb.t[:, k, tb * 128:(tb + 1) * 128], wv[:, k, :], k == 0, k == 7,
                            [(self.ring, s), (self.xb, k)])
                self.tt(DVE, vt.t[:, 1 + tb, :, :].rearrange("p a b -> p (a b)"), ps[:, b, :], self.vec(l, "bvrow"), ALU.add,
                        [(self.psb, b), (self.cvs, None)], [(vt, 1 + tb)])
                self.ps_put(b)
            for qb in range(4):
                for hk in range(2):
                    gblk = it * 4 + qb
                    kbs = [1] if gblk == 0 else [0, 1]
                    sb_ = self.ps_get(2)
                    Sv = ps[:, sb_:sb_ + 2, :].rearrange("p a (g q) -> p a g q", g=4)
                    for kb in kbs:
                        self.mm(sb_ + kb, ps[:, sb_ + kb, :], self.identb.t[:, :],
                                self.bias8.t[:, kb, 4 * hk:4 * hk + 4, :].rearrange("p g q -> p (g q)"), True, False,
                                [(self.identb, None), (self.bias8, None)])
                        for g in range(4):
                            ch = 2 * hk + g // 2
                            hp = hk * 2 + g % 2
                            self.mm(sb_ + kb, Sv[:, kb, g, :], kT.t[:, hp, (qb + kb) * 128:(qb + kb + 1) * 128],
                                    self.qT.t[:, ch, qb * 128:(qb + 1) * 128], False, g == 3, [(kT, hp), (self.qT, ch)])
                    sl = (qb * 2 + hk) % 2
                    for kb in kbs:
                        self.act(self.pT.t[:, sl, kb, :, :].rearrange("p g q -> p (g q)"), ps[:, sb_ + kb, :], AF.Exp,
                                 [(self.psb, sb_ + kb)], [(self.pT, sl)], scale=0.125)
                    self.ps_put(sb_, 2)
                    ob = self.ps_get()
                    for chl in range(2):
                        combos = [(par, kb) for par in range(2) for kb in kbs]
                        for i, (par, kb) in enumerate(combos):
                            self.mm(ob, ps[:, ob, chl * 128:(chl + 1) * 128], vt.t[:, qb + kb, hk * 2 + par, :],
                                    self.pT.t[:, sl, kb, 2 * chl + par, :], i == 0, i == len(combos) - 1,
                                    [(vt, qb + kb), (self.pT, sl)])
                        for i, (par, kb) in enumerate(combos):
                            self.mm(ob, ps[:, ob, 256 + chl * 128:256 + (chl + 1) * 128], self.onesb.t[:, par, :],
                                    self.pT.t[:, sl, kb, 2 * chl + par, :], i == 0, i == len(combos) - 1,
                                    [(self.onesb, None), (self.pT, sl)])
                    for chl in range(2):
                        ch = 2 * hk + chl
                        self.act(self.rec.t[:, chl, :], ps[:, ob, 256 + chl * 128:256 + (chl + 1) * 128], AF.Ln,
                                 [(self.psb, ob), (self.cvs, None)], [(self.rec, None)], bias=self.vec(l, "sink", ch))
                    self.act(self.rec.t[:, :, :], self.rec.t[:, :, :], AF.Exp, [(self.rec, None)], [(self.rec, None)], scale=-1.0)
                    self.tt(DVE, self.attnT.t[:, 2 * hk:2 * hk + 2, qb * 128:(qb + 1) * 128],
                            ps[:, ob, 0:256].rearrange("p (c q) -> p c q", c=2), self.rec.t[:, :, :], ALU.mult,
                            [(self.psb, ob), (self.rec, None)], [(self.attnT, [2 * hk, 2 * hk + 1])])
                    self.ps_put(ob)
            for hp in range(4):
                self.cp(POOL, kT.t[:, hp, 0:128], kT.t[:, hp, T:T + 128], [(kT, hp)], [(kT, hp)])
            self.cp(POOL, vt.t[:, 0, :, :], vt.t[:, 4, :, :], [(vt, 4)], [(vt, 0)])
        if do_rw:
            self.rwkv(l, it)
        else:
            for name in ["XS", "V", "RK0", "RK1", "RK2", "RK3"]:
                self.w_get(name)
        for o in range(8):
            s, w = self.w_get(f"MG{o}")
            if not (do_attn or do_rw):
                continue
            wba = w[:, 0:512].rearrange("p (k m) -> p k m", k=4)
            wbb = w[:, 512:1024].rearrange("p (k m) -> p k m", k=4)
            wga = w[:, 1024:2048].rearrange("p (k m) -> p k m", k=8)
            wgb = w[:, 2048:3072].rearrange("p (k m) -> p k m", k=8)
            parts = []
            if do_attn:
                parts.append((wga, wba, self.attnT, "bga", 0))
            if do_rw:
                parts.append((wgb, wbb, self.rwT, "bgb", 1))
            for i, (wg, wb, src, bn, gi0) in enumerate(parts):
                gi = gi0
                b1 = self.ps_get()
                for k in range(8):
                    self.mm(b1, ps[:, b1, :], wg[:, k, :], self.xb.t[:, k, :], k == 0, k == 7, [(self.ring, s), (self.xb, k)])
                self.act(self.gat.t[:, gi, :], ps[:, b1, :], AF.Sigmoid, [(self.psb, b1), (self.cvs, None)], [(self.gat, gi)],
                         bias=self.vec(l, bn, o))
                self.ps_put(b1)
                b2 = self.ps_get()
                for k in range(4):
                    self.mm(b2, ps[:, b2, :], wb[:, k, :], src.t[:, k, :], k == 0, k == 3, [(self.ring, s), (src, k)])
                last = (i == len(parts) - 1)
                if i == 0:
                    dst = self.merged.t[:, o, :] if last else self.m1.t[:, o % 2, :]
                    self.tt(DVE, dst, self.gat.t[:, gi, :], ps[:, b2, :], ALU.mult, [(self.gat, gi), (self.psb, b2)],
                            [(self.merged, o)] if last else [(self.m1, o % 2)])
                else:
                    self.tt(DVE, self.gat.t[:, gi, :], self.gat.t[:, gi, :], ps[:, b2, :], ALU.mult, [(self.gat, gi), (self.psb, b2)],
                            [(self.gat, gi)])
                    self.tt(DVE, self.merged.t[:, o, :], self.gat.t[:, gi, :], self.m1.t[:, o % 2, :], ALU.add,
                            [(self.gat, gi), (self.m1, o % 2)], [(self.merged, o)])
                self.ps_put(b2)
        self.ln_stats_begin()
        for j in range(2):
            s, w = self.w_get(f"WO{j}")
            wv = w.rearrange("p (c k m) -> p c k m", c=4, k=8)
            for oo in range(4):
                o = 4 * j + oo
                if do_attn or do_rw:
                    b = self.ps_get()
                    for k in range(8):
                        self.mm(b, ps[:, b, :], wv[:, oo, k, :], self.merged.t[:, k, :], k == 0, k == 7,
                                [(self.ring, s), (self.merged, k)])
                    self.stt(self.x.t[:, o, :], ps[:, b, :], 1.0 / ALPHA, self.x.t[:, o, :], ALU.mult, ALU.add,
                             [(self.psb, b), (self.x, o)], [(self.x, o)])
                    self.ps_put(b)
                self.ln_stats_chunk(o)
        self.ln_finish(l, 1)
        self.guard(self.set_ffn)

    def shift_mix(self, l, bank, grp, gi, car, ci, dest, dest_rw):
        ps = self.psum
        sl = self.uc_i % 2
        self.uc_i += 1
        uc = self.ucur
        V = lambda n: self.vec(l, n + "_" + grp, gi)
        self.act(dest, ps[:, bank, :], AF.Identity, [(self.psb, bank), (self.cvs, None)], dest_rw, bias=V("bom"), scale=V("om"))
        self.act(uc.t[:, sl, 1:T + 1], ps[:, bank, :], AF.Identity, [(self.psb, bank), (self.cvs, None)], [(uc, sl)],
                 bias=V("bmu"), scale=V("mu"))
        self.cp(POOL, uc.t[:, sl, 0:1], car.t[:, ci:ci + 1], [(car, None)], [(uc, sl)])
        self.tt(DVE, dest, dest, uc.t[:, sl, 0:T], ALU.add, list(dest_rw) + [(uc, sl)], dest_rw)
        self.cp(POOL, car.t[:, ci:ci + 1], uc.t[:, sl, T:T + 1], [(uc, sl)], [(car, None)])

    def rwkv(self, l, it):
        P = self.P
        ps = self.psum
        V = lambda n, c=None: self.vec(l, n, c)
        car = self.ucar[l]
        tS = self.tS
        so = l * NSMALL
        w2a2 = self.wsmall.t[:, so:so + 512]
        g2w = self.wsmall.t[:, so + 512:so + 1024]
        v1w = self.wsmall.t[:, so + 1024:so + 1152].rearrange("p (c r) -> p c r", c=4)
        v2w = self.wsmall.t[:, so + 1152:so + 1664]
        self.guard(self.set_prep)
        self.uc_i = 0
        s, w = self.w_get("XS")
        wv = w.rearrange("p (c k m) -> p c k m", c=2, k=8)
        for ci in range(2):
            b = self.ps_get()
            for k in range(8):
                self.mm(b, ps[:, b, :], wv[:, ci, k, :], self.xb.t[:, k, :], k == 0, k == 7, [(self.ring, s), (self.xb, k)])
            self.shift_mix(l, b, "xs", ci, car, ci, self.xs32.t[:, ci, :], [(self.xs32, ci)])
            self.ps_put(b)
        self.act(self.txa.t[0:64, :], self.xs32.t[0:64, 0, :], AF.Tanh, [(self.xs32, 0)], [(self.txa, None)])
        self.cp(ACT, self.txa.t[64:128, :], self.xs32.t[64:128, 0, :], [(self.xs32, 0)], [(self.txa, None)])
        self.act(self.sxg.t[:, :], self.xs32.t[:, 1, :], AF.Sigmoid, [(self.xs32, 1)], [(self.sxg, None)])
        s, w = self.w_get("V")
        wv = w.rearrange("p (c k m) -> p c k m", c=4, k=8)
        for fc in range(4):
            b = self.ps_get()
            for k in range(8):
                self.mm(b, ps[:, b, :], wv[:, fc, k, :], self.xb.t[:, k, :], k == 0, k == 7, [(self.ring, s), (self.xb, k)])
            self.shift_mix(l, b, "v", fc, car, 2 + fc, self.v32.t[:, fc, :], [(self.v32, fc)])
            self.ps_put(b)
        if l == 0:
            for fc in range(4):
                self.cp(ACT, self.vfirst.t[:, fc, :], self.v32.t[:, fc, :], [(self.v32, fc)], [(self.vfirst, fc)])
        else:
            for fc in range(4):
                self.cp(ACT if fc % 2 else DVE, self.vb.t[:, fc, :], self.v32.t[:, fc, :], [(self.v32, fc)], [(self.vb, fc)])
            b = self.ps_get()
            for fc in range(4):
                self.mm(b, ps[0:32, b, :], v1w[:, fc, :], self.vb.t[:, fc, :], fc == 0, fc == 3, [(self.wsmall, None), (self.vb, fc)])
            self.cp(ACT, self.lob.t[:, :], ps[0:32, b, :], [(self.psb, b)], [(self.lob, None)])
            self.ps_put(b)
            for fc in range(4):
                b = self.ps_get()
                self.mm(b, ps[:, b, :], v2w[0:32, fc * 128:(fc + 1) * 128], self.lob.t[:, :], True, True, [(self.wsmall, None), (self.lob, None)])
                self.act(tS.t[:, 0, :], ps[:, b, :], AF.Sigmoid, [(self.psb, b), (self.cvs, None)], [(tS, 0)], bias=V("v0", fc))
                self.ps_put(b)
                self.tt(DVE, tS.t[:, 1, :], self.vfirst.t[:, fc, :], self.v32.t[:, fc, :], ALU.subtract,
                        [(self.vfirst, fc), (self.v32, fc)], [(tS, 1)])
                self.tt(DVE, tS.t[:, 1, :], tS.t[:, 1, :], tS.t[:, 0, :], ALU.mult, [(tS, 0), (tS, 1)], [(tS, 1)])
                self.tt(DVE, self.v32.t[:, fc, :], self.v32.t[:, fc, :], tS.t[:, 1, :], ALU.add, [(self.v32, fc), (tS, 1)], [(self.v32, fc)])
        for fc in range(4):
            self.cp(ACT, self.vb.t[:, fc, :], self.v32.t[:, fc, :], [(self.v32, fc)], [(self.vb, fc)])
        for fc in range(4):
            s, w = self.w_get(f"RK{fc}")
            wv = w.rearrange("p (c k m) -> p c k m", c=2, k=8)
            r32, k32, sw, a32, cs, e1, e2, e3 = [tS.t[:, i, :] for i in range(8)]
            R = lambda *i: [(tS, j) for j in i]
            for ci, (grp, cbase, slot) in enumerate((("r", 6, 0), ("k", 10, 1))):
                b = self.ps_get()
                for k in range(8):
                    self.mm(b, ps[:, b, :], wv[:, ci, k, :], self.xb.t[:, k, :], k == 0, k == 7, [(self.ring, s), (self.xb, k)])
                self.shift_mix(l, b, grp, fc, car, cbase + fc, tS.t[:, slot, :], [(tS, slot)])
                self.ps_put(b)
            fsl = slice(fc * 128, (fc + 1) * 128)
            b = self.ps_get()
            self.mm(b, ps[:, b, :], w2a2[0:64, fsl], self.txa.t[0:64, :], True, True, [(self.wsmall, None), (self.txa, None)])
            self.act(sw, ps[:, b, :], AF.Sigmoid, [(self.psb, b), (self.cvs, None)], R(2), bias=V("w0", fc))
            self.ps_put(b)
            b = self.ps_get()
            self.mm(b, ps[:, b, :], w2a2[64:128, fsl], self.txa.t[64:128, :], True, True, [(self.wsmall, None), (self.txa, None)])
            self.act(a32, ps[:, b, :], AF.Sigmoid, [(self.psb, b), (self.cvs, None)], R(3), bias=V("a0", fc))
            self.ps_put(b)
            bgz = self.ps_get()
            self.mm(bgz, ps[:, bgz, :], g2w[:, fsl], self.sxg.t[:, :], True, True, [(self.wsmall, None), (self.sxg, None)])
            self.cp(ACT, self.gF.t[:, fc, :], ps[:, bgz, :], [(self.psb, bgz)], [(self.gF, fc)])
            self.P.add(DVE, lambda e, o=cs, d1=sw: e.tensor_tensor_scan(out=o, data0=self.resetm.t[:, :], data1=d1, initial=0.0,
                                                                        op0=ALU.mult, op1=ALU.add),
                       reads=R(2) + [(self.resetm, None)], writes=R(4), cost=1150.0)
            self.act(e1, cs, AF.Exp, R(4), R(5), scale=-DECAY_C)
            self.act(e2, cs, AF.Exp, R(4), R(6), scale=DECAY_C)
            self.tt(DVE, e3, cs, sw, ALU.subtract, R(4, 2), R(7))
            self.act(e3, e3, AF.Exp, R(7), R(7), scale=-DECAY_C)
            self.cp(POOL, self.gam.t[:, fc, :], e1[:, CH - 1::CH], R(5), [(self.gam, fc)])
            kk = sw
            self.ts(DVE, kk, k32, V("k_k", fc), ALU.mult, R(1) + [(self.cvs, None)], R(2))
            self.act(self.kk2.t[:, :], kk, AF.Square, R(2), [(self.kk2, None)])
            b = self.ps_get()
            self.mm(b, ps[:, b, :], self.bdiagb.t[:, :], self.kk2.t[:, :], True, True, [(self.bdiagb, None), (self.kk2, None)])
            nrm = cs
            self.ts(DVE, nrm, ps[:, b, :], 1e-18, ALU.max, [(self.psb, b)], R(4))
            self.ps_put(b)
            self.act(nrm, nrm, AF.Ln, R(4), R(4))
            self.act(nrm, nrm, AF.Exp, R(4), R(4), scale=-0.5)
            kkn = kk
            self.tt(DVE, kkn, kk, nrm, ALU.mult, R(2, 4), R(2))
            ARv = self.AR.t[:, fc, :, :, :]
            self.stt(ARv[:, :, 0, :], kkn.rearrange("p (c t) -> p c t", c=NCH), -1.0, e3.rearrange("p (c t) -> p c t", c=NCH),
                     ALU.mult, ALU.mult, R(2, 7), [(self.AR, fc)])
            self.tt(DVE, ARv[:, :, 1, :], r32.rearrange("p (c t) -> p c t", c=NCH), e1.rearrange("p (c t) -> p c t", c=NCH),
                    ALU.mult, R(0, 5), [(self.AR, fc)])
            bp = e3
            self.tt(DVE, bp, kkn, a32, ALU.mult, R(2, 3), R(7))
            self.tt(DVE, self.Bt.t[:, fc, :], bp, e2, ALU.mult, R(7, 6), [(self.Bt, fc)])
            t1 = e1
            self.ts(DVE, t1, a32, V("k_a", fc), ALU.mult, R(3) + [(self.cvs, None)], R(5), s2=V("omka", fc), op1=ALU.add)
            k2 = t1
            self.tt(DVE, k2, k32, t1, ALU.mult, R(1, 5), R(5))
            self.tt(DVE, self.Kt.t[:, fc, :], k2, e2, ALU.mult, R(5, 6), [(self.Kt, fc)])
            rk = k32
            self.tt(DVE, rk, r32, k2, ALU.mult, R(0, 5), R(1))
            self.ts(DVE, self.rkr.t[:, :], rk, V("r_k", fc), ALU.mult, R(1) + [(self.cvs, None)], [(self.rkr, None)])
            b = self.ps_get()
            self.mm(b, ps[:, b, :], self.bdiagb.t[:, :], self.rkr.t[:, :], True, True, [(self.bdiagb, None), (self.rkr, None)])
            bon = e2
            self.tt(DVE, bon, ps[:, b, :], self.v32.t[:, fc, :], ALU.mult, [(self.psb, b), (self.v32, fc)], R(6))
            self.ps_put(b)
            self.tt(DVE, self.bg.t[:, fc, :], ps[:, bgz, :], bon, ALU.mult, [(self.psb, bgz)] + R(6), [(self.bg, fc)])
            self.ps_put(bgz)
        self.guard(self.set_scan)
        for sl in range(2):
            self.memset(POOL, self.VTp.t[:, sl, :, :, :], 0.0, [(self.VTp, sl)])
        self.memset(POOL, self.UTp.t[:, 0, :, :, :], 0.0, [(self.UTp, 0)])
        ST32 = self.ST32[l]
        STb = self.STb[l]

        def psbf(b):
            return ps[:, b, :].bitcast(BF16)

        def l_steps(c):
            sl = c % 2
            m3 = c % 3
            tok = slice(c * CH, (c + 1) * CH)
            st = {}
            steps = []

            def s_tr():
                b1 = self.ps_get()
                b2 = self.ps_get()
                for fc in range(4):
                    self.tr(b1, psbf(b1)[0:64, fc * 128:(fc + 1) * 128], self.vb.t[:, fc, tok], [(self.vb, fc)])
                for fc in range(4):
                    self.tr(b1, psbf(b1)[0:64, 512 + fc * 128:512 + (fc + 1) * 128], self.Bt.t[:, fc, tok], [(self.Bt, fc)])
                for fc in range(4):
                    self.tr(b2, psbf(b2)[0:64, fc * 128:(fc + 1) * 128], self.Kt.t[:, fc, tok], [(self.Kt, fc)])
                self.cp(ACT, self.VT.t[:, sl, :], psbf(b1)[0:64, 0:512], [(self.psb, b1)], [(self.VT, sl)])
                for par in range(2):
                    self.cp(DVE, self.VTp.t[:, sl, :, par, par * 64:(par + 1) * 64],
                            psbf(b1)[0:64, 0:512].rearrange("p (f a i) -> p f a i", f=4, a=2)[:, :, par, :],
                            [(self.psb, b1)], [(self.VTp, sl)])
                self.cp(ACT, self.BT.t[:, sl, :], psbf(b1)[0:64, 512:1024], [(self.psb, b1)], [(self.BT, sl)])
                self.cp(DVE, self.KT.t[:, sl, :], psbf(b2)[0:64, 0:512], [(self.psb, b2)], [(self.KT, sl)])
                self.ps_put(b1)
                self.ps_put(b2)
            steps.append(s_tr)

            def s_m():
                for (src, dst) in ((self.Bt, self.NMbr), (self.Kt, self.MakMkr)):
                    banks = [self.ps_get(), self.ps_get()]
                    for fc in range(4):
                        for par in range(2):
                            pb = par * 64
                            self.mm(banks[par], ps[0:64, banks[par], fc * 128:(fc + 1) * 128], src.t[pb:pb + 64, fc, tok],
                                    self.AR.t[pb:pb + 64, fc, c, :, :].rearrange("p a t -> p (a t)"), True, True,
                                    [(src, fc), (self.AR, fc)])
                    for par in range(2):
                        self.tt(DVE, dst.t[:, m3, par * 4:par * 4 + 4, :, :].rearrange("p f a t -> p (f a t)"), ps[0:64, banks[par], :],
                                self.mAB.t[:, 0:4, :, :].rearrange("p f a t -> p (f a t)"), ALU.mult,
                                [(self.psb, banks[par]), (self.mAB, None)], [(dst, m3)])
                    self.ps_put(banks[0])
                    self.ps_put(banks[1])
            steps.append(s_m)

            def s_nt():
                b = self.ps_get()
                for hs in range(8):
                    self.tr(b, psbf(b)[0:64, hs * 64:(hs + 1) * 64], self.NMbr.t[:, m3, hs, 0, :], [(self.NMbr, m3)])
                i0 = sl * 2
                self.cp(ACT, self.ATb.t[:, i0, :, :].rearrange("p h t -> p (h t)"), psbf(b)[0:64, 0:512], [(self.psb, b)], [(self.ATb, i0)])
                self.ps_put(b)
                self.cp(POOL, self.Ab.t[:, i0, :, :], self.NMbr.t[:, m3, :, 0, :], [(self.NMbr, m3)], [(self.Ab, i0)])
                self.tt(DVE, self.Pb.t[:, i0, :, :], self.NMbr.t[:, m3, :, 0, :], self.id64.t[:, :, :], ALU.add,
                        [(self.NMbr, m3), (self.id64, None)], [(self.Pb, i0)])
            steps.append(s_nt)

            def mk_stage(kst):
                def s_sq():
                    cur = sl * 2 + (kst - 1) % 2
                    nxt = sl * 2 + kst % 2
                    last = (kst == 5)
                    bAT = self.ps_get()
                    bA = None if last else self.ps_get()
                    for hs in range(8):
                        a_ = self.Ab.t[:, cur, hs, :]
                        at_ = self.ATb.t[:, cur, hs, :]
                        hsl = slice(hs * 64, (hs + 1) * 64)
                        self.mm(bAT, ps[0:64, bAT, hsl], a_, at_, True, True, [(self.Ab, cur), (self.ATb, cur)])
                        if not last:
                            self.mm(bA, ps[0:64, bA, hsl], at_, a_, True, True, [(self.Ab, cur), (self.ATb, cur)])
                    self.cp(ACT, self.ATb.t[:, nxt, :, :].rearrange("p h t -> p (h t)"), ps[0:64, bAT, :], [(self.psb, bAT)], [(self.ATb, nxt)])
                    self.ps_put(bAT)
                    if not last:
                        self.cp(DVE, self.Ab.t[:, nxt, :, :].rearrange("p h t -> p (h t)"), ps[0:64, bA, :], [(self.psb, bA)], [(self.Ab, nxt)])
                        self.ps_put(bA)

                def s_p():
                    cur = sl * 2 + (kst - 1) % 2
                    nxt = sl * 2 + kst % 2
                    last = (kst == 5)
                    bP = self.ps_get()
                    for hs in range(8):
                        hsl = slice(hs * 64, (hs + 1) * 64)
                        self.mm(bP, ps[0:64, bP, hsl], self.ATb.t[:, nxt, hs, :], self.Pb.t[:, cur, hs, :], True, True,
                                [(self.ATb, nxt), (self.Pb, cur)])
                    pc = self.Pb.t[:, cur, :, :].rearrange("p h t -> p (h t)")
                    if last:
                        self.tt(DVE, self.Tb.t[:, m3, :, :].rearrange("p h t -> p (h t)"), ps[0:64, bP, :], pc, ALU.add,
                                [(self.psb, bP), (self.Pb, cur)], [(self.Tb, m3)])
                    else:
                        self.tt(DVE, self.Pb.t[:, nxt, :, :].rearrange("p h t -> p (h t)"), ps[0:64, bP, :], pc, ALU.add,
                                [(self.psb, bP), (self.Pb, cur)], [(self.Pb, nxt)])
                    self.ps_put(bP)
                return [s_sq, s_p]
            for kst in range(1, 6):
                steps.extend(mk_stage(kst))
            return steps

        def s_phase(c):
            sl = c % 2
            m3 = c % 3
            tok = slice(c * CH, (c + 1) * CH)
            bx = self.ps_get()
            self.memset(DVE, ps[0:64, bx, :], 0.0, [(self.psb, bx)])
            for fc in range(4):
                self.mmx(bx, ps[0:64, bx, fc * 128:(fc + 1) * 128], self.AR.t[:, fc, c, 0, :], STb.t[:, fc, :],
                         [(self.AR, fc), (STb, None)])
            for fc in range(4):
                for par in range(2):
                    hs = par * 4 + fc
                    cs_ = slice(fc * 128 + par * 64, fc * 128 + par * 64 + 64)
                    self.mmx(bx, ps[0:64, bx, cs_], self.MakMkr.t[:, m3, hs, 0, :], self.VT.t[:, sl, cs_],
                             [(self.MakMkr, m3), (self.VT, sl)])
            self.cp(ACT, self.XTb.t[:, 0, :], ps[0:64, bx, :], [(self.psb, bx)], [(self.XTb, 0)])
            self.ps_put(bx)
            bu = self.ps_get()
            for fc in range(4):
                for par in range(2):
                    hs = par * 4 + fc
                    cs_ = slice(fc * 128 + par * 64, fc * 128 + par * 64 + 64)
                    self.mm(bu, ps[0:64, bu, cs_], self.Tb.t[:, m3, hs, :], self.XTb.t[:, 0, cs_], True, True, [(self.Tb, m3), (self.XTb, 0)])
            self.cp(DVE, self.UT.t[:, 0, :], ps[0:64, bu, :], [(self.psb, bu)], [(self.UT, 0)])
            for par in range(2):
                self.cp(ACT if par else DVE, self.UTp.t[:, 0, :, par, par * 64:(par + 1) * 64],
                        ps[0:64, bu, :].rearrange("p (f a i) -> p f a i", f=4, a=2)[:, :, par, :], [(self.psb, bu)], [(self.UTp, 0)])
            self.ps_put(bu)
            by = self.ps_get()
            self.memset(DVE, ps[:, by, 0:256], 0.0, [(self.psb, by)])
            for fc in range(4):
                self.mmx(by, ps[:, by, fc * 64:(fc + 1) * 64], STb.t[:, fc, :], self.AR.t[:, fc, c, 1, :], [(STb, None), (self.AR, fc)])
            for fc in range(4):
                o_ = ps[:, by, fc * 64:(fc + 1) * 64]
                for par in range(2):
                    hs = par * 4 + fc
                    self.mmx(by, o_, self.UTp.t[:, 0, fc, par, :], self.NMbr.t[:, m3, hs, 1, :], [(self.UTp, 0), (self.NMbr, m3)])
                    self.mmx(by, o_, self.VTp.t[:, sl, fc, par, :], self.MakMkr.t[:, m3, hs, 1, :], [(self.VTp, sl), (self.MakMkr, m3)])
            self.cp(ACT, self.yT.t[:, :, tok], ps[:, by, 0:256].rearrange("p (f t) -> p f t", f=4), [(self.psb, by)], [(self.yT, None)])
            self.ps_put(by)
            bs = self.ps_get()
            for fc in range(4):
                fsl = slice(fc * 128, (fc + 1) * 128)
                self.mm(bs, ps[:, bs, fsl], self.BT.t[:, sl, fsl], self.UT.t[:, 0, fsl], True, False, [(self.BT, sl), (self.UT, 0)])
                self.mm(bs, ps[:, bs, fsl], self.KT.t[:, sl, fsl], self.VT.t[:, sl, fsl], False, True, [(self.KT, sl), (self.VT, sl)])
            for par in range(2):
                pb = par * 64
                dv = ST32.t[pb:pb + 64, :, par * 64:(par + 1) * 64]
                self.tt(DVE, dv, ps[pb:pb + 64, bs, :].rearrange("p (f a i) -> p f a i", f=4, a=2)[:, :, par, :], dv, ALU.add,
                        [(self.psb, bs), (ST32, None)], [(ST32, None)])
            self.ps_put(bs)
            for fc in range(4):
                self.act(STb.t[:, fc, :], ST32.t[:, fc, :], AF.Identity, [(ST32, None), (self.gam, fc)], [(STb, None)],
                         scale=self.gam.t[:, fc, c:c + 1])
            for fc in range(4):
                self.act(ST32.t[:, fc, :], ST32.t[:, fc, :], AF.Identity, [(ST32, None), (self.gam, fc)], [(ST32, None)],
                         scale=self.gam.t[:, fc, c:c + 1])

        for c0 in range(0, NCH, 2):
            sa = l_steps(c0)
            sb2 = l_steps(c0 + 1)
            for fa, fb in zip(sa, sb2):
                fa()
                fb()
            s_phase(c0)
            s_phase(c0 + 1)
        self.guard(self.set_post)
        for fc in range(4):
            yv = self.yT.t[:, fc, :]
            self.act(self.gsq.t[:, fc, :], yv, AF.Square, [(self.yT, fc)], [(self.gsq, fc)])
            b1 = self.ps_get()
            b2 = self.ps_get()
            self.mm(b1, ps[:, b1, :], self.bdiagb.t[:, :], yv, True, True, [(self.bdiagb, None), (self.yT, fc)])
            self.mm(b2, ps[:, b2, :], self.bdiagb.t[:, :], self.gsq.t[:, fc, :], True, True, [(self.bdiagb, None), (self.gsq, fc)])
            self.act(self.gmean.t[:, fc, :], ps[:, b1, :], AF.Identity, [(self.psb, b1)], [(self.gmean, fc)], scale=1.0 / 64)
            self.act(self.gtmp.t[:, fc, :], ps[:, b1, :], AF.Square, [(self.psb, b1)], [(self.gtmp, fc)], scale=1.0 / 64)
            self.stt(self.gvar.t[:, fc, :], ps[:, b2, :], 1.0 / 64, self.gtmp.t[:, fc, :], ALU.mult, ALU.subtract,
                     [(self.psb, b2), (self.gtmp, fc)], [(self.gvar, fc)])
            self.ps_put(b1)
            self.ps_put(b2)
            self.act(self.gvar.t[:, fc, :], self.gvar.t[:, fc, :], AF.Ln, [(self.gvar, fc), (self.epsv, None)], [(self.gvar, fc)], bias=self.epsv.t[:, 1:2])
            self.act(self.gvar.t[:, fc, :], self.gvar.t[:, fc, :], AF.Exp, [(self.gvar, fc)], [(self.gvar, fc)], scale=-0.5)
            self.tt(DVE, self.gtmp.t[:, fc, :], yv, self.gmean.t[:, fc, :], ALU.subtract, [(self.yT, fc), (self.gmean, fc)], [(self.gtmp, fc)])
            self.tt(DVE, self.gtmp.t[:, fc, :], self.gtmp.t[:, fc, :], self.gvar.t[:, fc, :], ALU.mult, [(self.gtmp, fc), (self.gvar, fc)], [(self.gtmp, fc)])
            self.ts(DVE, self.gtmp.t[:, fc, :], self.gtmp.t[:, fc, :], V("gn_g", fc), ALU.mult, [(self.gtmp, fc), (self.cvs, None)], [(self.gtmp, fc)],
                    s2=V("gn_b", fc), op1=ALU.add)
            self.tt(DVE, self.gtmp.t[:, fc, :], self.gtmp.t[:, fc, :], self.gF.t[:, fc, :], ALU.mult, [(self.gtmp, fc), (self.gF, fc)], [(self.gtmp, fc)])
            self.tt(DVE, self.rwT.t[:, fc, :], self.gtmp.t[:, fc, :], self.bg.t[:, fc, :], ALU.add, [(self.gtmp, fc), (self.bg, fc)], [(self.rwT, fc)])


_CACHE = {}


def _run(inp, NT, **kw):
    key = (NT, tuple(sorted(kw.items())))
    b = Builder(NT, **kw)
    stats = b.build()
    return b, stats


def kernel(**inputs):
    inp = {k: np.asarray(v) for k, v in inputs.items()}
    x = inp["x"].astype(np.float32, copy=False)
    B = x.shape[0]
    NT = SEQ // T
    b, stats = _run(inp, NT)
    wf = pack_weights(inp)
    wsm = pack_small(inp)
    cv = pack_vecs(inp)
    gc = pack_gconst(inp)
    in_maps = []
    for c in range(B):
        in_maps.append({"xT": np.ascontiguousarray(x[c].T), "wf": wf, "wsm": wsm, "cv": cv, "gc": gc})
    res = run_bass_kernel_spmd(b.nc, in_maps, core_ids=list(range(B)))
    out = np.stack([np.ascontiguousarray(r["outT"].T) for r in res.results], axis=0)
    return out.astype(np.float32, copy=False)
```
